# Optimizing a Trainium2 kernel written in Bass

```python
import jax, jax.numpy as jnp
from jax import lax
import numpy as np

D_MODEL = 1024
BATCH = 8
SEQ = 2048
DEPTH = 1

CHUNK = 128
SGU_WIDTH = D_MODEL
SGU_GROUPS = 8
SGU_GROUP_DIM = SGU_WIDTH // SGU_GROUPS
CONV_WIDTH = D_MODEL
CONV_KERNEL = 31
N_BRANCH = 2
D_FF = 4 * D_MODEL
D_IN = 2 * SGU_WIDTH + 2 * CONV_WIDTH + N_BRANCH * D_MODEL
EPS = 1e-6

kernel_name = "hybrid_gmlp_conformer_gated_block"


def rmsnorm(x, g):
    xf = x.astype(jnp.float32)
    y = xf * lax.rsqrt(jnp.mean(xf * xf, axis=-1, keepdims=True) + EPS)
    return (y * g.astype(jnp.float32)).astype(x.dtype)


def layernorm(x, g, b):
    xf = x.astype(jnp.float32)
    mu = jnp.mean(xf, axis=-1, keepdims=True)
    var = jnp.mean(jnp.square(xf - mu), axis=-1, keepdims=True)
    y = (xf - mu) * lax.rsqrt(var + EPS)
    return (y * g.astype(jnp.float32) + b.astype(jnp.float32)).astype(x.dtype)


def chunked_causal_sgu(u, v, w_s, b_s):
    B, S, _ = v.shape
    n_chunks = S // CHUNK
    vc = v.reshape(B, n_chunks, CHUNK, SGU_GROUPS, SGU_GROUP_DIM)
    causal = jnp.tril(jnp.ones((CHUNK, CHUNK), dtype=bool))
    w = jnp.where(causal[None], w_s, jnp.zeros((), w_s.dtype))
    mixed = jnp.einsum('gts,bnsgc->bntgc', w, vc) + b_s.T[None, None, :, :, None]
    return u * mixed.reshape(B, S, SGU_WIDTH)


def causal_depthwise_conv(a, w, b):
    y = lax.conv_general_dilated(
        a, w[:, None, :], window_strides=(1,), padding=[(CONV_KERNEL - 1, 0)],
        dimension_numbers=('NWC', 'WIO', 'NWC'), feature_group_count=CONV_WIDTH)
    return y + b


def setup_inputs(seed: int = 0) -> dict:
    key = jax.random.key(seed)
    ks = jax.random.split(key, 20)
    f32 = jnp.float32
    L = DEPTH

    def nrm(k, shape, scale):
        return jax.random.normal(k, shape, f32) * scale

    return {
        "x": jax.random.normal(ks[0], (BATCH, SEQ, D_MODEL), f32),
        "norm1_g": 1.0 + nrm(ks[1], (L, D_MODEL), 0.02),
        "w_in": nrm(ks[2], (L, D_MODEL, D_IN), D_MODEL ** -0.5),
        "b_in": nrm(ks[3], (L, D_IN), 0.02),
        "sgu_ln_g": 1.0 + nrm(ks[4], (L, SGU_WIDTH), 0.02),
        "sgu_ln_b": nrm(ks[5], (L, SGU_WIDTH), 0.02),
        "sgu_w": nrm(ks[6], (L, SGU_GROUPS, CHUNK, CHUNK), CHUNK ** -0.5),
        "sgu_b": 1.0 + nrm(ks[7], (L, SGU_GROUPS, CHUNK), 0.02),
        "w_proj_a": nrm(ks[8], (L, SGU_WIDTH, D_MODEL), SGU_WIDTH ** -0.5),
        "conv_w": nrm(ks[9], (L, CONV_KERNEL, CONV_WIDTH), CONV_KERNEL ** -0.5),
        "conv_b": nrm(ks[10], (L, CONV_WIDTH), 0.02),
        "conv_ln_g": 1.0 + nrm(ks[11], (L, CONV_WIDTH), 0.02),
        "conv_ln_b": nrm(ks[12], (L, CONV_WIDTH), 0.02),
        "w_proj_b": nrm(ks[13], (L, CONV_WIDTH, D_MODEL), CONV_WIDTH ** -0.5),
        "b_proj_b": nrm(ks[14], (L, D_MODEL), 0.02),
        "w_out": nrm(ks[15], (L, D_MODEL, D_MODEL), D_MODEL ** -0.5),
        "norm2_g": 1.0 + nrm(ks[16], (L, D_MODEL), 0.02),
        "w_ff1": nrm(ks[17], (L, D_MODEL, D_FF), D_MODEL ** -0.5),
        "w_ff2": nrm(ks[18], (L, D_FF, D_MODEL), D_FF ** -0.5),
        "norm_f_g": 1.0 + nrm(ks[19], (D_MODEL,), 0.02),
    }


def reference(x, norm1_g, w_in, b_in, sgu_ln_g, sgu_ln_b, sgu_w, sgu_b, w_proj_a,
              conv_w, conv_b, conv_ln_g, conv_ln_b, w_proj_b, b_proj_b, w_out,
              norm2_g, w_ff1, w_ff2, norm_f_g):
    B, S, D = x.shape
    split_a = 2 * SGU_WIDTH
    split_b = split_a + 2 * CONV_WIDTH
    for l in range(DEPTH):
        h = rmsnorm(x, norm1_g[l])
        z = jnp.einsum('bsd,de->bse', h, w_in[l]) + b_in[l]
        z_a, z_b, z_g = z[..., :split_a], z[..., split_a:split_b], z[..., split_b:]

        z_a = jax.nn.gelu(z_a)
        u, v = z_a[..., :SGU_WIDTH], z_a[..., SGU_WIDTH:]
        v = layernorm(v, sgu_ln_g[l], sgu_ln_b[l])
        y_a = jnp.einsum('bsc,cd->bsd', chunked_causal_sgu(u, v, sgu_w[l], sgu_b[l]), w_proj_a[l])

        val, gate = z_b[..., :CONV_WIDTH], z_b[..., CONV_WIDTH:]
        a = val * jax.nn.sigmoid(gate)
        c = causal_depthwise_conv(a, conv_w[l], conv_b[l])
        c = jax.nn.silu(layernorm(c, conv_ln_g[l], conv_ln_b[l]))
        y_b = jnp.einsum('bsc,cd->bsd', c, w_proj_b[l]) + b_proj_b[l]

        g = jax.nn.sigmoid(z_g).reshape(B, S, N_BRANCH, D)
        merged = g[:, :, 0, :] * y_a + g[:, :, 1, :] * y_b
        x = x + jnp.einsum('bsd,de->bse', merged, w_out[l])

        h2 = rmsnorm(x, norm2_g[l])
        f = jnp.square(jax.nn.relu(jnp.einsum('bsd,df->bsf', h2, w_ff1[l])))
        x = x + jnp.einsum('bsf,fd->bsd', f, w_ff2[l])
    return rmsnorm(x, norm_f_g)
```

```python
import contextlib
import numpy as np
import concourse.bass as bass
import concourse.mybir as mybir
from concourse.bass_utils import run_bass_kernel_spmd

F32 = mybir.dt.float32
BF16 = mybir.dt.bfloat16
AF = mybir.ActivationFunctionType
ALU = mybir.AluOpType
DSZ = {F32: 4, BF16: 2}

SEQ = 2048
D = 1024
NT = 4
NCH = 16
KC = 8
CK = 31
EPS = 1e-6


class Tk:
    __slots__ = ("sp", "lo", "hi", "lw", "rd", "ov")

    def __init__(self, sp, lo, hi):
        self.sp, self.lo, self.hi = sp, lo, hi
        self.lw = []
        self.rd = []
        self.ov = None


class Sched:
    ENG = ("pe", "dve", "act", "pool", "sp")

    def __init__(self, nc, n_dma_sems=40):
        self.nc = nc
        self.sem = {e: nc.alloc_semaphore(name="es_" + e) for e in self.ENG}
        self.cnt = {e: 0 for e in self.ENG}
        self.prog = {e: [] for e in self.ENG}
        self.seen = {e: {} for e in self.ENG}
        self.pending = {e: None for e in self.ENG}
        self.dsem = [nc.alloc_semaphore(name="ds_%d" % i) for i in range(n_dma_sems)]
        self.dval = [0] * n_dma_sems
        self.drr = 0
        self.tiles = {}

    def tile(self, sp, lo, hi):
        t = Tk(sp, lo, hi)
        lst = self.tiles.setdefault(sp, [])
        t.ov = [t]
        for o in lst:
            if o.lo < hi and lo < o.hi:
                t.ov.append(o)
                o.ov.append(t)
        lst.append(t)
        return t

    def _need(self, eng, tok):
        if tok[0] == "e":
            _, p, c = tok
            if c > self.cnt[p]:
                ent = self.pending[p]
                assert ent is not None and c == self.cnt[p] + 1, (p, c, self.cnt[p])
                ent["inc"] = True
                self.cnt[p] += 1
                self.pending[p] = None
            key = ("e", p)
            sem = self.sem[p]
            val = c
        else:
            _, i, val = tok
            key = ("d", i)
            sem = self.dsem[i]
        if self.seen[eng].get(key, 0) >= val:
            return None
        self.seen[eng][key] = val
        return (sem, val)

    def _deps(self, eng, r, w):
        toks = []
        for t in r:
            for o in t.ov:
                toks.extend(o.lw)
        same = []
        if eng != "pe":
            same = [tok for tok in toks if tok[0] == "e" and tok[1] == eng]
        for t in w:
            for o in t.ov:
                toks.extend(o.lw)
                toks.extend(o.rd)
        waits = []
        for tok in toks:
            if tok[0] == "e" and tok[1] == eng:
                continue
            wt = self._need(eng, tok)
            if wt is not None:
                waits.append(wt)
        for tok in same:
            wt = self._need(eng, tok)
            if wt is not None:
                waits.append(wt)
        return waits

    def _mark(self, tok, r, w):
        for t in r:
            if not t.rd or t.rd[-1] != tok:
                t.rd.append(tok)
        for t in w:
            for o in t.ov:
                o.lw = [tok]
                o.rd = []

    def op(self, eng, fn, r=(), w=()):
        waits = self._deps(eng, r, w)
        ent = {"fn": fn, "inc": False, "waits": waits, "dma": None}
        self.prog[eng].append(ent)
        self.pending[eng] = ent
        tok = ("e", eng, self.cnt[eng] + 1)
        self._mark(tok, r, w)
        return tok

    def dma(self, eng, fn, r=(), w=()):
        waits = self._deps(eng, r, w)
        i = self.drr
        self.drr = (self.drr + 1) % len(self.dsem)
        if self.dval[i] > 0:
            wt = self._need(eng, ("d", i, self.dval[i]))
            if wt is not None:
                waits.append(wt)
        self.dval[i] += 16
        tok = ("d", i, self.dval[i])
        self.prog[eng].append({"fn": fn, "inc": False, "waits": waits, "dma": i})
        self._mark(tok, r, w)
        return tok

    def wait(self, eng, toks):
        waits = []
        for tok in toks:
            wt = self._need(eng, tok)
            if wt is not None:
                waits.append(wt)
        if waits:
            self.prog[eng].append({"fn": None, "inc": False, "waits": waits, "dma": None})

    def replay(self, eng, e):
        sem = self.sem[eng]
        for ent in self.prog[eng]:
            for (s, v) in ent["waits"]:
                e.wait_ge(s, v)
            if ent["fn"] is None:
                continue
            ins = ent["fn"](e)
            if ent["dma"] is not None:
                ins.then_inc(self.dsem[ent["dma"]], 16)
            elif ent["inc"]:
                ins.then_inc(sem, 1)

    def run(self):
        with self.nc.Block() as block:
            @block.tensor
            def _(e):
                self.replay("pe", e)

            @block.vector
            def _(e):
                self.replay("dve", e)

            @block.scalar
            def _(e):
                self.replay("act", e)

            @block.gpsimd
            def _(e):
                self.replay("pool", e)

            @block.sync
            def _(e):
                self.replay("sp", e)


class Region:
    def __init__(self, S, stack, name, nbytes):
        self.S = S
        self.name = name
        self.nbytes = nbytes
        self.t = stack.enter_context(S.nc.sbuf_tensor(name, [128, nbytes // 4], F32))

    def view(self, lo, dtype, shape):
        n = 1
        for s in shape:
            n *= s
        nb = n * DSZ[dtype]
        assert lo % 4 == 0 and nb % 4 == 0 and lo + nb <= self.nbytes, (self.name, lo, nb)
        ap = self.t[:, lo // 4:(lo + nb) // 4]
        if dtype != F32:
            ap = ap.bitcast(dtype)
        if len(shape) == 2:
            ap = ap.rearrange("p (a b) -> p a b", a=shape[0])
        elif len(shape) == 3:
            ap = ap.rearrange("p (a b c) -> p a b c", a=shape[0], b=shape[1])
        return ap

    def tk(self, lo, nb):
        assert lo + nb <= self.nbytes
        return self.S.tile("sb:" + self.name, lo, lo + nb)


def build_nc(dbg=False):
    nc = bass.Bass("TRN2", target_bir_lowering=False)

    def din(name, shape):
        return nc.dram_tensor(name, shape, F32, kind="ExternalInput").ap()

    x = din("x", [SEQ, D])
    norm1_g = din("norm1_g", [D])
    w_in = din("w_in", [D, 6 * D])
    b_in = din("b_in", [6 * D])
    sgu_ln_g = din("sgu_ln_g", [D])
    sgu_ln_b = din("sgu_ln_b", [D])
    sgu_w = din("sgu_w", [8, 128, 128])
    sgu_b = din("sgu_b", [D])
    w_proj_a = din("w_proj_a", [D, D])
    conv_w = din("conv_w", [CK, D])
    conv_b = din("conv_b", [D])
    conv_ln_g = din("conv_ln_g", [D])
    conv_ln_b = din("conv_ln_b", [D])
    w_proj_b = din("w_proj_b", [D, D])
    b_proj_b = din("b_proj_b", [D])
    w_out = din("w_out", [D, D])
    norm2_g = din("norm2_g", [D])
    w_ff1 = din("w_ff1", [D, 4 * D])
    w_ff2 = din("w_ff2", [4 * D, D])
    norm_f_g = din("norm_f_g", [D])
    out = nc.dram_tensor("out", [SEQ, D], F32, kind="ExternalOutput").ap()
    dbg_t = {}
    if dbg:
        for nm, shp, dt_ in (("d_hT", [128, NT * KC * 512], BF16), ("d_cs", [128, KC * SEQ], BF16),
                             ("d_m2", [128, NT * KC * 512], BF16), ("d_mg", [128, NT * KC * 512], BF16),
                             ("d_x1", [128, NCH * D], F32), ("d_h2", [128, NT * KC * 512], BF16),
                             ("d_vec", [128, 128], F32), ("d_cw", [128, KC * CK], F32),
                             ("d_bg", [128, D], F32), ("d_wst", [128, D], BF16),
                             ("d_vh", [128, 4 * D], BF16), ("d_su", [128, KC * 512], BF16)):
            dbg_t[nm] = nc.dram_tensor(nm, shp, dt_, kind="ExternalOutput").ap()

    S = Sched(nc)
    with contextlib.ExitStack() as st:
        RING = Region(S, st, "RING", 65536)
        P1 = Region(S, st, "P1", 49152)
        P23 = Region(S, st, "P23", 65536)
        CST = Region(S, st, "CST", 28672)
        psf = [st.enter_context(nc.psum_tensor("ps%d" % i, [128, 512], F32)) for i in range(8)]
        psb = [p[:, :].bitcast(BF16) for p in psf]
        pst = [S.tile("ps", i, i + 1) for i in range(8)]
        bank_rr = [0]

        def newbank():
            i = bank_rr[0]
            bank_rr[0] = (i + 1) % 8
            return i

        out_toks = []

        def dump(nm, ap, tks):
            if dbg:
                out_toks.append(S.dma("sp", lambda e: e.dma_start(out=dbg_t[nm][:, :], in_=ap), r=tks))

        coff = [0]

        def calloc(dtype, shape):
            n = int(np.prod(shape)) * DSZ[dtype]
            n = (n + 31) // 32 * 32
            v = CST.view(coff[0], dtype, shape)
            t = CST.tk(coff[0], n)
            coff[0] += n
            return v, t

        ident, identT = calloc(BF16, [128])
        identf, identfT = calloc(F32, [128])
        vec, vecT = calloc(F32, [128])
        convw, convwT = calloc(F32, [KC, 32])
        gb1, gb1T = calloc(F32, [D])
        gb2, gb2T = calloc(F32, [D])
        gbf, gbfT = calloc(F32, [D])
        Bg, BgT = calloc(F32, [D])
        WsT, WsTT = calloc(BF16, [8, 128])
        bvrow, bvrowT = calloc(BF16, [D])
        onesrow, onesrowT = calloc(BF16, [128])
        onesm, onesmT = calloc(BF16, [128])
        epsc, epscT = calloc(F32, [8])
        ss1, ss1T = calloc(F32, [NCH])
        ss2, ss2T = calloc(F32, [NCH])
        ss3, ss3T = calloc(F32, [NCH])
        rs1, rs1T = calloc(F32, [NCH])
        sd1, sd1T = calloc(F32, [NCH])
        bst, bstT = calloc(F32, [4, 12])
        mv, mvT = calloc(F32, [4, 2])
        rv, rvT = calloc(F32, [4])
        sdv, sdvT = calloc(F32, [4])
        junk, junkT = calloc(F32, [D])
        V_BU, V_BV, V_BVAL, V_BGATE, V_BG0, V_BG1, V_N1, V_N2, V_SLG, V_CB, V_CLG, V_CLB, V_BPB = [8 * i for i in range(13)]

        S.op("pool", lambda e: e.memset(identf[:, :], 0.0), w=[identfT])
        S.op("pool", lambda e: e.affine_select(out=identf[:, :], in_=identf[:, :], pattern=[[-1, 128]],
                                               compare_op=ALU.not_equal, fill=1.0, base=0, channel_multiplier=1),
             r=[identfT], w=[identfT])
        S.op("dve", lambda e: e.tensor_copy(out=ident[:, :], in_=identf[:, :]), r=[identfT], w=[identT])
        S.op("pool", lambda e: e.memset(onesrow[0:1, :], 1.0), w=[onesrowT])
        S.op("pool", lambda e: e.memset(onesm[:, :], 1.0 / D), w=[onesmT])
        S.op("pool", lambda e: e.memset(epsc[:, :], EPS), w=[epscT])
        for s_, sT_ in ((ss1, ss1T), (ss2, ss2T), (ss3, ss3T)):
            S.op("pool", lambda e, s_=s_: e.memset(s_[:, :], 0.0), w=[sT_])
        slotv = [RING.view(s * 16384, BF16, [8, 1024]) for s in range(4)]
        slotT = [[RING.tk(s * 16384 + kp * 4096, 4096) for kp in range(4)] for s in range(4)]
        w_in_v = w_in.rearrange("(k p) e -> p k e", p=128)
        wsrc = {
            "u": w_in_v[:, :, 0 * D:1 * D], "v": w_in_v[:, :, 1 * D:2 * D], "val": w_in_v[:, :, 2 * D:3 * D],
            "gate": w_in_v[:, :, 3 * D:4 * D], "g0": w_in_v[:, :, 4 * D:5 * D], "g1": w_in_v[:, :, 5 * D:6 * D],
            "pa": w_proj_a.rearrange("(k p) e -> p k e", p=128), "pb": w_proj_b.rearrange("(k p) e -> p k e", p=128),
            "out": w_out.rearrange("(k p) e -> p k e", p=128),
        }
        w1v = w_ff1.rearrange("(k p) e -> p k e", p=128)
        w2v = w_ff2.rearrange("(j p) e -> p j e", p=128)
        for q in range(4):
            wsrc["f1_%d" % q] = w1v[:, :, q * D:(q + 1) * D]
            wsrc["f2_%d" % q] = w2v[:, q * 8:(q + 1) * 8, :]
        worder = ["val", "gate", "pb", "g1", "v", "u", "pa", "g0", "out",
                  "f1_0", "f2_0", "f1_1", "f2_1", "f1_2", "f2_2", "f1_3", "f2_3"]
        wslot = {nm: i % 4 for i, nm in enumerate(worder)}
        wnext = [0]

        def load_next_w():
            if wnext[0] >= len(worder):
                return
            nm = worder[wnext[0]]
            wnext[0] += 1
            s = wslot[nm]
            src = wsrc[nm]
            for kp in range(4):
                S.dma("pool", lambda e, s=s, kp=kp, src=src: e.dma_start(out=slotv[s][:, 2 * kp:2 * kp + 2, :], in_=src[:, 2 * kp:2 * kp + 2, :]),
                      w=[slotT[s][kp]])

        def W(nm):
            s = wslot[nm]
            return slotv[s], slotT[s]

        for _ in range(2):
            load_next_w()

        so = 32768
        VR = P23.view(so, F32, [128]); VRT = P23.tk(so, 512); so += 512
        CW = P23.view(so, F32, [D]); CWT = P23.tk(so, 4096); so += 4096
        Wtf = P23.view(so, F32, [8, 128]); WtfT = P23.tk(so, 4096); so += 4096
        Wtb = P23.view(so, BF16, [8, 128]); WtbT = P23.tk(so, 2048); so += 2048
        betaf = P23.view(so, F32, [D]); betafT = P23.tk(so, 4096); so += 4096
        betab = P23.view(so, BF16, [D]); betabT = P23.tk(so, 2048); so += 2048

        hT4 = P23.view(0, BF16, [NT, KC, 512])
        hTT = [P23.tk(T * 8192, 8192) for T in range(NT)]
        xt = [P1.view(i * 16384, F32, [4, D]) for i in range(2)]
        xtT = [[P1.tk(i * 16384 + cc * 4096, 4096) for cc in range(4)] for i in range(2)]
        xv = x.rearrange("(c p) d -> p c d", p=128)

        def load_x_tile(T):
            i = T % 2
            for cc in range(4):
                c = T * 4 + cc
                S.dma("sp", lambda e, i=i, cc=cc, c=c: e.dma_start(out=xt[i][:, cc, :], in_=xv[:, c, :]), w=[xtT[i][cc]])

        S.op("dve", lambda e: e.memset(VR[:, :], 0.0), w=[VRT])
        load_x_tile(0)
        S.dma("sp", lambda e: e.dma_start(out=gb1[:, :], in_=norm1_g.partition_broadcast(128)), w=[gb1T])
        S.dma("sp", lambda e: e.dma_start(out=VR[0:48, :], in_=b_in.rearrange("(r p) -> r p", p=128)), w=[VRT])
        for i, v_ in enumerate((norm1_g, norm2_g, sgu_ln_g, conv_b, conv_ln_g, conv_ln_b, b_proj_b)):
            S.dma("sp", lambda e, i=i, v_=v_: e.dma_start(out=VR[48 + 8 * i:56 + 8 * i, :], in_=v_.rearrange("(r p) -> r p", p=128)), w=[VRT])
        S.dma("sp", lambda e: e.dma_start(out=CW[0:CK, :], in_=conv_w[:, :]), w=[CWT])
        load_x_tile(1)
        S.dma("sp", lambda e: e.dma_start(out=Wtf[:, :, :], in_=sgu_w.rearrange("g t s -> t g s")), w=[WtfT])
        S.dma("sp", lambda e: e.dma_start(out=Bg[:, :], in_=sgu_b.partition_broadcast(128)), w=[BgT])
        S.dma("sp", lambda e: e.dma_start(out=betaf[:, :], in_=sgu_ln_b.partition_broadcast(128)), w=[betafT])
        S.dma("sp", lambda e: e.dma_start(out=gb2[:, :], in_=norm2_g.partition_broadcast(128)), w=[gb2T])
        S.dma("sp", lambda e: e.dma_start(out=gbf[:, :], in_=norm_f_g.partition_broadcast(128)), w=[gbfT])
        S.dma("pool", lambda e: e.dma_start(out=bvrow[0:1, :], in_=b_in[D:2 * D].rearrange("(o n) -> o n", o=1)), w=[bvrowT])

        bk = newbank()
        S.op("pe", lambda e, bk=bk: e.transpose(out=psf[bk][:, 0:104], in_=VR[0:104, :], identity=identf[0:104, 0:104]),
             r=[VRT, identfT], w=[pst[bk]])
        S.op("dve", lambda e, bk=bk: e.tensor_copy(out=vec[:, 0:104], in_=psf[bk][:, 0:104]), r=[pst[bk]], w=[vecT])
        bk = newbank()
        for j in range(KC):
            S.op("pe", lambda e, bk=bk, j=j: e.transpose(out=psf[bk][:, j * 32:j * 32 + CK], in_=CW[0:CK, j * 128:(j + 1) * 128],
                                                         identity=identf[0:CK, 0:CK]), r=[CWT, identfT], w=[pst[bk]])
        S.op("dve", lambda e, bk=bk: e.tensor_copy(out=convw[:, :, 0:CK],
                                                   in_=psf[bk][:, 0:256].rearrange("p (j k) -> p j k", k=32)[:, :, 0:CK]),
             r=[pst[bk]], w=[convwT])
        dump("d_vec", vec[:, :], [vecT])
        dump("d_cw", convw[:, :, 0:CK], [convwT])

        xn = [P1.view(32768 + i * 2048, BF16, [D]) for i in range(8)]
        xnT = [P1.tk(32768 + i * 2048, 2048) for i in range(8)]
        xn_rr = [0]

        def norm_to_hT(T, src_ap_fn, src_tk_fn, gb, gbT, dst4, dstT, ss, ssT, rs, rsT, sd, sdT, xn_lo=0):
            for cc in range(4):
                c = T * 4 + cc
                S.op("act", lambda e, cc=cc, c=c: e.activation(out=junk[:, :], in_=src_ap_fn(cc), func=AF.Square, accum_out=ss[:, c:c + 1]),
                     r=[src_tk_fn(cc)], w=[junkT, ssT])
            S.op("dve", lambda e: e.tensor_scalar(out=sd[:, T * 4:T * 4 + 4], in0=ss[:, T * 4:T * 4 + 4], scalar1=1.0 / D, scalar2=EPS,
                                                  op0=ALU.mult, op1=ALU.add), r=[ssT], w=[sdT])
            S.op("act", lambda e: e.activation(out=sd[:, T * 4:T * 4 + 4], in_=sd[:, T * 4:T * 4 + 4], func=AF.Sqrt), r=[sdT], w=[sdT])
            S.op("dve", lambda e: e.reciprocal(out=rs[:, T * 4:T * 4 + 4], in_=sd[:, T * 4:T * 4 + 4]), r=[sdT], w=[rsT])
            for cc in range(4):
                c = T * 4 + cc
                xi = xn_lo + xn_rr[0] % (8 - xn_lo)
                xn_rr[0] += 1
                S.op("dve", lambda e, cc=cc, c=c, xi=xi: e.scalar_tensor_tensor(out=xn[xi][:, :], in0=src_ap_fn(cc), scalar=rs[:, c:c + 1],
                                                                               in1=gb[:, :], op0=ALU.mult, op1=ALU.mult),
                     r=[src_tk_fn(cc), rsT, gbT], w=[xnT[xi]])
                bk = newbank()
                for k in range(KC):
                    S.op("pe", lambda e, bk=bk, k=k, xi=xi: e.transpose(out=psb[bk][:, k * 128:(k + 1) * 128], in_=xn[xi][:, k * 128:(k + 1) * 128],
                                                                        identity=ident[:, :]), r=[xnT[xi], identT], w=[pst[bk]])
                eng = "act" if (cc % 2 == 0) else "dve"
                if eng == "act":
                    S.op("act", lambda e, bk=bk, cc=cc: e.activation(out=dst4[:, T, :, cc * 128:(cc + 1) * 128],
                                                                     in_=psb[bk][:, :].rearrange("p (k t) -> p k t", k=KC), func=AF.Copy),
                         r=[pst[bk]], w=[dstT[T]])
                else:
                    S.op("dve", lambda e, bk=bk, cc=cc: e.tensor_copy(out=dst4[:, T, :, cc * 128:(cc + 1) * 128],
                                                                      in_=psb[bk][:, :].rearrange("p (k t) -> p k t", k=KC)),
                         r=[pst[bk]], w=[dstT[T]])

        for T in range(NT):
            i = T % 2
            norm_to_hT(T, lambda cc, i=i: xt[i][:, cc, :], lambda cc, i=i: xtT[i][cc], gb1, gb1T, hT4, hTT, ss1, ss1T, rs1, rs1T, sd1, sd1T)
            if T + 2 < NT:
                load_x_tile(T + 2)
        dump("d_hT", hT4[:, :, :, :], hTT)
        S.op("pool", lambda e: e.affine_select(out=Wtf[:, :, :], in_=Wtf[:, :, :], pattern=[[0, 8], [-1, 128]],
                                               compare_op=ALU.is_ge, fill=0.0, base=0, channel_multiplier=1),
             r=[WtfT], w=[WtfT])
        S.op("dve", lambda e: e.tensor_copy(out=Wtb[:, :, :], in_=Wtf[:, :, :]), r=[WtfT], w=[WtbT])
        bk = newbank()
        for g in range(8):
            S.op("pe", lambda e, bk=bk, g=g: e.transpose(out=psb[bk][:, g * 128:(g + 1) * 128], in_=Wtb[:, g, :], identity=ident[:, :]),
                 r=[WtbT, identT], w=[pst[bk]])
        S.op("dve", lambda e, bk=bk: e.tensor_copy(out=WsT[:, :, :], in_=psb[bk][:, :].rearrange("p (g t) -> p g t", g=8)),
             r=[pst[bk]], w=[WsTT])
        S.op("dve", lambda e: e.tensor_copy(out=betab[:, :], in_=betaf[:, :]), r=[betafT], w=[betabT])
        for hh in range(2):
            bk = newbank()
            for g4 in range(4):
                g = hh * 4 + g4
                S.op("pe", lambda e, bk=bk, g=g, g4=g4: e.matmul(psf[bk][:, g4 * 128:(g4 + 1) * 128], lhsT=betab[:, g * 128:(g + 1) * 128],
                                                                rhs=WsT[:, g, :], start=True, stop=True),
                     r=[betabT, WsTT], w=[pst[bk]])
            S.op("dve", lambda e, bk=bk, hh=hh: e.tensor_tensor(out=Bg[:, hh * 512:(hh + 1) * 512], in0=psf[bk][:, :],
                                                               in1=Bg[:, hh * 512:(hh + 1) * 512], op=ALU.add),
                 r=[pst[bk], BgT], w=[BgT])
        dump("d_bg", Bg[:, :], [BgT])
        dump("d_wst", WsT[:, :, :], [WsTT])

        CV = P23.view(32768, BF16, [KC, SEQ])
        CVT = [[P23.tk(32768 + j * 4096 + T * 1024, 1024) for T in range(NT)] for j in range(KC)]
        APAD = 2080
        abuf = [P1.view(i * 4160, BF16, [APAD]) for i in range(2)]
        abufT = [[P1.tk(i * 4160, 60)] + [P1.tk(i * 4160 + 60 + T * 1024, 1024) for T in range(NT)] for i in range(2)]
        dg = [P1.view(8320 + i * 7936, BF16, [CK, 128]) for i in range(2)]
        dgT = [P1.tk(8320 + i * 7936, 7936) for i in range(2)]
        sig = [P1.view(24192 + i * 1024, BF16, [512]) for i in range(2)]
        sigT = [P1.tk(24192 + i * 1024, 1024) for i in range(2)]
        M2 = P1.view(0, BF16, [NT, KC, 512])
        M2T = [P1.tk(T * 8192, 8192) for T in range(NT)]
        sqt = [P1.view(26240 + i * 1024, BF16, [512]) for i in range(8)]
        sqtT = [P1.tk(26240 + i * 1024, 1024) for i in range(8)]
        NYB = 3
        yb = [P1.view(34432 + i * 2048, F32, [512]) for i in range(NYB)]
        ybT = [P1.tk(34432 + i * 2048, 2048) for i in range(NYB)]
        stt = P1.view(40576, F32, [512]); sttT = P1.tk(40576, 2048)
        rstdb = P1.view(42624, F32, [512]); rstdbT = P1.tk(42624, 2048)
        nmrb = P1.view(44672, F32, [512]); nmrbT = P1.tk(44672, 2048)
        sg = [P1.view(46720 + i * 1024, BF16, [512]) for i in range(2)]
        sgT = [P1.tk(46720 + i * 1024, 1024) for i in range(2)]
        Wpb, WpbT = W("pb")
        Wg1, Wg1T = W("g1")

        def btail_sq(T):
            for j in range(KC):
                S.op("dve", lambda e, j=j: e.tensor_tensor(out=sqt[j][:, :], in0=CV[:, j, T * 512:(T + 1) * 512], in1=CV[:, j, T * 512:(T + 1) * 512], op=ALU.mult),
                     r=[CVT[j][T]], w=[sqtT[j]])

        def btail_stats(T):
            bM = newbank()
            bQ = newbank()
            for j in range(KC):
                qi = j
                S.op("pe", lambda e, j=j, bM=bM: e.matmul(psf[bM][:, :], lhsT=onesm[:, :], rhs=CV[:, j, T * 512:(T + 1) * 512], start=(j == 0), stop=(j == KC - 1)),
                     r=[onesmT, CVT[j][T]], w=[pst[bM]])
                S.op("pe", lambda e, j=j, bQ=bQ, qi=qi: e.matmul(psf[bQ][:, :], lhsT=onesm[:, :], rhs=sqt[qi][:, :], start=(j == 0), stop=(j == KC - 1)),
                     r=[onesmT, sqtT[qi]], w=[pst[bQ]])
            S.op("act", lambda e: e.activation(out=stt[:, :], in_=psf[bM][:, :], func=AF.Square), r=[pst[bM]], w=[sttT])
            S.op("dve", lambda e: e.tensor_tensor(out=stt[:, :], in0=psf[bQ][:, :], in1=stt[:, :], op=ALU.subtract), r=[pst[bQ], sttT], w=[sttT])
            S.op("act", lambda e: e.activation(out=stt[:, :], in_=stt[:, :], func=AF.Sqrt, bias=epsc[:, 0:1], scale=1.0), r=[sttT, epscT], w=[sttT])
            S.op("dve", lambda e: e.reciprocal(out=rstdb[:, :], in_=stt[:, :]), r=[sttT], w=[rstdbT])
            S.op("dve", lambda e: e.scalar_tensor_tensor(out=nmrb[:, :], in0=psf[bM][:, :], scalar=-1.0, in1=rstdb[:, :], op0=ALU.mult, op1=ALU.mult),
                 r=[pst[bM], rstdbT], w=[nmrbT])

        def btail_norm(T):
            for j in range(KC):
                yi = j % NYB
                S.op("dve", lambda e, j=j, yi=yi: e.tensor_tensor(out=yb[yi][:, :], in0=CV[:, j, T * 512:(T + 1) * 512], in1=rstdb[:, :], op=ALU.mult),
                     r=[CVT[j][T], rstdbT], w=[ybT[yi]])
                S.op("dve", lambda e, j=j, yi=yi: e.tensor_tensor(out=yb[yi][:, :], in0=yb[yi][:, :], in1=nmrb[:, :], op=ALU.add),
                     r=[ybT[yi], nmrbT], w=[ybT[yi]])
                S.op("act", lambda e, j=j, yi=yi: e.activation(out=CV[:, j, T * 512:(T + 1) * 512], in_=yb[yi][:, :], func=AF.Silu,
                                                               bias=vec[:, V_CLB + j:V_CLB + j + 1], scale=vec[:, V_CLG + j:V_CLG + j + 1]),
                     r=[ybT[yi], vecT], w=[CVT[j][T]])

        sg_rr = [0]

        def btail_proj(T):
            for m in range(KC):
                bG = newbank()
                for k in range(KC):
                    S.op("pe", lambda e, bG=bG, k=k, m=m: e.matmul(psf[bG][:, :], lhsT=Wg1[:, k, m * 128:(m + 1) * 128], rhs=hT4[:, T, k, :],
                                                                  start=(k == 0), stop=(k == KC - 1)),
                         r=[Wg1T[k // 2], hTT[T]], w=[pst[bG]])
                bY = newbank()
                for j in range(KC):
                    S.op("pe", lambda e, bY=bY, j=j, m=m: e.matmul(psf[bY][:, :], lhsT=Wpb[:, j, m * 128:(m + 1) * 128], rhs=CV[:, j, T * 512:(T + 1) * 512],
                                                                  start=(j == 0), stop=(j == KC - 1)),
                         r=[WpbT[j // 2], CVT[j][T]], w=[pst[bY]])
                si = sg_rr[0]
                sg_rr[0] = (si + 1) % 2
                S.op("act", lambda e, bG=bG, si=si, m=m: e.activation(out=sg[si][:, :], in_=psf[bG][:, :], func=AF.Sigmoid,
                                                                     bias=vec[:, V_BG1 + m:V_BG1 + m + 1], scale=1.0),
                     r=[pst[bG], vecT], w=[sgT[si]])
                S.op("dve", lambda e, bY=bY, si=si, m=m: e.scalar_tensor_tensor(out=M2[:, T, m, :], in0=psf[bY][:, :], scalar=vec[:, V_BPB + m:V_BPB + m + 1],
                                                                               in1=sg[si][:, :], op0=ALU.add, op1=ALU.mult),
                     r=[pst[bY], vecT, sgT[si]], w=[M2T[T]])

        Wval, WvalT = W("val")
        Wgate, WgateT = W("gate")
        for i in range(2):
            S.op("pool", lambda e, i=i: e.memset(abuf[i][:, 0:30], 0.0), w=[abufT[i][0]])
        sig_rr = [0]
        for j in range(KC):
            ab = abuf[j % 2]
            abT = abufT[j % 2]
            dgi = dg[j % 2]
            dgiT = dgT[j % 2]
            S.op("dve", lambda e, dgi=dgi, j=j: e.tensor_tensor(out=dgi[:, :, :], in0=ident[:, None, :].to_broadcast([128, CK, 128]),
                                                                in1=convw[:, j, 0:CK, None].to_broadcast([128, CK, 128]), op=ALU.mult),
                 r=[identT, convwT], w=[dgiT])
            pend = None
            for T in range(NT + 1):
                if T < NT:
                    bA = newbank()
                    for k in range(KC):
                        S.op("pe", lambda e, bA=bA, k=k, j=j, T=T: e.matmul(psf[bA][:, :], lhsT=Wval[:, k, j * 128:(j + 1) * 128], rhs=hT4[:, T, k, :],
                                                                           start=(k == 0), stop=(k == KC - 1)),
                             r=[WvalT[k // 2], hTT[T]], w=[pst[bA]])
                    bB = newbank()
                    for k in range(KC):
                        S.op("pe", lambda e, bB=bB, k=k, j=j, T=T: e.matmul(psf[bB][:, :], lhsT=Wgate[:, k, j * 128:(j + 1) * 128], rhs=hT4[:, T, k, :],
                                                                           start=(k == 0), stop=(k == KC - 1)),
                             r=[WgateT[k // 2], hTT[T]], w=[pst[bB]])
                    si = sig_rr[0]
                    sig_rr[0] = (si + 1) % 2
                    S.op("act", lambda e, bB=bB, si=si, j=j: e.activation(out=sig[si][:, :], in_=psf[bB][:, :], func=AF.Sigmoid,
                                                                         bias=vec[:, V_BGATE + j:V_BGATE + j + 1], scale=1.0),
                         r=[pst[bB], vecT], w=[sigT[si]])
                    S.op("dve", lambda e, bA=bA, si=si, j=j, T=T, ab=ab: e.scalar_tensor_tensor(out=ab[:, 30 + T * 512:30 + (T + 1) * 512], in0=psf[bA][:, :],
                                                                                                  scalar=vec[:, V_BVAL + j:V_BVAL + j + 1], in1=sig[si][:, :],
                                                                                                  op0=ALU.add, op1=ALU.mult),
                         r=[pst[bA], vecT, sigT[si]], w=[abT[T + 1]])
                if pend is not None:
                    Tp = pend
                    bC = newbank()
                    for k in range(CK):
                        S.op("pe", lambda e, bC=bC, k=k, Tp=Tp, ab=ab, dgi=dgi: e.matmul(psf[bC][:, :], lhsT=dgi[:, k, :], rhs=ab[:, Tp * 512 + k:Tp * 512 + k + 512],
                                                                                        start=(k == 0), stop=(k == CK - 1)),
                             r=[dgiT, abT[Tp], abT[Tp + 1]], w=[pst[bC]])
                    S.op("act", lambda e, bC=bC, Tp=Tp, j=j: e.activation(out=CV[:, j, Tp * 512:(Tp + 1) * 512], in_=psf[bC][:, :], func=AF.Identity,
                                                                         bias=vec[:, V_CB + j:V_CB + j + 1], scale=1.0),
                         r=[pst[bC], vecT], w=[CVT[j][Tp]])
                    if j == KC - 1:
                        if Tp >= 1:
                            btail_stats(Tp - 1)
                            btail_norm(Tp - 1)
                        btail_sq(Tp)
                pend = T if T < NT else None
            if j == 1:
                load_next_w()
                load_next_w()
        load_next_w()
        load_next_w()

        btail_proj(0)
        btail_stats(NT - 1)
        btail_norm(NT - 1)
        for T in range(1, NT):
            btail_proj(T)
        dump("d_cs", CV[:, :, :], [t for row in CVT for t in row])
        dump("d_m2", M2[:, :, :, :], M2T)
        load_next_w()
        load_next_w()

        vh = [P23.view(32768 + i * 8192, BF16, [4, D]) for i in range(2)]
        vhT = [[P23.tk(32768 + i * 8192 + cc * 2048, 2048) for cc in range(4)] for i in range(2)]
        ub = [P23.view(49152 + i * 8192, BF16, [KC, 512]) for i in range(2)]
        ubT = [[P23.tk(49152 + i * 8192 + m * 1024, 1024) for m in range(KC)] for i in range(2)]
        stmp = [P1.view(32768 + i * 1024, BF16, [512]) for i in range(2)]
        stmpT = [P1.tk(32768 + i * 1024, 1024) for i in range(2)]
        sg0 = [P1.view(34816 + i * 1024, BF16, [512]) for i in range(2)]
        sg0T = [P1.tk(34816 + i * 1024, 1024) for i in range(2)]
        t1 = [P1.view(36864 + i * 1024, BF16, [512]) for i in range(2)]
        t1T = [P1.tk(36864 + i * 1024, 1024) for i in range(2)]
        Wv, WvT = W("v")
        Wu, WuT = W("u")
        Wpa, WpaT = W("pa")
        Wg0, Wg0T = W("g0")
        rr2 = [0, 0, 0]

        def stageA_v(T):
            i = T % 2
            for cc in range(4):
                bV = [newbank(), newbank()]
                for hf in range(2):
                    for k in range(KC):
                        S.op("pe", lambda e, b=bV[hf], k=k, cc=cc, hf=hf: e.matmul(psf[b][:, :], lhsT=hT4[:, T, k, cc * 128:(cc + 1) * 128],
                                                                                  rhs=Wv[:, k, hf * 512:(hf + 1) * 512], start=(k == 0), stop=False),
                             r=[hTT[T], WvT[k // 2]], w=[pst[bV[hf]]])
                    S.op("pe", lambda e, b=bV[hf], hf=hf: e.matmul(psf[b][:, :], lhsT=onesrow[0:1, :], rhs=bvrow[0:1, hf * 512:(hf + 1) * 512], start=False, stop=True),
                         r=[onesrowT, bvrowT], w=[pst[bV[hf]]])
                    S.op("act", lambda e, b=bV[hf], hf=hf, cc=cc, i=i: e.activation(out=vh[i][:, cc, hf * 512:(hf + 1) * 512], in_=psf[b][:, :], func=AF.Gelu_apprx_tanh),
                         r=[pst[bV[hf]]], w=[vhT[i][cc]])
                    S.op("dve", lambda e, hf=hf, cc=cc, i=i: e.bn_stats(out=bst[:, cc, hf * 6:(hf + 1) * 6], in_=vh[i][:, cc, hf * 512:(hf + 1) * 512]),
                         r=[vhT[i][cc]], w=[bstT])
                S.op("dve", lambda e, cc=cc: e.bn_aggr(out=mv[:, cc, :], in_=bst[:, cc, :]), r=[bstT], w=[mvT])
            S.op("dve", lambda e: e.tensor_scalar(out=sdv[:, :], in0=mv[:, :, 1], scalar1=EPS, scalar2=None, op0=ALU.add), r=[mvT], w=[sdvT])
            S.op("act", lambda e: e.activation(out=sdv[:, :], in_=sdv[:, :], func=AF.Sqrt), r=[sdvT], w=[sdvT])
            S.op("dve", lambda e: e.reciprocal(out=rv[:, :], in_=sdv[:, :]), r=[sdvT], w=[rvT])
            for cc in range(4):
                S.op("dve", lambda e, cc=cc, i=i: e.tensor_scalar(out=vh[i][:, cc, :], in0=vh[i][:, cc, :], scalar1=mv[:, cc, 0:1], scalar2=rv[:, cc:cc + 1],
                                                                  op0=ALU.subtract, op1=ALU.mult),
                     r=[vhT[i][cc], mvT, rvT], w=[vhT[i][cc]])

        def stageA_u(T):
            i = T % 2
            for m in range(KC):
                bU = newbank()
                for k in range(KC):
                    S.op("pe", lambda e, bU=bU, k=k, m=m: e.matmul(psf[bU][:, :], lhsT=Wu[:, k, m * 128:(m + 1) * 128], rhs=hT4[:, T, k, :],
                                                                  start=(k == 0), stop=(k == KC - 1)),
                         r=[WuT[k // 2], hTT[T]], w=[pst[bU]])
                S.op("act", lambda e, bU=bU, m=m, i=i: e.activation(out=ub[i][:, m, :], in_=psf[bU][:, :], func=AF.Gelu_apprx_tanh,
                                                                   bias=vec[:, V_BU + m:V_BU + m + 1], scale=1.0),
                     r=[pst[bU], vecT], w=[ubT[i][m]])

        def stageA_sgu(T):
            i = T % 2
            for g in range(8):
                bS = newbank()
                for cc in range(4):
                    S.op("pe", lambda e, bS=bS, g=g, cc=cc, i=i: e.matmul(psf[bS][:, cc * 128:(cc + 1) * 128], lhsT=vh[i][:, cc, g * 128:(g + 1) * 128],
                                                                         rhs=WsT[:, g, :], start=True, stop=True),
                         r=[vhT[i][cc], WsTT], w=[pst[bS]])
                si = rr2[0]
                rr2[0] = (si + 1) % 2
                S.op("dve", lambda e, bS=bS, g=g, si=si: e.scalar_tensor_tensor(out=stmp[si][:, :].rearrange("p (c t) -> p c t", c=4),
                                                                               in0=psf[bS][:, :].rearrange("p (c t) -> p c t", c=4),
                                                                               scalar=vec[:, V_SLG + g:V_SLG + g + 1],
                                                                               in1=Bg[:, None, g * 128:(g + 1) * 128].to_broadcast([128, 4, 128]),
                                                                               op0=ALU.mult, op1=ALU.add),
                     r=[pst[bS], vecT, BgT], w=[stmpT[si]])
                S.op("dve", lambda e, g=g, si=si, i=i: e.tensor_tensor(out=ub[i][:, g, :], in0=ub[i][:, g, :], in1=stmp[si][:, :], op=ALU.mult),
                     r=[ubT[i][g], stmpT[si]], w=[ubT[i][g]])

        def stageA_proj(T):
            i = T % 2
            for m in range(KC):
                bG = newbank()
                for k in range(KC):
                    S.op("pe", lambda e, bG=bG, k=k, m=m: e.matmul(psf[bG][:, :], lhsT=Wg0[:, k, m * 128:(m + 1) * 128], rhs=hT4[:, T, k, :],
                                                                  start=(k == 0), stop=(k == KC - 1)),
                         r=[Wg0T[k // 2], hTT[T]], w=[pst[bG]])
                bY = newbank()
                for j in range(KC):
                    S.op("pe", lambda e, bY=bY, j=j, m=m, i=i: e.matmul(psf[bY][:, :], lhsT=Wpa[:, j, m * 128:(m + 1) * 128], rhs=ub[i][:, j, :],
                                                                       start=(j == 0), stop=(j == KC - 1)),
                         r=[WpaT[j // 2], ubT[i][j]], w=[pst[bY]])
                si = rr2[1]
                rr2[1] = (si + 1) % 2
                S.op("act", lambda e, bG=bG, si=si, m=m: e.activation(out=sg0[si][:, :], in_=psf[bG][:, :], func=AF.Sigmoid,
                                                                     bias=vec[:, V_BG0 + m:V_BG0 + m + 1], scale=1.0),
                     r=[pst[bG], vecT], w=[sg0T[si]])
                S.op("dve", lambda e, bY=bY, si=si: e.tensor_tensor(out=t1[si][:, :], in0=psf[bY][:, :], in1=sg0[si][:, :], op=ALU.mult),
                     r=[pst[bY], sg0T[si]], w=[t1T[si]])
                S.op("dve", lambda e, si=si, m=m: e.tensor_tensor(out=M2[:, T, m, :], in0=M2[:, T, m, :], in1=t1[si][:, :], op=ALU.add),
                     r=[M2T[T], t1T[si]], w=[M2T[T]])

        stageA_v(0)
        stageA_u(0)
        for T in range(NT):
            stageA_sgu(T)
            if T == 0 and dbg:
                dump("d_vh", vh[0][:, :, :], vhT[0])
            if T + 1 < NT:
                stageA_v(T + 1)
                if T + 1 == NT - 1:
                    load_next_w()
                stageA_u(T + 1)
                if T + 1 == NT - 1:
                    load_next_w()
            stageA_proj(T)
            if T == 0 and dbg:
                dump("d_su", ub[0][:, :, :], ubT[0])
        dump("d_mg", M2[:, :, :, :], M2T)
        load_next_w()
        load_next_w()

        X = P23.view(0, F32, [NCH, D])
        XT = [P23.tk(c * 4096, 4096) for c in range(NCH)]
        h2T4 = P1.view(0, BF16, [NT, KC, 512])
        h2TT = M2T
        Wout, WoutT = W("out")
        for c in range(NCH):
            S.dma("sp", lambda e, c=c: e.dma_start(out=X[:, c, :], in_=xv[:, c, :]), w=[XT[c]])

        def stageW(T):
            for cc in range(4):
                c = T * 4 + cc
                for hf in range(2):
                    bO = newbank()
                    for m in range(KC):
                        S.op("pe", lambda e, bO=bO, m=m, cc=cc, hf=hf: e.matmul(psf[bO][:, :], lhsT=M2[:, T, m, cc * 128:(cc + 1) * 128],
                                                                               rhs=Wout[:, m, hf * 512:(hf + 1) * 512], start=(m == 0), stop=(m == KC - 1)),
                             r=[M2T[T], WoutT[m // 2]], w=[pst[bO]])
                    S.op("dve", lambda e, bO=bO, c=c, hf=hf: e.tensor_tensor(out=X[:, c, hf * 512:(hf + 1) * 512], in0=psf[bO][:, :],
                                                                            in1=X[:, c, hf * 512:(hf + 1) * 512], op=ALU.add),
                         r=[pst[bO], XT[c]], w=[XT[c]])

        def norm2(T, xn_lo=0):
            norm_to_hT(T, lambda cc, T=T: X[:, T * 4 + cc, :], lambda cc, T=T: XT[T * 4 + cc], gb2, gb2T, h2T4, h2TT, ss2, ss2T, rs1, rs1T, sd1, sd1T,
                       xn_lo=xn_lo)

        stageW(0)
        stageW(1)
        norm2(0)
        stageW(2)
        norm2(1)
        stageW(3)
        load_next_w()
        norm2(2)
        if dbg:
            dump("d_x1", X[:, :, :], XT)

        fb = [P1.view(32768 + i * 8192, BF16, [KC, 512]) for i in range(2)]
        fbT = [[P1.tk(32768 + i * 8192 + j * 1024, 1024) for j in range(KC)] for i in range(2)]
        fb_rr = [0]

        def ffn1(q, T, fi):
            W1, W1T = W("f1_%d" % q)
            for j in range(KC):
                bF = newbank()
                for k in range(KC):
                    S.op("pe", lambda e, bF=bF, k=k, j=j, W1=W1: e.matmul(psf[bF][:, :], lhsT=W1[:, k, j * 128:(j + 1) * 128], rhs=h2T4[:, T, k, :],
                                                                         start=(k == 0), stop=(k == KC - 1)),
                         r=[W1T[k // 2], h2TT[T]], w=[pst[bF]])
                S.op("act", lambda e, bF=bF, j=j, fi=fi: e.activation(out=fb[fi][:, j, :], in_=psf[bF][:, :], func=AF.Relu), r=[pst[bF]], w=[fbT[fi][j]])
                S.op("dve", lambda e, j=j, fi=fi: e.tensor_tensor(out=fb[fi][:, j, :], in0=fb[fi][:, j, :], in1=fb[fi][:, j, :], op=ALU.mult),
                     r=[fbT[fi][j]], w=[fbT[fi][j]])

        def ffn2(q, T, fi):
            W2, W2T = W("f2_%d" % q)
            for cc in range(4):
                c = T * 4 + cc
                for hf in range(2):
                    bO = newbank()
                    for j in range(KC):
                        S.op("pe", lambda e, bO=bO, j=j, cc=cc, hf=hf, W2=W2: e.matmul(psf[bO][:, :], lhsT=fb[fi][:, j, cc * 128:(cc + 1) * 128],
                                                                                      rhs=W2[:, j, hf * 512:(hf + 1) * 512], start=(j == 0), stop=(j == KC - 1)),
                             r=[fbT[fi][j], W2T[j // 2]], w=[pst[bO]])
                    S.op("dve", lambda e, bO=bO, c=c, hf=hf: e.tensor_tensor(out=X[:, c, hf * 512:(hf + 1) * 512], in0=psf[bO][:, :],
                                                                            in1=X[:, c, hf * 512:(hf + 1) * 512], op=ALU.add),
                         r=[pst[bO], XT[c]], w=[XT[c]])

        def final_norm(T):
            for cc in range(4):
                c = T * 4 + cc
                S.op("act", lambda e, c=c: e.activation(out=junk[:, :], in_=X[:, c, :], func=AF.Square, accum_out=ss3[:, c:c + 1]),
                     r=[XT[c]], w=[junkT, ss3T])
            S.op("dve", lambda e: e.tensor_scalar(out=sd1[:, T * 4:T * 4 + 4], in0=ss3[:, T * 4:T * 4 + 4], scalar1=1.0 / D, scalar2=EPS,
                                                  op0=ALU.mult, op1=ALU.add), r=[ss3T], w=[sd1T])
            S.op("act", lambda e: e.activation(out=sd1[:, T * 4:T * 4 + 4], in_=sd1[:, T * 4:T * 4 + 4], func=AF.Sqrt), r=[sd1T], w=[sd1T])
            S.op("dve", lambda e: e.reciprocal(out=rs1[:, T * 4:T * 4 + 4], in_=sd1[:, T * 4:T * 4 + 4]), r=[sd1T], w=[rs1T])
            for cc in range(4):
                c = T * 4 + cc
                S.op("dve", lambda e, c=c: e.scalar_tensor_tensor(out=X[:, c, :], in0=X[:, c, :], scalar=rs1[:, c:c + 1], in1=gbf[:, :],
                                                                  op0=ALU.mult, op1=ALU.mult),
                     r=[XT[c], rs1T, gbfT], w=[XT[c]])
                out_toks.append(S.dma("sp", lambda e, c=c: e.dma_start(out=out[c * 128:(c + 1) * 128, :], in_=X[:, c, :]), r=[XT[c]]))

        seq = [(q, T) for q in range(4) for T in range(NT)]
        fis = []
        for n, (q, T) in enumerate(seq):
            fis.append(n % 2)
        ffn1(seq[0][0], seq[0][1], fis[0])
        norm2(3, xn_lo=4)
        dump("d_h2", h2T4[:, :, :, :], h2TT)
        for n, (q, T) in enumerate(seq):
            if n + 1 < len(seq):
                ffn1(seq[n + 1][0], seq[n + 1][1], fis[n + 1])
            if n + 1 < len(seq) and seq[n + 1][1] == NT - 1 and seq[n + 1][0] < 2:
                load_next_w()
            ffn2(q, T, fis[n])
            if T == NT - 1 and q < 2:
                load_next_w()
            if q == 3:
                final_norm(T)

        S.wait("sp", out_toks)
        S.run()
    return nc


_W_NAMES = ["norm1_g", "w_in", "b_in", "sgu_ln_g", "sgu_ln_b", "sgu_w", "sgu_b", "w_proj_a", "conv_w", "conv_b",
            "conv_ln_g", "conv_ln_b", "w_proj_b", "b_proj_b", "w_out", "norm2_g", "w_ff1", "w_ff2", "norm_f_g"]


def _prep(inputs):
    d = {}
    for nm in _W_NAMES:
        a = np.asarray(inputs[nm], dtype=np.float32)
        if nm != "norm_f_g":
            a = a[0]
        if nm == "sgu_b":
            a = a.reshape(-1)
        d[nm] = np.ascontiguousarray(a)
    return d


def kernel(**inputs):
    x = np.asarray(inputs["x"], dtype=np.float32)
    B = x.shape[0]
    wd = _prep(inputs)
    nc = build_nc()
    in_maps = []
    for b in range(B):
        m = dict(wd)
        m["x"] = np.ascontiguousarray(x[b])
        in_maps.append(m)
    res = run_bass_kernel_spmd(nc, in_maps, core_ids=list(range(B)))
    return np.stack([np.asarray(r["out"], dtype=np.float32) for r in res.results], axis=0)
```

```python
import contextlib
import numpy as np
import concourse.bass as bass
import concourse.mybir as mybir
from concourse.bass_utils import run_bass_kernel_spmd

F32 = mybir.dt.float32
BF16 = mybir.dt.bfloat16
AF = mybir.ActivationFunctionType
ALU = mybir.AluOpType
DSZ = {F32: 4, BF16: 2}

SEQ = 2048
D = 1024
NT = 4
NCH = 16
KC = 8
CK = 31
EPS = 1e-6


class Tk:
    __slots__ = ("sp", "lo", "hi", "lw", "rd", "ov")

    def __init__(self, sp, lo, hi):
        self.sp, self.lo, self.hi = sp, lo, hi
        self.lw = []
        self.rd = []
        self.ov = None


class Sched:
    ENG = ("pe", "dve", "act", "pool", "sp")

    def __init__(self, nc, n_dma_sems=40):
        self.nc = nc
        self.sem = {e: nc.alloc_semaphore(name="es_" + e) for e in self.ENG}
        self.cnt = {e: 0 for e in self.ENG}
        self.prog = {e: [] for e in self.ENG}
        self.seen = {e: {} for e in self.ENG}
        self.pending = {e: None for e in self.ENG}
        self.dsem = [nc.alloc_semaphore(name="ds_%d" % i) for i in range(n_dma_sems)]
        self.dval = [0] * n_dma_sems
        self.drr = 0
        self.tiles = {}

    def tile(self, sp, lo, hi):
        t = Tk(sp, lo, hi)
        lst = self.tiles.setdefault(sp, [])
        t.ov = [t]
        for o in lst:
            if o.lo < hi and lo < o.hi:
                t.ov.append(o)
                o.ov.append(t)
        lst.append(t)
        return t

    def _need(self, eng, tok):
        if tok[0] == "e":
            _, p, c = tok
            if c > self.cnt[p]:
                ent = self.pending[p]
                assert ent is not None and c == self.cnt[p] + 1, (p, c, self.cnt[p])
                ent["inc"] = True
                self.cnt[p] += 1
                self.pending[p] = None
            key = ("e", p)
            sem = self.sem[p]
            val = c
        else:
            _, i, val = tok
            key = ("d", i)
            sem = self.dsem[i]
        if self.seen[eng].get(key, 0) >= val:
            return None
        self.seen[eng][key] = val
        return (sem, val)

    def _deps(self, eng, r, w):
        toks = []
        for t in r:
            for o in t.ov:
                toks.extend(o.lw)
        same = []
        if eng != "pe":
            same = [tok for tok in toks if tok[0] == "e" and tok[1] == eng]
        for t in w:
            for o in t.ov:
                toks.extend(o.lw)
                toks.extend(o.rd)
        waits = []
        for tok in toks:
            if tok[0] == "e" and tok[1] == eng:
                continue
            wt = self._need(eng, tok)
            if wt is not None:
                waits.append(wt)
        for tok in same:
            wt = self._need(eng, tok)
            if wt is not None:
                waits.append(wt)
        return waits

    def _mark(self, tok, r, w):
        for t in r:
            if not t.rd or t.rd[-1] != tok:
                t.rd.append(tok)
        for t in w:
            for o in t.ov:
                o.lw = [tok]
                o.rd = []

    def op(self, eng, fn, r=(), w=()):
        waits = self._deps(eng, r, w)
        ent = {"fn": fn, "inc": False, "waits": waits, "dma": None}
        self.prog[eng].append(ent)
        self.pending[eng] = ent
        tok = ("e", eng, self.cnt[eng] + 1)
        self._mark(tok, r, w)
        return tok

    def dma(self, eng, fn, r=(), w=()):
        waits = self._deps(eng, r, w)
        i = self.drr
        self.drr = (self.drr + 1) % len(self.dsem)
        if self.dval[i] > 0:
            wt = self._need(eng, ("d", i, self.dval[i]))
            if wt is not None:
                waits.append(wt)
        self.dval[i] += 16
        tok = ("d", i, self.dval[i])
        self.prog[eng].append({"fn": fn, "inc": False, "waits": waits, "dma": i})
        self._mark(tok, r, w)
        return tok

    def wait(self, eng, toks):
        waits = []
        for tok in toks:
            wt = self._need(eng, tok)
            if wt is not None:
                waits.append(wt)
        if waits:
            self.prog[eng].append({"fn": None, "inc": False, "waits": waits, "dma": None})

    def replay(self, eng, e):
        sem = self.sem[eng]
        for ent in self.prog[eng]:
            for (s, v) in ent["waits"]:
                e.wait_ge(s, v)
            if ent["fn"] is None:
                continue
            ins = ent["fn"](e)
            if ent["dma"] is not None:
                ins.then_inc(self.dsem[ent["dma"]], 16)
            elif ent["inc"]:
                ins.then_inc(sem, 1)

    def run(self):
        with self.nc.Block() as block:
            @block.tensor
            def _(e):
                self.replay("pe", e)

            @block.vector
            def _(e):
                self.replay("dve", e)

            @block.scalar
            def _(e):
                self.replay("act", e)

            @block.gpsimd
            def _(e):
                self.replay("pool", e)

            @block.sync
            def _(e):
                self.replay("sp", e)


class Region:
    def __init__(self, S, stack, name, nbytes):
        self.S = S
        self.name = name
        self.nbytes = nbytes
        self.t = stack.enter_context(S.nc.sbuf_tensor(name, [128, nbytes // 4], F32))

    def view(self, lo, dtype, shape):
        n = 1
        for s in shape:
            n *= s
        nb = n * DSZ[dtype]
        assert lo % 4 == 0 and nb % 4 == 0 and lo + nb <= self.nbytes, (self.name, lo, nb)
        ap = self.t[:, lo // 4:(lo + nb) // 4]
        if dtype != F32:
            ap = ap.bitcast(dtype)
        if len(shape) == 2:
            ap = ap.rearrange("p (a b) -> p a b", a=shape[0])
        elif len(shape) == 3:
            ap = ap.rearrange("p (a b c) -> p a b c", a=shape[0], b=shape[1])
        return ap

    def tk(self, lo, nb):
        assert lo + nb <= self.nbytes
        return self.S.tile("sb:" + self.name, lo, lo + nb)


def build_nc(dbg=False):
    nc = bass.Bass("TRN2", target_bir_lowering=False)

    def din(name, shape):
        return nc.dram_tensor(name, shape, F32, kind="ExternalInput").ap()

    x = din("x", [SEQ, D])
    norm1_g = din("norm1_g", [D])
    w_in = din("w_in", [D, 6 * D])
    b_in = din("b_in", [6 * D])
    sgu_ln_g = din("sgu_ln_g", [D])
    sgu_ln_b = din("sgu_ln_b", [D])
    sgu_w = din("sgu_w", [8, 128, 128])
    sgu_b = din("sgu_b", [D])
    w_proj_a = din("w_proj_a", [D, D])
    conv_w = din("conv_w", [CK, D])
    conv_b = din("conv_b", [D])
    conv_ln_g = din("conv_ln_g", [D])
    conv_ln_b = din("conv_ln_b", [D])
    w_proj_b = din("w_proj_b", [D, D])
    b_proj_b = din("b_proj_b", [D])
    w_out = din("w_out", [D, D])
    norm2_g = din("norm2_g", [D])
    w_ff1 = din("w_ff1", [D, 4 * D])
    w_ff2 = din("w_ff2", [4 * D, D])
    norm_f_g = din("norm_f_g", [D])
    out = nc.dram_tensor("out", [SEQ, D], F32, kind="ExternalOutput").ap()
    dbg_t = {}
    if dbg:
        for nm, shp, dt_ in (("d_hT", [128, NT * KC * 512], BF16), ("d_cs", [128, KC * SEQ], BF16),
                             ("d_m2", [128, NT * KC * 512], BF16), ("d_mg", [128, NT * KC * 512], BF16),
                             ("d_x1", [128, NCH * D], F32), ("d_h2", [128, NT * KC * 512], BF16),
                             ("d_vec", [128, 128], F32), ("d_cw", [128, KC * CK], F32),
                             ("d_bg", [128, D], F32), ("d_wst", [128, D], BF16),
                             ("d_vh", [128, 4 * D], BF16), ("d_su", [128, KC * 512], BF16)):
            dbg_t[nm] = nc.dram_tensor(nm, shp, dt_, kind="ExternalOutput").ap()

    S = Sched(nc)
    with contextlib.ExitStack() as st:
        RING = Region(S, st, "RING", 65536)
        P1 = Region(S, st, "P1", 49152)
        P23 = Region(S, st, "P23", 65536)
        CST = Region(S, st, "CST", 28672)
        psf = [st.enter_context(nc.psum_tensor("ps%d" % i, [128, 512], F32)) for i in range(8)]
        psb = [p[:, :].bitcast(BF16) for p in psf]
        pst = [S.tile("ps", i, i + 1) for i in range(8)]
        bank_rr = [0]

        def newbank():
            i = bank_rr[0]
            bank_rr[0] = (i + 1) % 8
            return i

        out_toks = []

        def dump(nm, ap, tks):
            if dbg:
                out_toks.append(S.dma("sp", lambda e: e.dma_start(out=dbg_t[nm][:, :], in_=ap), r=tks))

        coff = [0]

        def calloc(dtype, shape):
            n = int(np.prod(shape)) * DSZ[dtype]
            n = (n + 31) // 32 * 32
            v = CST.view(coff[0], dtype, shape)
            t = CST.tk(coff[0], n)
            coff[0] += n
            return v, t

        ident, identT = calloc(BF16, [128])
        identf, identfT = calloc(F32, [128])
        vec, vecT = calloc(F32, [128])
        convw, convwT = calloc(F32, [KC, 32])
        gb1, gb1T = calloc(F32, [D])
        gb2, gb2T = calloc(F32, [D])
        gbf, gbfT = calloc(F32, [D])
        Bg, BgT = calloc(F32, [D])
        WsT, WsTT = calloc(BF16, [8, 128])
        bvrow, bvrowT = calloc(BF16, [D])
        onesrow, onesrowT = calloc(BF16, [128])
        onesm, onesmT = calloc(BF16, [128])
        epsc, epscT = calloc(F32, [8])
        ss1, ss1T = calloc(F32, [NCH])
        ss2, ss2T = calloc(F32, [NCH])
        ss3, ss3T = calloc(F32, [NCH])
        rs1, rs1T = calloc(F32, [NCH])
        sd1, sd1T = calloc(F32, [NCH])
        bst, bstT = calloc(F32, [4, 12])
        mv, mvT = calloc(F32, [4, 2])
        rv, rvT = calloc(F32, [4])
        sdv, sdvT = calloc(F32, [4])
        junk, junkT = calloc(F32, [D])
        V_BU, V_BV, V_BVAL, V_BGATE, V_BG0, V_BG1, V_N1, V_N2, V_SLG, V_CB, V_CLG, V_CLB, V_BPB = [8 * i for i in range(13)]

        S.op("pool", lambda e: e.memset(identf[:, :], 0.0), w=[identfT])
        S.op("pool", lambda e: e.affine_select(out=identf[:, :], in_=identf[:, :], pattern=[[-1, 128]],
                                               compare_op=ALU.not_equal, fill=1.0, base=0, channel_multiplier=1),
             r=[identfT], w=[identfT])
        S.op("dve", lambda e: e.tensor_copy(out=ident[:, :], in_=identf[:, :]), r=[identfT], w=[identT])
        S.op("pool", lambda e: e.memset(onesrow[0:1, :], 1.0), w=[onesrowT])
        S.op("pool", lambda e: e.memset(onesm[:, :], 1.0 / D), w=[onesmT])
        S.op("pool", lambda e: e.memset(epsc[:, :], EPS), w=[epscT])
        for s_, sT_ in ((ss1, ss1T), (ss2, ss2T), (ss3, ss3T)):
            S.op("pool", lambda e, s_=s_: e.memset(s_[:, :], 0.0), w=[sT_])
        slotv = [RING.view(s * 16384, BF16, [8, 1024]) for s in range(4)]
        slotT = [[RING.tk(s * 16384 + kp * 4096, 4096) for kp in range(4)] for s in range(4)]
        w_in_v = w_in.rearrange("(k p) e -> p k e", p=128)
        wsrc = {
            "u": w_in_v[:, :, 0 * D:1 * D], "v": w_in_v[:, :, 1 * D:2 * D], "val": w_in_v[:, :, 2 * D:3 * D],
            "gate": w_in_v[:, :, 3 * D:4 * D], "g0": w_in_v[:, :, 4 * D:5 * D], "g1": w_in_v[:, :, 5 * D:6 * D],
            "pa": w_proj_a.rearrange("(k p) e -> p k e", p=128), "pb": w_proj_b.rearrange("(k p) e -> p k e", p=128),
            "out": w_out.rearrange("(k p) e -> p k e", p=128),
        }
        w1v = w_ff1.rearrange("(k p) e -> p k e", p=128)
        w2v = w_ff2.rearrange("(j p) e -> p j e", p=128)
        for q in range(4):
            wsrc["f1_%d" % q] = w1v[:, :, q * D:(q + 1) * D]
            wsrc["f2_%d" % q] = w2v[:, q * 8:(q + 1) * 8, :]
        worder = ["val", "gate", "pb", "g1", "v", "u", "pa", "g0", "out",
                  "f1_0", "f2_0", "f1_1", "f2_1", "f1_2", "f2_2", "f1_3", "f2_3"]
        wslot = {nm: i % 4 for i, nm in enumerate(worder)}
        wnext = [0]

        def load_next_w():
            if wnext[0] >= len(worder):
                return
            nm = worder[wnext[0]]
            wnext[0] += 1
            s = wslot[nm]
            src = wsrc[nm]
            for kp in range(4):
                S.dma("pool", lambda e, s=s, kp=kp, src=src: e.dma_start(out=slotv[s][:, 2 * kp:2 * kp + 2, :], in_=src[:, 2 * kp:2 * kp + 2, :]),
                      w=[slotT[s][kp]])

        def W(nm):
            s = wslot[nm]
            return slotv[s], slotT[s]

        for _ in range(2):
            load_next_w()

        so = 32768
        VR = P23.view(so, F32, [128]); VRT = P23.tk(so, 512); so += 512
        CW = P23.view(so, F32, [D]); CWT = P23.tk(so, 4096); so += 4096
        Wtf = P23.view(so, F32, [8, 128]); WtfT = P23.tk(so, 4096); so += 4096
        Wtb = P23.view(so, BF16, [8, 128]); WtbT = P23.tk(so, 2048); so += 2048
        betaf = P23.view(so, F32, [D]); betafT = P23.tk(so, 4096); so += 4096
        betab = P23.view(so, BF16, [D]); betabT = P23.tk(so, 2048); so += 2048

        hT4 = P23.view(0, BF16, [NT, KC, 512])
        hTT = [P23.tk(T * 8192, 8192) for T in range(NT)]
        xt = [P1.view(i * 16384, F32, [4, D]) for i in range(2)]
        xtT = [[P1.tk(i * 16384 + cc * 4096, 4096) for cc in range(4)] for i in range(2)]
        xv = x.rearrange("(c p) d -> p c d", p=128)

        def load_x_tile(T):
            i = T % 2
            for cc in range(4):
                c = T * 4 + cc
                S.dma("sp", lambda e, i=i, cc=cc, c=c: e.dma_start(out=xt[i][:, cc, :], in_=xv[:, c, :]), w=[xtT[i][cc]])

        S.op("dve", lambda e: e.memset(VR[:, :], 0.0), w=[VRT])
        load_x_tile(0)
        S.dma("sp", lambda e: e.dma_start(out=gb1[:, :], in_=norm1_g.partition_broadcast(128)), w=[gb1T])
        S.dma("sp", lambda e: e.dma_start(out=VR[0:48, :], in_=b_in.rearrange("(r p) -> r p", p=128)), w=[VRT])
        for i, v_ in enumerate((norm1_g, norm2_g, sgu_ln_g, conv_b, conv_ln_g, conv_ln_b, b_proj_b)):
            S.dma("sp", lambda e, i=i, v_=v_: e.dma_start(out=VR[48 + 8 * i:56 + 8 * i, :], in_=v_.rearrange("(r p) -> r p", p=128)), w=[VRT])
        load_x_tile(1)
        S.dma("sp", lambda e: e.dma_start(out=CW[0:CK, :], in_=conv_w[:, :]), w=[CWT])
        S.dma("sp", lambda e: e.dma_start(out=Wtf[:, :, :], in_=sgu_w.rearrange("g t s -> t g s")), w=[WtfT])
        S.dma("sp", lambda e: e.dma_start(out=Bg[:, :], in_=sgu_b.partition_broadcast(128)), w=[BgT])
        S.dma("sp", lambda e: e.dma_start(out=betaf[:, :], in_=sgu_ln_b.partition_broadcast(128)), w=[betafT])
        S.dma("sp", lambda e: e.dma_start(out=gb2[:, :], in_=norm2_g.partition_broadcast(128)), w=[gb2T])
        S.dma("sp", lambda e: e.dma_start(out=gbf[:, :], in_=norm_f_g.partition_broadcast(128)), w=[gbfT])
        S.dma("pool", lambda e: e.dma_start(out=bvrow[0:1, :], in_=b_in[D:2 * D].rearrange("(o n) -> o n", o=1)), w=[bvrowT])


        xn = [P1.view(32768 + i * 2048, BF16, [D]) for i in range(8)]
        xnT = [P1.tk(32768 + i * 2048, 2048) for i in range(8)]
        xn_rr = [0]

        def norm_to_hT(T, src_ap_fn, src_tk_fn, gb, gbT, dst4, dstT, ss, ssT, rs, rsT, sd, sdT, xn_lo=0):
            for cc in range(4):
                c = T * 4 + cc
                S.op("act", lambda e, cc=cc, c=c: e.activation(out=junk[:, :], in_=src_ap_fn(cc), func=AF.Square, accum_out=ss[:, c:c + 1]),
                     r=[src_tk_fn(cc)], w=[junkT, ssT])
            S.op("dve", lambda e: e.tensor_scalar(out=sd[:, T * 4:T * 4 + 4], in0=ss[:, T * 4:T * 4 + 4], scalar1=1.0 / D, scalar2=EPS,
                                                  op0=ALU.mult, op1=ALU.add), r=[ssT], w=[sdT])
            S.op("act", lambda e: e.activation(out=sd[:, T * 4:T * 4 + 4], in_=sd[:, T * 4:T * 4 + 4], func=AF.Sqrt), r=[sdT], w=[sdT])
            S.op("dve", lambda e: e.reciprocal(out=rs[:, T * 4:T * 4 + 4], in_=sd[:, T * 4:T * 4 + 4]), r=[sdT], w=[rsT])
            for cc in range(4):
                c = T * 4 + cc
                xi = xn_lo + xn_rr[0] % (8 - xn_lo)
                xn_rr[0] += 1
                S.op("dve", lambda e, cc=cc, c=c, xi=xi: e.scalar_tensor_tensor(out=xn[xi][:, :], in0=src_ap_fn(cc), scalar=rs[:, c:c + 1],
                                                                               in1=gb[:, :], op0=ALU.mult, op1=ALU.mult),
                     r=[src_tk_fn(cc), rsT, gbT], w=[xnT[xi]])
                bk = newbank()
                for k in range(KC):
                    S.op("pe", lambda e, bk=bk, k=k, xi=xi: e.transpose(out=psb[bk][:, k * 128:(k + 1) * 128], in_=xn[xi][:, k * 128:(k + 1) * 128],
                                                                        identity=ident[:, :]), r=[xnT[xi], identT], w=[pst[bk]])
                eng = "act" if (cc % 2 == 0) else "dve"
                if eng == "act":
                    S.op("act", lambda e, bk=bk, cc=cc: e.activation(out=dst4[:, T, :, cc * 128:(cc + 1) * 128],
                                                                     in_=psb[bk][:, :].rearrange("p (k t) -> p k t", k=KC), func=AF.Copy),
                         r=[pst[bk]], w=[dstT[T]])
                else:
                    S.op("dve", lambda e, bk=bk, cc=cc: e.tensor_copy(out=dst4[:, T, :, cc * 128:(cc + 1) * 128],
                                                                      in_=psb[bk][:, :].rearrange("p (k t) -> p k t", k=KC)),
                         r=[pst[bk]], w=[dstT[T]])

        for T in range(NT):
            i = T % 2
            norm_to_hT(T, lambda cc, i=i: xt[i][:, cc, :], lambda cc, i=i: xtT[i][cc], gb1, gb1T, hT4, hTT, ss1, ss1T, rs1, rs1T, sd1, sd1T)
            if T + 2 < NT:
                load_x_tile(T + 2)
        bk = newbank()
        S.op("pe", lambda e, bk=bk: e.transpose(out=psf[bk][:, 0:104], in_=VR[0:104, :], identity=identf[0:104, 0:104]),
             r=[VRT, identfT], w=[pst[bk]])
        S.op("dve", lambda e, bk=bk: e.tensor_copy(out=vec[:, 0:104], in_=psf[bk][:, 0:104]), r=[pst[bk]], w=[vecT])
        bk = newbank()
        for j in range(KC):
            S.op("pe", lambda e, bk=bk, j=j: e.transpose(out=psf[bk][:, j * 32:j * 32 + CK], in_=CW[0:CK, j * 128:(j + 1) * 128],
                                                         identity=identf[0:CK, 0:CK]), r=[CWT, identfT], w=[pst[bk]])
        S.op("dve", lambda e, bk=bk: e.tensor_copy(out=convw[:, :, 0:CK],
                                                   in_=psf[bk][:, 0:256].rearrange("p (j k) -> p j k", k=32)[:, :, 0:CK]),
             r=[pst[bk]], w=[convwT])
        dump("d_hT", hT4[:, :, :, :], hTT)
        dump("d_vec", vec[:, :], [vecT])
        dump("d_cw", convw[:, :, 0:CK], [convwT])
        S.op("pool", lambda e: e.affine_select(out=Wtf[:, :, :], in_=Wtf[:, :, :], pattern=[[0, 8], [-1, 128]],
                                               compare_op=ALU.is_ge, fill=0.0, base=0, channel_multiplier=1),
             r=[WtfT], w=[WtfT])
        S.op("dve", lambda e: e.tensor_copy(out=Wtb[:, :, :], in_=Wtf[:, :, :]), r=[WtfT], w=[WtbT])
        bk = newbank()
        for g in range(8):
            S.op("pe", lambda e, bk=bk, g=g: e.transpose(out=psb[bk][:, g * 128:(g + 1) * 128], in_=Wtb[:, g, :], identity=ident[:, :]),
                 r=[WtbT, identT], w=[pst[bk]])
        S.op("dve", lambda e, bk=bk: e.tensor_copy(out=WsT[:, :, :], in_=psb[bk][:, :].rearrange("p (g t) -> p g t", g=8)),
             r=[pst[bk]], w=[WsTT])
        S.op("dve", lambda e: e.tensor_copy(out=betab[:, :], in_=betaf[:, :]), r=[betafT], w=[betabT])
        for hh in range(2):
            bk = newbank()
            for g4 in range(4):
                g = hh * 4 + g4
                S.op("pe", lambda e, bk=bk, g=g, g4=g4: e.matmul(psf[bk][:, g4 * 128:(g4 + 1) * 128], lhsT=betab[:, g * 128:(g + 1) * 128],
                                                                rhs=WsT[:, g, :], start=True, stop=True),
                     r=[betabT, WsTT], w=[pst[bk]])
            S.op("dve", lambda e, bk=bk, hh=hh: e.tensor_tensor(out=Bg[:, hh * 512:(hh + 1) * 512], in0=psf[bk][:, :],
                                                               in1=Bg[:, hh * 512:(hh + 1) * 512], op=ALU.add),
                 r=[pst[bk], BgT], w=[BgT])
        dump("d_bg", Bg[:, :], [BgT])
        dump("d_wst", WsT[:, :, :], [WsTT])

        CV = P23.view(32768, BF16, [KC, SEQ])
        CVT = [[P23.tk(32768 + j * 4096 + T * 1024, 1024) for T in range(NT)] for j in range(KC)]
        APAD = 2080
        abuf = [P1.view(i * 4160, BF16, [APAD]) for i in range(2)]
        abufT = [[P1.tk(i * 4160, 60)] + [P1.tk(i * 4160 + 60 + T * 1024, 1024) for T in range(NT)] for i in range(2)]
        dg = [P1.view(8320 + i * 7936, BF16, [CK, 128]) for i in range(2)]
        dgT = [P1.tk(8320 + i * 7936, 7936) for i in range(2)]
        sig = [P1.view(24192 + i * 1024, BF16, [512]) for i in range(2)]
        sigT = [P1.tk(24192 + i * 1024, 1024) for i in range(2)]
        M2 = P1.view(0, BF16, [NT, KC, 512])
        M2T = [P1.tk(T * 8192, 8192) for T in range(NT)]
        sqt = [P1.view(26240 + i * 1024, BF16, [512]) for i in range(8)]
        sqtT = [P1.tk(26240 + i * 1024, 1024) for i in range(8)]
        NYB = 3
        yb = [P1.view(34432 + i * 2048, F32, [512]) for i in range(NYB)]
        ybT = [P1.tk(34432 + i * 2048, 2048) for i in range(NYB)]
        stt = P1.view(40576, F32, [512]); sttT = P1.tk(40576, 2048)
        rstdb = P1.view(42624, F32, [512]); rstdbT = P1.tk(42624, 2048)
        nmrb = P1.view(44672, F32, [512]); nmrbT = P1.tk(44672, 2048)
        sg = [P1.view(46720 + i * 1024, BF16, [512]) for i in range(2)]
        sgT = [P1.tk(46720 + i * 1024, 1024) for i in range(2)]
        Wpb, WpbT = W("pb")
        Wg1, Wg1T = W("g1")

        def btail_sq(T):
            for j in range(KC):
                S.op("dve", lambda e, j=j: e.tensor_tensor(out=sqt[j][:, :], in0=CV[:, j, T * 512:(T + 1) * 512], in1=CV[:, j, T * 512:(T + 1) * 512], op=ALU.mult),
                     r=[CVT[j][T]], w=[sqtT[j]])

        def btail_stats(T):
            bM = newbank()
            bQ = newbank()
            for j in range(KC):
                qi = j
                S.op("pe", lambda e, j=j, bM=bM: e.matmul(psf[bM][:, :], lhsT=onesm[:, :], rhs=CV[:, j, T * 512:(T + 1) * 512], start=(j == 0), stop=(j == KC - 1)),
                     r=[onesmT, CVT[j][T]], w=[pst[bM]])
                S.op("pe", lambda e, j=j, bQ=bQ, qi=qi: e.matmul(psf[bQ][:, :], lhsT=onesm[:, :], rhs=sqt[qi][:, :], start=(j == 0), stop=(j == KC - 1)),
                     r=[onesmT, sqtT[qi]], w=[pst[bQ]])
            S.op("act", lambda e: e.activation(out=stt[:, :], in_=psf[bM][:, :], func=AF.Square), r=[pst[bM]], w=[sttT])
            S.op("dve", lambda e: e.tensor_tensor(out=stt[:, :], in0=psf[bQ][:, :], in1=stt[:, :], op=ALU.subtract), r=[pst[bQ], sttT], w=[sttT])
            S.op("act", lambda e: e.activation(out=stt[:, :], in_=stt[:, :], func=AF.Sqrt, bias=epsc[:, 0:1], scale=1.0), r=[sttT, epscT], w=[sttT])
            S.op("dve", lambda e: e.reciprocal(out=rstdb[:, :], in_=stt[:, :]), r=[sttT], w=[rstdbT])
            S.op("dve", lambda e: e.scalar_tensor_tensor(out=nmrb[:, :], in0=psf[bM][:, :], scalar=-1.0, in1=rstdb[:, :], op0=ALU.mult, op1=ALU.mult),
                 r=[pst[bM], rstdbT], w=[nmrbT])

        def btail_norm(T):
            for j in range(KC):
                yi = j % NYB
                S.op("dve", lambda e, j=j, yi=yi: e.tensor_tensor(out=yb[yi][:, :], in0=CV[:, j, T * 512:(T + 1) * 512], in1=rstdb[:, :], op=ALU.mult),
                     r=[CVT[j][T], rstdbT], w=[ybT[yi]])
                S.op("dve", lambda e, j=j, yi=yi: e.tensor_tensor(out=yb[yi][:, :], in0=yb[yi][:, :], in1=nmrb[:, :], op=ALU.add),
                     r=[ybT[yi], nmrbT], w=[ybT[yi]])
                S.op("act", lambda e, j=j, yi=yi: e.activation(out=CV[:, j, T * 512:(T + 1) * 512], in_=yb[yi][:, :], func=AF.Silu,
                                                               bias=vec[:, V_CLB + j:V_CLB + j + 1], scale=vec[:, V_CLG + j:V_CLG + j + 1]),
                     r=[ybT[yi], vecT], w=[CVT[j][T]])

        sg_rr = [0]

        def btail_proj(T):
            for m in range(KC):
                bG = newbank()
                for k in range(KC):
                    S.op("pe", lambda e, bG=bG, k=k, m=m: e.matmul(psf[bG][:, :], lhsT=Wg1[:, k, m * 128:(m + 1) * 128], rhs=hT4[:, T, k, :],
                                                                  start=(k == 0), stop=(k == KC - 1)),
                         r=[Wg1T[k // 2], hTT[T]], w=[pst[bG]])
                bY = newbank()
                for j in range(KC):
                    S.op("pe", lambda e, bY=bY, j=j, m=m: e.matmul(psf[bY][:, :], lhsT=Wpb[:, j, m * 128:(m + 1) * 128], rhs=CV[:, j, T * 512:(T + 1) * 512],
                                                                  start=(j == 0), stop=(j == KC - 1)),
                         r=[WpbT[j // 2], CVT[j][T]], w=[pst[bY]])
                si = sg_rr[0]
                sg_rr[0] = (si + 1) % 2
                S.op("act", lambda e, bG=bG, si=si, m=m: e.activation(out=sg[si][:, :], in_=psf[bG][:, :], func=AF.Sigmoid,
                                                                     bias=vec[:, V_BG1 + m:V_BG1 + m + 1], scale=1.0),
                     r=[pst[bG], vecT], w=[sgT[si]])
                S.op("dve", lambda e, bY=bY, si=si, m=m: e.scalar_tensor_tensor(out=M2[:, T, m, :], in0=psf[bY][:, :], scalar=vec[:, V_BPB + m:V_BPB + m + 1],
                                                                               in1=sg[si][:, :], op0=ALU.add, op1=ALU.mult),
                     r=[pst[bY], vecT, sgT[si]], w=[M2T[T]])

        Wval, WvalT = W("val")
        Wgate, WgateT = W("gate")
        for i in range(2):
            S.op("pool", lambda e, i=i: e.memset(abuf[i][:, 0:30], 0.0), w=[abufT[i][0]])
        sig_rr = [0]
        for j in range(KC):
            ab = abuf[j % 2]
            abT = abufT[j % 2]
            dgi = dg[j % 2]
            dgiT = dgT[j % 2]
            S.op("dve", lambda e, dgi=dgi, j=j: e.tensor_tensor(out=dgi[:, :, :], in0=ident[:, None, :].to_broadcast([128, CK, 128]),
                                                                in1=convw[:, j, 0:CK, None].to_broadcast([128, CK, 128]), op=ALU.mult),
                 r=[identT, convwT], w=[dgiT])
            pend = None
            for T in range(NT + 1):
                if T < NT:
                    bA = newbank()
                    for k in range(KC):
                        S.op("pe", lambda e, bA=bA, k=k, j=j, T=T: e.matmul(psf[bA][:, :], lhsT=Wval[:, k, j * 128:(j + 1) * 128], rhs=hT4[:, T, k, :],
                                                                           start=(k == 0), stop=(k == KC - 1)),
                             r=[WvalT[k // 2], hTT[T]], w=[pst[bA]])
                    bB = newbank()
                    for k in range(KC):
                        S.op("pe", lambda e, bB=bB, k=k, j=j, T=T: e.matmul(psf[bB][:, :], lhsT=Wgate[:, k, j * 128:(j + 1) * 128], rhs=hT4[:, T, k, :],
                                                                           start=(k == 0), stop=(k == KC - 1)),
                             r=[WgateT[k // 2], hTT[T]], w=[pst[bB]])
                    si = sig_rr[0]
                    sig_rr[0] = (si + 1) % 2
                    S.op("act", lambda e, bB=bB, si=si, j=j: e.activation(out=sig[si][:, :], in_=psf[bB][:, :], func=AF.Sigmoid,
                                                                         bias=vec[:, V_BGATE + j:V_BGATE + j + 1], scale=1.0),
                         r=[pst[bB], vecT], w=[sigT[si]])
                    S.op("dve", lambda e, bA=bA, si=si, j=j, T=T, ab=ab: e.scalar_tensor_tensor(out=ab[:, 30 + T * 512:30 + (T + 1) * 512], in0=psf[bA][:, :],
                                                                                                  scalar=vec[:, V_BVAL + j:V_BVAL + j + 1], in1=sig[si][:, :],
                                                                                                  op0=ALU.add, op1=ALU.mult),
                         r=[pst[bA], vecT, sigT[si]], w=[abT[T + 1]])
                if pend is not None:
                    Tp = pend
                    bC = newbank()
                    for k in range(CK):
                        S.op("pe", lambda e, bC=bC, k=k, Tp=Tp, ab=ab, dgi=dgi: e.matmul(psf[bC][:, :], lhsT=dgi[:, k, :], rhs=ab[:, Tp * 512 + k:Tp * 512 + k + 512],
                                                                                        start=(k == 0), stop=(k == CK - 1)),
                             r=[dgiT, abT[Tp], abT[Tp + 1]], w=[pst[bC]])
                    S.op("act", lambda e, bC=bC, Tp=Tp, j=j: e.activation(out=CV[:, j, Tp * 512:(Tp + 1) * 512], in_=psf[bC][:, :], func=AF.Identity,
                                                                         bias=vec[:, V_CB + j:V_CB + j + 1], scale=1.0),
                         r=[pst[bC], vecT], w=[CVT[j][Tp]])
                    if j == KC - 1:
                        if Tp >= 1:
                            btail_stats(Tp - 1)
                            btail_norm(Tp - 1)
                        btail_sq(Tp)
                pend = T if T < NT else None
            if j == 1:
                load_next_w()
                load_next_w()
        load_next_w()
        load_next_w()

        btail_proj(0)
        btail_stats(NT - 1)
        btail_norm(NT - 1)
        for T in range(1, NT):
            btail_proj(T)
        dump("d_cs", CV[:, :, :], [t for row in CVT for t in row])
        dump("d_m2", M2[:, :, :, :], M2T)
        load_next_w()
        load_next_w()

        vh = [P23.view(32768 + i * 8192, BF16, [4, D]) for i in range(2)]
        vhT = [[P23.tk(32768 + i * 8192 + cc * 2048, 2048) for cc in range(4)] for i in range(2)]
        ub = [P23.view(49152 + i * 8192, BF16, [KC, 512]) for i in range(2)]
        ubT = [[P23.tk(49152 + i * 8192 + m * 1024, 1024) for m in range(KC)] for i in range(2)]
        stmp = [P1.view(32768 + i * 1024, BF16, [512]) for i in range(2)]
        stmpT = [P1.tk(32768 + i * 1024, 1024) for i in range(2)]
        sg0 = [P1.view(34816 + i * 1024, BF16, [512]) for i in range(2)]
        sg0T = [P1.tk(34816 + i * 1024, 1024) for i in range(2)]
        t1 = [P1.view(36864 + i * 1024, BF16, [512]) for i in range(2)]
        t1T = [P1.tk(36864 + i * 1024, 1024) for i in range(2)]
        Wv, WvT = W("v")
        Wu, WuT = W("u")
        Wpa, WpaT = W("pa")
        Wg0, Wg0T = W("g0")
        rr2 = [0, 0, 0]

        def stageA_v(T):
            i = T % 2
            for cc in range(4):
                bV = [newbank(), newbank()]
                for hf in range(2):
                    for k in range(KC):
                        S.op("pe", lambda e, b=bV[hf], k=k, cc=cc, hf=hf: e.matmul(psf[b][:, :], lhsT=hT4[:, T, k, cc * 128:(cc + 1) * 128],
                                                                                  rhs=Wv[:, k, hf * 512:(hf + 1) * 512], start=(k == 0), stop=False),
                             r=[hTT[T], WvT[k // 2]], w=[pst[bV[hf]]])
                    S.op("pe", lambda e, b=bV[hf], hf=hf: e.matmul(psf[b][:, :], lhsT=onesrow[0:1, :], rhs=bvrow[0:1, hf * 512:(hf + 1) * 512], start=False, stop=True),
                         r=[onesrowT, bvrowT], w=[pst[bV[hf]]])
                    S.op("act", lambda e, b=bV[hf], hf=hf, cc=cc, i=i: e.activation(out=vh[i][:, cc, hf * 512:(hf + 1) * 512], in_=psf[b][:, :], func=AF.Gelu_apprx_tanh),
                         r=[pst[bV[hf]]], w=[vhT[i][cc]])
                    S.op("dve", lambda e, hf=hf, cc=cc, i=i: e.bn_stats(out=bst[:, cc, hf * 6:(hf + 1) * 6], in_=vh[i][:, cc, hf * 512:(hf + 1) * 512]),
                         r=[vhT[i][cc]], w=[bstT])
                S.op("dve", lambda e, cc=cc: e.bn_aggr(out=mv[:, cc, :], in_=bst[:, cc, :]), r=[bstT], w=[mvT])
            S.op("dve", lambda e: e.tensor_scalar(out=sdv[:, :], in0=mv[:, :, 1], scalar1=EPS, scalar2=None, op0=ALU.add), r=[mvT], w=[sdvT])
            S.op("act", lambda e: e.activation(out=sdv[:, :], in_=sdv[:, :], func=AF.Sqrt), r=[sdvT], w=[sdvT])
            S.op("dve", lambda e: e.reciprocal(out=rv[:, :], in_=sdv[:, :]), r=[sdvT], w=[rvT])
            for cc in range(4):
                S.op("dve", lambda e, cc=cc, i=i: e.tensor_scalar(out=vh[i][:, cc, :], in0=vh[i][:, cc, :], scalar1=mv[:, cc, 0:1], scalar2=rv[:, cc:cc + 1],
                                                                  op0=ALU.subtract, op1=ALU.mult),
                     r=[vhT[i][cc], mvT, rvT], w=[vhT[i][cc]])

        def stageA_u(T):
            i = T % 2
            for m in range(KC):
                bU = newbank()
                for k in range(KC):
                    S.op("pe", lambda e, bU=bU, k=k, m=m: e.matmul(psf[bU][:, :], lhsT=Wu[:, k, m * 128:(m + 1) * 128], rhs=hT4[:, T, k, :],
                                                                  start=(k == 0), stop=(k == KC - 1)),
                         r=[WuT[k // 2], hTT[T]], w=[pst[bU]])
                S.op("act", lambda e, bU=bU, m=m, i=i: e.activation(out=ub[i][:, m, :], in_=psf[bU][:, :], func=AF.Gelu_apprx_tanh,
                                                                   bias=vec[:, V_BU + m:V_BU + m + 1], scale=1.0),
                     r=[pst[bU], vecT], w=[ubT[i][m]])

        def stageA_sgu(T):
            i = T % 2
            for g in range(8):
                bS = newbank()
                for cc in range(4):
                    S.op("pe", lambda e, bS=bS, g=g, cc=cc, i=i: e.matmul(psf[bS][:, cc * 128:(cc + 1) * 128], lhsT=vh[i][:, cc, g * 128:(g + 1) * 128],
                                                                         rhs=WsT[:, g, :], start=True, stop=True),
                         r=[vhT[i][cc], WsTT], w=[pst[bS]])
                si = rr2[0]
                rr2[0] = (si + 1) % 2
                S.op("dve", lambda e, bS=bS, g=g, si=si: e.scalar_tensor_tensor(out=stmp[si][:, :].rearrange("p (c t) -> p c t", c=4),
                                                                               in0=psf[bS][:, :].rearrange("p (c t) -> p c t", c=4),
                                                                               scalar=vec[:, V_SLG + g:V_SLG + g + 1],
                                                                               in1=Bg[:, None, g * 128:(g + 1) * 128].to_broadcast([128, 4, 128]),
                                                                               op0=ALU.mult, op1=ALU.add),
                     r=[pst[bS], vecT, BgT], w=[stmpT[si]])
                S.op("dve", lambda e, g=g, si=si, i=i: e.tensor_tensor(out=ub[i][:, g, :], in0=ub[i][:, g, :], in1=stmp[si][:, :], op=ALU.mult),
                     r=[ubT[i][g], stmpT[si]], w=[ubT[i][g]])

        def stageA_proj(T):
            i = T % 2
            for m in range(KC):
                bG = newbank()
                for k in range(KC):
                    S.op("pe", lambda e, bG=bG, k=k, m=m: e.matmul(psf[bG][:, :], lhsT=Wg0[:, k, m * 128:(m + 1) * 128], rhs=hT4[:, T, k, :],
                                                                  start=(k == 0), stop=(k == KC - 1)),
                         r=[Wg0T[k // 2], hTT[T]], w=[pst[bG]])
                bY = newbank()
                for j in range(KC):
                    S.op("pe", lambda e, bY=bY, j=j, m=m, i=i: e.matmul(psf[bY][:, :], lhsT=Wpa[:, j, m * 128:(m + 1) * 128], rhs=ub[i][:, j, :],
                                                                       start=(j == 0), stop=(j == KC - 1)),
                         r=[WpaT[j // 2], ubT[i][j]], w=[pst[bY]])
                si = rr2[1]
                rr2[1] = (si + 1) % 2
                S.op("act", lambda e, bG=bG, si=si, m=m: e.activation(out=sg0[si][:, :], in_=psf[bG][:, :], func=AF.Sigmoid,
                                                                     bias=vec[:, V_BG0 + m:V_BG0 + m + 1], scale=1.0),
                     r=[pst[bG], vecT], w=[sg0T[si]])
                S.op("dve", lambda e, bY=bY, si=si: e.tensor_tensor(out=t1[si][:, :], in0=psf[bY][:, :], in1=sg0[si][:, :], op=ALU.mult),
                     r=[pst[bY], sg0T[si]], w=[t1T[si]])
                S.op("dve", lambda e, si=si, m=m: e.tensor_tensor(out=M2[:, T, m, :], in0=M2[:, T, m, :], in1=t1[si][:, :], op=ALU.add),
                     r=[M2T[T], t1T[si]], w=[M2T[T]])

        stageA_v(0)
        stageA_u(0)
        for T in range(NT):
            stageA_sgu(T)
            if T == 0 and dbg:
                dump("d_vh", vh[0][:, :, :], vhT[0])
            if T + 1 < NT:
                stageA_v(T + 1)
                if T + 1 == NT - 1:
                    load_next_w()
                stageA_u(T + 1)
                if T + 1 == NT - 1:
                    load_next_w()
            stageA_proj(T)
            if T == 0 and dbg:
                dump("d_su", ub[0][:, :, :], ubT[0])
        dump("d_mg", M2[:, :, :, :], M2T)
        load_next_w()
        load_next_w()

        X = P23.view(0, F32, [NCH, D])
        XT = [P23.tk(c * 4096, 4096) for c in range(NCH)]
        h2T4 = P1.view(0, BF16, [NT, KC, 512])
        h2TT = M2T
        Wout, WoutT = W("out")
        for c in range(NCH):
            S.dma("sp", lambda e, c=c: e.dma_start(out=X[:, c, :], in_=xv[:, c, :]), w=[XT[c]])

        def stageW(T):
            for cc in range(4):
                c = T * 4 + cc
                for hf in range(2):
                    bO = newbank()
                    for m in range(KC):
                        S.op("pe", lambda e, bO=bO, m=m, cc=cc, hf=hf: e.matmul(psf[bO][:, :], lhsT=M2[:, T, m, cc * 128:(cc + 1) * 128],
                                                                               rhs=Wout[:, m, hf * 512:(hf + 1) * 512], start=(m == 0), stop=(m == KC - 1)),
                             r=[M2T[T], WoutT[m // 2]], w=[pst[bO]])
                    S.op("dve", lambda e, bO=bO, c=c, hf=hf: e.tensor_tensor(out=X[:, c, hf * 512:(hf + 1) * 512], in0=psf[bO][:, :],
                                                                            in1=X[:, c, hf * 512:(hf + 1) * 512], op=ALU.add),
                         r=[pst[bO], XT[c]], w=[XT[c]])

        def norm2(T, xn_lo=0):
            norm_to_hT(T, lambda cc, T=T: X[:, T * 4 + cc, :], lambda cc, T=T: XT[T * 4 + cc], gb2, gb2T, h2T4, h2TT, ss2, ss2T, rs1, rs1T, sd1, sd1T,
                       xn_lo=xn_lo)

        stageW(0)
        stageW(1)
        norm2(0)
        stageW(2)
        norm2(1)
        stageW(3)
        load_next_w()
        norm2(2)
        if dbg:
            dump("d_x1", X[:, :, :], XT)

        fb = [P1.view(32768 + i * 8192, BF16, [KC, 512]) for i in range(2)]
        fbT = [[P1.tk(32768 + i * 8192 + j * 1024, 1024) for j in range(KC)] for i in range(2)]
        fb_rr = [0]

        def ffn1(q, T, fi):
            W1, W1T = W("f1_%d" % q)
            for j in range(KC):
                bF = newbank()
                for k in range(KC):
                    S.op("pe", lambda e, bF=bF, k=k, j=j, W1=W1: e.matmul(psf[bF][:, :], lhsT=W1[:, k, j * 128:(j + 1) * 128], rhs=h2T4[:, T, k, :],
                                                                         start=(k == 0), stop=(k == KC - 1)),
                         r=[W1T[k // 2], h2TT[T]], w=[pst[bF]])
                S.op("act", lambda e, bF=bF, j=j, fi=fi: e.activation(out=fb[fi][:, j, :], in_=psf[bF][:, :], func=AF.Relu), r=[pst[bF]], w=[fbT[fi][j]])
                S.op("dve", lambda e, j=j, fi=fi: e.tensor_tensor(out=fb[fi][:, j, :], in0=fb[fi][:, j, :], in1=fb[fi][:, j, :], op=ALU.mult),
                     r=[fbT[fi][j]], w=[fbT[fi][j]])

        def ffn2(q, T, fi):
            W2, W2T = W("f2_%d" % q)
            for cc in range(4):
                c = T * 4 + cc
                for hf in range(2):
                    bO = newbank()
                    for j in range(KC):
                        S.op("pe", lambda e, bO=bO, j=j, cc=cc, hf=hf, W2=W2: e.matmul(psf[bO][:, :], lhsT=fb[fi][:, j, cc * 128:(cc + 1) * 128],
                                                                                      rhs=W2[:, j, hf * 512:(hf + 1) * 512], start=(j == 0), stop=(j == KC - 1)),
                             r=[fbT[fi][j], W2T[j // 2]], w=[pst[bO]])
                    S.op("dve", lambda e, bO=bO, c=c, hf=hf: e.tensor_tensor(out=X[:, c, hf * 512:(hf + 1) * 512], in0=psf[bO][:, :],
                                                                            in1=X[:, c, hf * 512:(hf + 1) * 512], op=ALU.add),
                         r=[pst[bO], XT[c]], w=[XT[c]])

        def final_norm(T):
            for cc in range(4):
                c = T * 4 + cc
                S.op("act", lambda e, c=c: e.activation(out=junk[:, :], in_=X[:, c, :], func=AF.Square, accum_out=ss3[:, c:c + 1]),
                     r=[XT[c]], w=[junkT, ss3T])
            S.op("dve", lambda e: e.tensor_scalar(out=sd1[:, T * 4:T * 4 + 4], in0=ss3[:, T * 4:T * 4 + 4], scalar1=1.0 / D, scalar2=EPS,
                                                  op0=ALU.mult, op1=ALU.add), r=[ss3T], w=[sd1T])
            S.op("act", lambda e: e.activation(out=sd1[:, T * 4:T * 4 + 4], in_=sd1[:, T * 4:T * 4 + 4], func=AF.Sqrt), r=[sd1T], w=[sd1T])
            S.op("dve", lambda e: e.reciprocal(out=rs1[:, T * 4:T * 4 + 4], in_=sd1[:, T * 4:T * 4 + 4]), r=[sd1T], w=[rs1T])
            for cc in range(4):
                c = T * 4 + cc
                S.op("dve", lambda e, c=c: e.scalar_tensor_tensor(out=X[:, c, :], in0=X[:, c, :], scalar=rs1[:, c:c + 1], in1=gbf[:, :],
                                                                  op0=ALU.mult, op1=ALU.mult),
                     r=[XT[c], rs1T, gbfT], w=[XT[c]])
                out_toks.append(S.dma("sp", lambda e, c=c: e.dma_start(out=out[c * 128:(c + 1) * 128, :], in_=X[:, c, :]), r=[XT[c]]))

        seq = [(q, T) for q in range(4) for T in range(NT)]
        fis = []
        for n, (q, T) in enumerate(seq):
            fis.append(n % 2)
        ffn1(seq[0][0], seq[0][1], fis[0])
        norm2(3, xn_lo=4)
        dump("d_h2", h2T4[:, :, :, :], h2TT)
        for n, (q, T) in enumerate(seq):
            if n + 1 < len(seq):
                ffn1(seq[n + 1][0], seq[n + 1][1], fis[n + 1])
            if n + 1 < len(seq) and seq[n + 1][1] == NT - 1 and seq[n + 1][0] < 2:
                load_next_w()
            ffn2(q, T, fis[n])
            if T == NT - 1 and q < 2:
                load_next_w()
            if q == 3:
                final_norm(T)

        S.wait("sp", out_toks)
        S.run()
    return nc


_W_NAMES = ["norm1_g", "w_in", "b_in", "sgu_ln_g", "sgu_ln_b", "sgu_w", "sgu_b", "w_proj_a", "conv_w", "conv_b",
            "conv_ln_g", "conv_ln_b", "w_proj_b", "b_proj_b", "w_out", "norm2_g", "w_ff1", "w_ff2", "norm_f_g"]


def _prep(inputs):
    d = {}
    for nm in _W_NAMES:
        a = np.asarray(inputs[nm], dtype=np.float32)
        if nm != "norm_f_g":
            a = a[0]
        if nm == "sgu_b":
            a = a.reshape(-1)
        d[nm] = np.ascontiguousarray(a)
    return d


def kernel(**inputs):
    x = np.asarray(inputs["x"], dtype=np.float32)
    B = x.shape[0]
    wd = _prep(inputs)
    nc = build_nc()
    in_maps = []
    for b in range(B):
        m = dict(wd)
        m["x"] = np.ascontiguousarray(x[b])
        in_maps.append(m)
    res = run_bass_kernel_spmd(nc, in_maps, core_ids=list(range(B)))
    return np.stack([np.asarray(r["out"], dtype=np.float32) for r in res.results], axis=0)
```

```python
import contextlib
import numpy as np
import concourse.bass as bass
import concourse.mybir as mybir
from concourse.bass_utils import run_bass_kernel_spmd

F32 = mybir.dt.float32
BF16 = mybir.dt.bfloat16
AF = mybir.ActivationFunctionType
ALU = mybir.AluOpType
DSZ = {F32: 4, BF16: 2}

SEQ = 2048
D = 1024
NT = 4
NCH = 16
KC = 8
CK = 31
EPS = 1e-6


class Tk:
    __slots__ = ("sp", "lo", "hi", "lw", "rd", "ov")

    def __init__(self, sp, lo, hi):
        self.sp, self.lo, self.hi = sp, lo, hi
        self.lw = []
        self.rd = []
        self.ov = None


class Sched:
    ENG = ("pe", "dve", "act", "pool", "sp")

    def __init__(self, nc, n_dma_sems=40):
        self.nc = nc
        self.sem = {e: nc.alloc_semaphore(name="es_" + e) for e in self.ENG}
        self.cnt = {e: 0 for e in self.ENG}
        self.prog = {e: [] for e in self.ENG}
        self.seen = {e: {} for e in self.ENG}
        self.pending = {e: None for e in self.ENG}
        self.dsem = [nc.alloc_semaphore(name="ds_%d" % i) for i in range(n_dma_sems)]
        self.dval = [0] * n_dma_sems
        self.drr = 0
        self.tiles = {}

    def tile(self, sp, lo, hi):
        t = Tk(sp, lo, hi)
        lst = self.tiles.setdefault(sp, [])
        t.ov = [t]
        for o in lst:
            if o.lo < hi and lo < o.hi:
                t.ov.append(o)
                o.ov.append(t)
        lst.append(t)
        return t

    def _need(self, eng, tok):
        if tok[0] == "e":
            _, p, c = tok
            if c > self.cnt[p]:
                ent = self.pending[p]
                assert ent is not None and c == self.cnt[p] + 1, (p, c, self.cnt[p])
                ent["inc"] = True
                self.cnt[p] += 1
                self.pending[p] = None
            key = ("e", p)
            sem = self.sem[p]
            val = c
        else:
            _, i, val = tok
            key = ("d", i)
            sem = self.dsem[i]
        if self.seen[eng].get(key, 0) >= val:
            return None
        self.seen[eng][key] = val
        return (sem, val)

    def _deps(self, eng, r, w):
        toks = []
        for t in r:
            for o in t.ov:
                toks.extend(o.lw)
        same = []
        if eng != "pe":
            same = [tok for tok in toks if tok[0] == "e" and tok[1] == eng]
        for t in w:
            for o in t.ov:
                toks.extend(o.lw)
                toks.extend(o.rd)
        waits = []
        for tok in toks:
            if tok[0] == "e" and tok[1] == eng:
                continue
            wt = self._need(eng, tok)
            if wt is not None:
                waits.append(wt)
        for tok in same:
            wt = self._need(eng, tok)
            if wt is not None:
                waits.append(wt)
        return waits

    def _mark(self, tok, r, w):
        for t in r:
            if not t.rd or t.rd[-1] != tok:
                t.rd.append(tok)
        for t in w:
            for o in t.ov:
                o.lw = [tok]
                o.rd = []

    def op(self, eng, fn, r=(), w=()):
        waits = self._deps(eng, r, w)
        ent = {"fn": fn, "inc": False, "waits": waits, "dma": None}
        self.prog[eng].append(ent)
        self.pending[eng] = ent
        tok = ("e", eng, self.cnt[eng] + 1)
        self._mark(tok, r, w)
        return tok

    def dma(self, eng, fn, r=(), w=()):
        waits = self._deps(eng, r, w)
        i = self.drr
        self.drr = (self.drr + 1) % len(self.dsem)
        if self.dval[i] > 0:
            wt = self._need(eng, ("d", i, self.dval[i]))
            if wt is not None:
                waits.append(wt)
        self.dval[i] += 16
        tok = ("d", i, self.dval[i])
        self.prog[eng].append({"fn": fn, "inc": False, "waits": waits, "dma": i})
        self._mark(tok, r, w)
        return tok

    def wait(self, eng, toks):
        waits = []
        for tok in toks:
            wt = self._need(eng, tok)
            if wt is not None:
                waits.append(wt)
        if waits:
            self.prog[eng].append({"fn": None, "inc": False, "waits": waits, "dma": None})

    def replay(self, eng, e):
        sem = self.sem[eng]
        for ent in self.prog[eng]:
            for (s, v) in ent["waits"]:
                e.wait_ge(s, v)
            if ent["fn"] is None:
                continue
            ins = ent["fn"](e)
            if ent["dma"] is not None:
                ins.then_inc(self.dsem[ent["dma"]], 16)
            elif ent["inc"]:
                ins.then_inc(sem, 1)

    def run(self):
        with self.nc.Block() as block:
            @block.tensor
            def _(e):
                self.replay("pe", e)

            @block.vector
            def _(e):
                self.replay("dve", e)

            @block.scalar
            def _(e):
                self.replay("act", e)

            @block.gpsimd
            def _(e):
                self.replay("pool", e)

            @block.sync
            def _(e):
                self.replay("sp", e)


class Region:
    def __init__(self, S, stack, name, nbytes):
        self.S = S
        self.name = name
        self.nbytes = nbytes
        self.t = stack.enter_context(S.nc.sbuf_tensor(name, [128, nbytes // 4], F32))

    def view(self, lo, dtype, shape):
        n = 1
        for s in shape:
            n *= s
        nb = n * DSZ[dtype]
        assert lo % 4 == 0 and nb % 4 == 0 and lo + nb <= self.nbytes, (self.name, lo, nb)
        ap = self.t[:, lo // 4:(lo + nb) // 4]
        if dtype != F32:
            ap = ap.bitcast(dtype)
        if len(shape) == 2:
            ap = ap.rearrange("p (a b) -> p a b", a=shape[0])
        elif len(shape) == 3:
            ap = ap.rearrange("p (a b c) -> p a b c", a=shape[0], b=shape[1])
        return ap

    def tk(self, lo, nb):
        assert lo + nb <= self.nbytes
        return self.S.tile("sb:" + self.name, lo, lo + nb)


def build_nc(dbg=False):
    nc = bass.Bass("TRN2", target_bir_lowering=False)

    def din(name, shape):
        return nc.dram_tensor(name, shape, F32, kind="ExternalInput").ap()

    x = din("x", [SEQ, D])
    norm1_g = din("norm1_g", [D])
    w_in = din("w_in", [D, 6 * D])
    b_in = din("b_in", [6 * D])
    sgu_ln_g = din("sgu_ln_g", [D])
    sgu_ln_b = din("sgu_ln_b", [D])
    sgu_w = din("sgu_w", [8, 128, 128])
    sgu_b = din("sgu_b", [D])
    w_proj_a = din("w_proj_a", [D, D])
    conv_w = din("conv_w", [CK, D])
    conv_b = din("conv_b", [D])
    conv_ln_g = din("conv_ln_g", [D])
    conv_ln_b = din("conv_ln_b", [D])
    w_proj_b = din("w_proj_b", [D, D])
    b_proj_b = din("b_proj_b", [D])
    w_out = din("w_out", [D, D])
    norm2_g = din("norm2_g", [D])
    w_ff1 = din("w_ff1", [D, 4 * D])
    w_ff2 = din("w_ff2", [4 * D, D])
    norm_f_g = din("norm_f_g", [D])
    out = nc.dram_tensor("out", [SEQ, D], F32, kind="ExternalOutput").ap()
    dbg_t = {}
    if dbg:
        for nm, shp, dt_ in (("d_hT", [128, NT * KC * 512], BF16), ("d_cs", [128, KC * SEQ], BF16),
                             ("d_m2", [128, NT * KC * 512], BF16), ("d_mg", [128, NT * KC * 512], BF16),
                             ("d_x1", [128, NCH * D], F32), ("d_h2", [128, NT * KC * 512], BF16),
                             ("d_vec", [128, 128], F32), ("d_cw", [128, KC * CK], F32),
                             ("d_bg", [128, D], F32), ("d_wst", [128, D], BF16),
                             ("d_vh", [128, 4 * D], BF16), ("d_su", [128, KC * 512], BF16)):
            dbg_t[nm] = nc.dram_tensor(nm, shp, dt_, kind="ExternalOutput").ap()

    S = Sched(nc)
    with contextlib.ExitStack() as st:
        RING = Region(S, st, "RING", 65536)
        P1 = Region(S, st, "P1", 49152)
        P23 = Region(S, st, "P23", 65536)
        CST = Region(S, st, "CST", 28672)
        psf = [st.enter_context(nc.psum_tensor("ps%d" % i, [128, 512], F32)) for i in range(8)]
        psb = [p[:, :].bitcast(BF16) for p in psf]
        pst = [S.tile("ps", i, i + 1) for i in range(8)]
        bank_rr = [0]

        def newbank():
            i = bank_rr[0]
            bank_rr[0] = (i + 1) % 8
            return i

        out_toks = []

        def dump(nm, ap, tks):
            if dbg:
                out_toks.append(S.dma("sp", lambda e: e.dma_start(out=dbg_t[nm][:, :], in_=ap), r=tks))

        coff = [0]

        def calloc(dtype, shape):
            n = int(np.prod(shape)) * DSZ[dtype]
            n = (n + 31) // 32 * 32
            v = CST.view(coff[0], dtype, shape)
            t = CST.tk(coff[0], n)
            coff[0] += n
            return v, t

        ident, identT = calloc(BF16, [128])
        identf, identfT = calloc(F32, [128])
        vec, vecT = calloc(F32, [128])
        convw, convwT = calloc(F32, [KC, 32])
        gb1, gb1T = calloc(F32, [D])
        gb2, gb2T = calloc(F32, [D])
        gbf, gbfT = calloc(F32, [D])
        Bg, BgT = calloc(F32, [D])
        WsT, WsTT = calloc(BF16, [8, 128])
        bvrow, bvrowT = calloc(BF16, [D])
        onesrow, onesrowT = calloc(BF16, [128])
        onesm, onesmT = calloc(BF16, [128])
        epsc, epscT = calloc(F32, [8])
        ss1, ss1T = calloc(F32, [NCH])
        ss2, ss2T = calloc(F32, [NCH])
        ss3, ss3T = calloc(F32, [NCH])
        rs1, rs1T = calloc(F32, [NCH])
        sd1, sd1T = calloc(F32, [NCH])
        bst, bstT = calloc(F32, [4, 12])
        mv, mvT = calloc(F32, [4, 2])
        rv, rvT = calloc(F32, [4])
        sdv, sdvT = calloc(F32, [4])
        junk, junkT = calloc(F32, [D])
        V_BU, V_BV, V_BVAL, V_BGATE, V_BG0, V_BG1, V_N1, V_N2, V_SLG, V_CB, V_CLG, V_CLB, V_BPB = [8 * i for i in range(13)]

        S.op("pool", lambda e: e.memset(identf[:, :], 0.0), w=[identfT])
        S.op("pool", lambda e: e.affine_select(out=identf[:, :], in_=identf[:, :], pattern=[[-1, 128]],
                                               compare_op=ALU.not_equal, fill=1.0, base=0, channel_multiplier=1),
             r=[identfT], w=[identfT])
        S.op("dve", lambda e: e.tensor_copy(out=ident[:, :], in_=identf[:, :]), r=[identfT], w=[identT])
        S.op("pool", lambda e: e.memset(onesrow[0:1, :], 1.0), w=[onesrowT])
        S.op("pool", lambda e: e.memset(onesm[:, :], 1.0 / D), w=[onesmT])
        S.op("pool", lambda e: e.memset(epsc[:, :], EPS), w=[epscT])
        for s_, sT_ in ((ss1, ss1T), (ss2, ss2T), (ss3, ss3T)):
            S.op("pool", lambda e, s_=s_: e.memset(s_[:, :], 0.0), w=[sT_])
        slotv = [RING.view(s * 16384, BF16, [8, 1024]) for s in range(4)]
        slotT = [[RING.tk(s * 16384 + kp * 4096, 4096) for kp in range(4)] for s in range(4)]
        w_in_v = w_in.rearrange("(k p) e -> p k e", p=128)
        wsrc = {
            "u": w_in_v[:, :, 0 * D:1 * D], "v": w_in_v[:, :, 1 * D:2 * D], "val": w_in_v[:, :, 2 * D:3 * D],
            "gate": w_in_v[:, :, 3 * D:4 * D], "g0": w_in_v[:, :, 4 * D:5 * D], "g1": w_in_v[:, :, 5 * D:6 * D],
            "pa": w_proj_a.rearrange("(k p) e -> p k e", p=128), "pb": w_proj_b.rearrange("(k p) e -> p k e", p=128),
            "out": w_out.rearrange("(k p) e -> p k e", p=128),
        }
        w1v = w_ff1.rearrange("(k p) e -> p k e", p=128)
        w2v = w_ff2.rearrange("(j p) e -> p j e", p=128)
        for q in range(4):
            wsrc["f1_%d" % q] = w1v[:, :, q * D:(q + 1) * D]
            wsrc["f2_%d" % q] = w2v[:, q * 8:(q + 1) * 8, :]
        worder = ["val", "gate", "pb", "g1", "v", "u", "pa", "g0", "out",
                  "f1_0", "f2_0", "f1_1", "f2_1", "f1_2", "f2_2", "f1_3", "f2_3"]
        wslot = {nm: i % 4 for i, nm in enumerate(worder)}
        wnext = [0]

        def load_next_w():
            if wnext[0] >= len(worder):
                return
            nm = worder[wnext[0]]
            wnext[0] += 1
            s = wslot[nm]
            src = wsrc[nm]
            for kp in range(4):
                S.dma("pool", lambda e, s=s, kp=kp, src=src: e.dma_start(out=slotv[s][:, 2 * kp:2 * kp + 2, :], in_=src[:, 2 * kp:2 * kp + 2, :]),
                      w=[slotT[s][kp]])

        def W(nm):
            s = wslot[nm]
            return slotv[s], slotT[s]

        for _ in range(2):
            load_next_w()

        so = 32768
        VR = P23.view(so, F32, [128]); VRT = P23.tk(so, 512); so += 512
        CW = P23.view(so, F32, [D]); CWT = P23.tk(so, 4096); so += 4096
        Wtf = P23.view(so, F32, [8, 128]); WtfT = P23.tk(so, 4096); so += 4096
        Wtb = P23.view(so, BF16, [8, 128]); WtbT = P23.tk(so, 2048); so += 2048
        betaf = P23.view(so, F32, [D]); betafT = P23.tk(so, 4096); so += 4096
        betab = P23.view(so, BF16, [D]); betabT = P23.tk(so, 2048); so += 2048

        hT4 = P23.view(0, BF16, [NT, KC, 512])
        hTT = [P23.tk(T * 8192, 8192) for T in range(NT)]
        xt = [P1.view(i * 16384, F32, [4, D]) for i in range(2)]
        xtT = [[P1.tk(i * 16384 + cc * 4096, 4096) for cc in range(4)] for i in range(2)]
        xv = x.rearrange("(c p) d -> p c d", p=128)

        def load_x_tile(T):
            i = T % 2
            for cc in range(4):
                c = T * 4 + cc
                S.dma("sp", lambda e, i=i, cc=cc, c=c: e.dma_start(out=xt[i][:, cc, :], in_=xv[:, c, :]), w=[xtT[i][cc]])

        S.op("dve", lambda e: e.memset(VR[:, :], 0.0), w=[VRT])
        load_x_tile(0)
        S.dma("sp", lambda e: e.dma_start(out=gb1[:, :], in_=norm1_g.partition_broadcast(128)), w=[gb1T])
        load_x_tile(1)
        S.dma("pool", lambda e: e.dma_start(out=bvrow[0:1, :], in_=b_in[D:2 * D].rearrange("(o n) -> o n", o=1)), w=[bvrowT])
        S.dma("pool", lambda e: e.dma_start(out=CW[0:CK, :], in_=conv_w[:, :]), w=[CWT])
        S.dma("pool", lambda e: e.dma_start(out=VR[0:48, :], in_=b_in.rearrange("(r p) -> r p", p=128)), w=[VRT])
        for i, v_ in enumerate((norm1_g, norm2_g, sgu_ln_g, conv_b, conv_ln_g, conv_ln_b, b_proj_b)):
            S.dma("pool", lambda e, i=i, v_=v_: e.dma_start(out=VR[48 + 8 * i:56 + 8 * i, :], in_=v_.rearrange("(r p) -> r p", p=128)), w=[VRT])

        def late_setup_dmas():
            S.dma("sp", lambda e: e.dma_start(out=Wtf[:, :, :], in_=sgu_w.rearrange("g t s -> t g s")), w=[WtfT])
            S.dma("sp", lambda e: e.dma_start(out=Bg[:, :], in_=sgu_b.partition_broadcast(128)), w=[BgT])
            S.dma("sp", lambda e: e.dma_start(out=betaf[:, :], in_=sgu_ln_b.partition_broadcast(128)), w=[betafT])
            S.dma("sp", lambda e: e.dma_start(out=gb2[:, :], in_=norm2_g.partition_broadcast(128)), w=[gb2T])
            S.dma("sp", lambda e: e.dma_start(out=gbf[:, :], in_=norm_f_g.partition_broadcast(128)), w=[gbfT])


        xn = [P1.view(32768 + i * 2048, BF16, [D]) for i in range(8)]
        xnT = [P1.tk(32768 + i * 2048, 2048) for i in range(8)]
        xn_rr = [0]

        def norm_to_hT(T, src_ap_fn, src_tk_fn, gb, gbT, dst4, dstT, ss, ssT, rs, rsT, sd, sdT, xn_lo=0):
            for cc in range(4):
                c = T * 4 + cc
                S.op("act", lambda e, cc=cc, c=c: e.activation(out=junk[:, :], in_=src_ap_fn(cc), func=AF.Square, accum_out=ss[:, c:c + 1]),
                     r=[src_tk_fn(cc)], w=[junkT, ssT])
            S.op("dve", lambda e: e.tensor_scalar(out=sd[:, T * 4:T * 4 + 4], in0=ss[:, T * 4:T * 4 + 4], scalar1=1.0 / D, scalar2=EPS,
                                                  op0=ALU.mult, op1=ALU.add), r=[ssT], w=[sdT])
            S.op("act", lambda e: e.activation(out=sd[:, T * 4:T * 4 + 4], in_=sd[:, T * 4:T * 4 + 4], func=AF.Sqrt), r=[sdT], w=[sdT])
            S.op("dve", lambda e: e.reciprocal(out=rs[:, T * 4:T * 4 + 4], in_=sd[:, T * 4:T * 4 + 4]), r=[sdT], w=[rsT])
            for cc in range(4):
                c = T * 4 + cc
                xi = xn_lo + xn_rr[0] % (8 - xn_lo)
                xn_rr[0] += 1
                S.op("dve", lambda e, cc=cc, c=c, xi=xi: e.scalar_tensor_tensor(out=xn[xi][:, :], in0=src_ap_fn(cc), scalar=rs[:, c:c + 1],
                                                                               in1=gb[:, :], op0=ALU.mult, op1=ALU.mult),
                     r=[src_tk_fn(cc), rsT, gbT], w=[xnT[xi]])
                bk = newbank()
                for k in range(KC):
                    S.op("pe", lambda e, bk=bk, k=k, xi=xi: e.transpose(out=psb[bk][:, k * 128:(k + 1) * 128], in_=xn[xi][:, k * 128:(k + 1) * 128],
                                                                        identity=ident[:, :]), r=[xnT[xi], identT], w=[pst[bk]])
                eng = "act" if (cc % 2 == 0) else "dve"
                if eng == "act":
                    S.op("act", lambda e, bk=bk, cc=cc: e.activation(out=dst4[:, T, :, cc * 128:(cc + 1) * 128],
                                                                     in_=psb[bk][:, :].rearrange("p (k t) -> p k t", k=KC), func=AF.Copy),
                         r=[pst[bk]], w=[dstT[T]])
                else:
                    S.op("dve", lambda e, bk=bk, cc=cc: e.tensor_copy(out=dst4[:, T, :, cc * 128:(cc + 1) * 128],
                                                                      in_=psb[bk][:, :].rearrange("p (k t) -> p k t", k=KC)),
                         r=[pst[bk]], w=[dstT[T]])

        for T in range(NT):
            i = T % 2
            norm_to_hT(T, lambda cc, i=i: xt[i][:, cc, :], lambda cc, i=i: xtT[i][cc], gb1, gb1T, hT4, hTT, ss1, ss1T, rs1, rs1T, sd1, sd1T)
            if T + 2 < NT:
                load_x_tile(T + 2)
            if T == NT - 1:
                late_setup_dmas()
        bk = newbank()
        S.op("pe", lambda e, bk=bk: e.transpose(out=psf[bk][:, 0:104], in_=VR[0:104, :], identity=identf[0:104, 0:104]),
             r=[VRT, identfT], w=[pst[bk]])
        S.op("dve", lambda e, bk=bk: e.tensor_copy(out=vec[:, 0:104], in_=psf[bk][:, 0:104]), r=[pst[bk]], w=[vecT])
        bk = newbank()
        for j in range(KC):
            S.op("pe", lambda e, bk=bk, j=j: e.transpose(out=psf[bk][:, j * 32:j * 32 + CK], in_=CW[0:CK, j * 128:(j + 1) * 128],
                                                         identity=identf[0:CK, 0:CK]), r=[CWT, identfT], w=[pst[bk]])
        S.op("dve", lambda e, bk=bk: e.tensor_copy(out=convw[:, :, 0:CK],
                                                   in_=psf[bk][:, 0:256].rearrange("p (j k) -> p j k", k=32)[:, :, 0:CK]),
             r=[pst[bk]], w=[convwT])
        dump("d_hT", hT4[:, :, :, :], hTT)
        dump("d_vec", vec[:, :], [vecT])
        dump("d_cw", convw[:, :, 0:CK], [convwT])
        S.op("pool", lambda e: e.affine_select(out=Wtf[:, :, :], in_=Wtf[:, :, :], pattern=[[0, 8], [-1, 128]],
                                               compare_op=ALU.is_ge, fill=0.0, base=0, channel_multiplier=1),
             r=[WtfT], w=[WtfT])
        S.op("dve", lambda e: e.tensor_copy(out=Wtb[:, :, :], in_=Wtf[:, :, :]), r=[WtfT], w=[WtbT])
        bk = newbank()
        for g in range(8):
            S.op("pe", lambda e, bk=bk, g=g: e.transpose(out=psb[bk][:, g * 128:(g + 1) * 128], in_=Wtb[:, g, :], identity=ident[:, :]),
                 r=[WtbT, identT], w=[pst[bk]])
        S.op("dve", lambda e, bk=bk: e.tensor_copy(out=WsT[:, :, :], in_=psb[bk][:, :].rearrange("p (g t) -> p g t", g=8)),
             r=[pst[bk]], w=[WsTT])
        S.op("dve", lambda e: e.tensor_copy(out=betab[:, :], in_=betaf[:, :]), r=[betafT], w=[betabT])
        for hh in range(2):
            bk = newbank()
            for g4 in range(4):
                g = hh * 4 + g4
                S.op("pe", lambda e, bk=bk, g=g, g4=g4: e.matmul(psf[bk][:, g4 * 128:(g4 + 1) * 128], lhsT=betab[:, g * 128:(g + 1) * 128],
                                                                rhs=WsT[:, g, :], start=True, stop=True),
                     r=[betabT, WsTT], w=[pst[bk]])
            S.op("dve", lambda e, bk=bk, hh=hh: e.tensor_tensor(out=Bg[:, hh * 512:(hh + 1) * 512], in0=psf[bk][:, :],
                                                               in1=Bg[:, hh * 512:(hh + 1) * 512], op=ALU.add),
                 r=[pst[bk], BgT], w=[BgT])
        dump("d_bg", Bg[:, :], [BgT])
        dump("d_wst", WsT[:, :, :], [WsTT])

        CV = P23.view(32768, BF16, [KC, SEQ])
        CVT = [[P23.tk(32768 + j * 4096 + T * 1024, 1024) for T in range(NT)] for j in range(KC)]
        APAD = 2080
        abuf = [P1.view(i * 4160, BF16, [APAD]) for i in range(2)]
        abufT = [[P1.tk(i * 4160, 60)] + [P1.tk(i * 4160 + 60 + T * 1024, 1024) for T in range(NT)] for i in range(2)]
        dg = [P1.view(8320 + i * 7936, BF16, [CK, 128]) for i in range(2)]
        dgT = [P1.tk(8320 + i * 7936, 7936) for i in range(2)]
        sig = [P1.view(24192 + i * 1024, BF16, [512]) for i in range(2)]
        sigT = [P1.tk(24192 + i * 1024, 1024) for i in range(2)]
        M2 = P1.view(0, BF16, [NT, KC, 512])
        M2T = [P1.tk(T * 8192, 8192) for T in range(NT)]
        sqt = [P1.view(26240 + i * 1024, BF16, [512]) for i in range(8)]
        sqtT = [P1.tk(26240 + i * 1024, 1024) for i in range(8)]
        NYB = 3
        yb = [P1.view(34432 + i * 2048, F32, [512]) for i in range(NYB)]
        ybT = [P1.tk(34432 + i * 2048, 2048) for i in range(NYB)]
        stt = P1.view(40576, F32, [512]); sttT = P1.tk(40576, 2048)
        rstdb = P1.view(42624, F32, [512]); rstdbT = P1.tk(42624, 2048)
        nmrb = P1.view(44672, F32, [512]); nmrbT = P1.tk(44672, 2048)
        sg = [P1.view(46720 + i * 1024, BF16, [512]) for i in range(2)]
        sgT = [P1.tk(46720 + i * 1024, 1024) for i in range(2)]
        Wpb, WpbT = W("pb")
        Wg1, Wg1T = W("g1")

        def btail_sq(T):
            for j in range(KC):
                S.op("dve", lambda e, j=j: e.tensor_tensor(out=sqt[j][:, :], in0=CV[:, j, T * 512:(T + 1) * 512], in1=CV[:, j, T * 512:(T + 1) * 512], op=ALU.mult),
                     r=[CVT[j][T]], w=[sqtT[j]])

        def btail_stats(T):
            bM = newbank()
            bQ = newbank()
            for j in range(KC):
                qi = j
                S.op("pe", lambda e, j=j, bM=bM: e.matmul(psf[bM][:, :], lhsT=onesm[:, :], rhs=CV[:, j, T * 512:(T + 1) * 512], start=(j == 0), stop=(j == KC - 1)),
                     r=[onesmT, CVT[j][T]], w=[pst[bM]])
                S.op("pe", lambda e, j=j, bQ=bQ, qi=qi: e.matmul(psf[bQ][:, :], lhsT=onesm[:, :], rhs=sqt[qi][:, :], start=(j == 0), stop=(j == KC - 1)),
                     r=[onesmT, sqtT[qi]], w=[pst[bQ]])
            S.op("act", lambda e: e.activation(out=stt[:, :], in_=psf[bM][:, :], func=AF.Square), r=[pst[bM]], w=[sttT])
            S.op("dve", lambda e: e.tensor_tensor(out=stt[:, :], in0=psf[bQ][:, :], in1=stt[:, :], op=ALU.subtract), r=[pst[bQ], sttT], w=[sttT])
            S.op("act", lambda e: e.activation(out=stt[:, :], in_=stt[:, :], func=AF.Sqrt, bias=epsc[:, 0:1], scale=1.0), r=[sttT, epscT], w=[sttT])
            S.op("dve", lambda e: e.reciprocal(out=rstdb[:, :], in_=stt[:, :]), r=[sttT], w=[rstdbT])
            S.op("dve", lambda e: e.scalar_tensor_tensor(out=nmrb[:, :], in0=psf[bM][:, :], scalar=-1.0, in1=rstdb[:, :], op0=ALU.mult, op1=ALU.mult),
                 r=[pst[bM], rstdbT], w=[nmrbT])

        def btail_norm(T):
            for j in range(KC):
                yi = j % NYB
                S.op("dve", lambda e, j=j, yi=yi: e.tensor_tensor(out=yb[yi][:, :], in0=CV[:, j, T * 512:(T + 1) * 512], in1=rstdb[:, :], op=ALU.mult),
                     r=[CVT[j][T], rstdbT], w=[ybT[yi]])
                S.op("dve", lambda e, j=j, yi=yi: e.tensor_tensor(out=yb[yi][:, :], in0=yb[yi][:, :], in1=nmrb[:, :], op=ALU.add),
                     r=[ybT[yi], nmrbT], w=[ybT[yi]])
                S.op("act", lambda e, j=j, yi=yi: e.activation(out=CV[:, j, T * 512:(T + 1) * 512], in_=yb[yi][:, :], func=AF.Silu,
                                                               bias=vec[:, V_CLB + j:V_CLB + j + 1], scale=vec[:, V_CLG + j:V_CLG + j + 1]),
                     r=[ybT[yi], vecT], w=[CVT[j][T]])

        sg_rr = [0]

        def btail_proj(T):
            for m in range(KC):
                bG = newbank()
                for k in range(KC):
                    S.op("pe", lambda e, bG=bG, k=k, m=m: e.matmul(psf[bG][:, :], lhsT=Wg1[:, k, m * 128:(m + 1) * 128], rhs=hT4[:, T, k, :],
                                                                  start=(k == 0), stop=(k == KC - 1)),
                         r=[Wg1T[k // 2], hTT[T]], w=[pst[bG]])
                bY = newbank()
                for j in range(KC):
                    S.op("pe", lambda e, bY=bY, j=j, m=m: e.matmul(psf[bY][:, :], lhsT=Wpb[:, j, m * 128:(m + 1) * 128], rhs=CV[:, j, T * 512:(T + 1) * 512],
                                                                  start=(j == 0), stop=(j == KC - 1)),
                         r=[WpbT[j // 2], CVT[j][T]], w=[pst[bY]])
                si = sg_rr[0]
                sg_rr[0] = (si + 1) % 2
                S.op("act", lambda e, bG=bG, si=si, m=m: e.activation(out=sg[si][:, :], in_=psf[bG][:, :], func=AF.Sigmoid,
                                                                     bias=vec[:, V_BG1 + m:V_BG1 + m + 1], scale=1.0),
                     r=[pst[bG], vecT], w=[sgT[si]])
                S.op("dve", lambda e, bY=bY, si=si, m=m: e.scalar_tensor_tensor(out=M2[:, T, m, :], in0=psf[bY][:, :], scalar=vec[:, V_BPB + m:V_BPB + m + 1],
                                                                               in1=sg[si][:, :], op0=ALU.add, op1=ALU.mult),
                     r=[pst[bY], vecT, sgT[si]], w=[M2T[T]])

        Wval, WvalT = W("val")
        Wgate, WgateT = W("gate")
        for i in range(2):
            S.op("pool", lambda e, i=i: e.memset(abuf[i][:, 0:30], 0.0), w=[abufT[i][0]])
        sig_rr = [0]
        for j in range(KC):
            ab = abuf[j % 2]
            abT = abufT[j % 2]
            dgi = dg[j % 2]
            dgiT = dgT[j % 2]
            S.op("dve", lambda e, dgi=dgi, j=j: e.tensor_tensor(out=dgi[:, :, :], in0=ident[:, None, :].to_broadcast([128, CK, 128]),
                                                                in1=convw[:, j, 0:CK, None].to_broadcast([128, CK, 128]), op=ALU.mult),
                 r=[identT, convwT], w=[dgiT])
            pend = None
            for T in range(NT + 1):
                if T < NT:
                    bA = newbank()
                    for k in range(KC):
                        S.op("pe", lambda e, bA=bA, k=k, j=j, T=T: e.matmul(psf[bA][:, :], lhsT=Wval[:, k, j * 128:(j + 1) * 128], rhs=hT4[:, T, k, :],
                                                                           start=(k == 0), stop=(k == KC - 1)),
                             r=[WvalT[k // 2], hTT[T]], w=[pst[bA]])
                    bB = newbank()
                    for k in range(KC):
                        S.op("pe", lambda e, bB=bB, k=k, j=j, T=T: e.matmul(psf[bB][:, :], lhsT=Wgate[:, k, j * 128:(j + 1) * 128], rhs=hT4[:, T, k, :],
                                                                           start=(k == 0), stop=(k == KC - 1)),
                             r=[WgateT[k // 2], hTT[T]], w=[pst[bB]])
                    si = sig_rr[0]
                    sig_rr[0] = (si + 1) % 2
                    S.op("act", lambda e, bB=bB, si=si, j=j: e.activation(out=sig[si][:, :], in_=psf[bB][:, :], func=AF.Sigmoid,
                                                                         bias=vec[:, V_BGATE + j:V_BGATE + j + 1], scale=1.0),
                         r=[pst[bB], vecT], w=[sigT[si]])
                    S.op("dve", lambda e, bA=bA, si=si, j=j, T=T, ab=ab: e.scalar_tensor_tensor(out=ab[:, 30 + T * 512:30 + (T + 1) * 512], in0=psf[bA][:, :],
                                                                                                  scalar=vec[:, V_BVAL + j:V_BVAL + j + 1], in1=sig[si][:, :],
                                                                                                  op0=ALU.add, op1=ALU.mult),
                         r=[pst[bA], vecT, sigT[si]], w=[abT[T + 1]])
                if pend is not None:
                    Tp = pend
                    bC = newbank()
                    for k in range(CK):
                        S.op("pe", lambda e, bC=bC, k=k, Tp=Tp, ab=ab, dgi=dgi: e.matmul(psf[bC][:, :], lhsT=dgi[:, k, :], rhs=ab[:, Tp * 512 + k:Tp * 512 + k + 512],
                                                                                        start=(k == 0), stop=(k == CK - 1)),
                             r=[dgiT, abT[Tp], abT[Tp + 1]], w=[pst[bC]])
                    S.op("act", lambda e, bC=bC, Tp=Tp, j=j: e.activation(out=CV[:, j, Tp * 512:(Tp + 1) * 512], in_=psf[bC][:, :], func=AF.Identity,
                                                                         bias=vec[:, V_CB + j:V_CB + j + 1], scale=1.0),
                         r=[pst[bC], vecT], w=[CVT[j][Tp]])
                    if j == KC - 1:
                        if Tp >= 1:
                            btail_stats(Tp - 1)
                            btail_norm(Tp - 1)
                        btail_sq(Tp)
                pend = T if T < NT else None
            if j == 1:
                load_next_w()
                load_next_w()
        load_next_w()
        load_next_w()

        btail_proj(0)
        btail_stats(NT - 1)
        btail_norm(NT - 1)
        for T in range(1, NT):
            btail_proj(T)
        dump("d_cs", CV[:, :, :], [t for row in CVT for t in row])
        dump("d_m2", M2[:, :, :, :], M2T)
        load_next_w()
        load_next_w()

        vh = [P23.view(32768 + i * 8192, BF16, [4, D]) for i in range(2)]
        vhT = [[P23.tk(32768 + i * 8192 + cc * 2048, 2048) for cc in range(4)] for i in range(2)]
        ub = [P23.view(49152 + i * 8192, BF16, [KC, 512]) for i in range(2)]
        ubT = [[P23.tk(49152 + i * 8192 + m * 1024, 1024) for m in range(KC)] for i in range(2)]
        stmp = [P1.view(32768 + i * 1024, BF16, [512]) for i in range(2)]
        stmpT = [P1.tk(32768 + i * 1024, 1024) for i in range(2)]
        sg0 = [P1.view(34816 + i * 1024, BF16, [512]) for i in range(2)]
        sg0T = [P1.tk(34816 + i * 1024, 1024) for i in range(2)]
        t1 = [P1.view(36864 + i * 1024, BF16, [512]) for i in range(2)]
        t1T = [P1.tk(36864 + i * 1024, 1024) for i in range(2)]
        Wv, WvT = W("v")
        Wu, WuT = W("u")
        Wpa, WpaT = W("pa")
        Wg0, Wg0T = W("g0")
        rr2 = [0, 0, 0]

        def stageA_v(T):
            i = T % 2
            for cc in range(4):
                bV = [newbank(), newbank()]
                for hf in range(2):
                    for k in range(KC):
                        S.op("pe", lambda e, b=bV[hf], k=k, cc=cc, hf=hf: e.matmul(psf[b][:, :], lhsT=hT4[:, T, k, cc * 128:(cc + 1) * 128],
                                                                                  rhs=Wv[:, k, hf * 512:(hf + 1) * 512], start=(k == 0), stop=False),
                             r=[hTT[T], WvT[k // 2]], w=[pst[bV[hf]]])
                    S.op("pe", lambda e, b=bV[hf], hf=hf: e.matmul(psf[b][:, :], lhsT=onesrow[0:1, :], rhs=bvrow[0:1, hf * 512:(hf + 1) * 512], start=False, stop=True),
                         r=[onesrowT, bvrowT], w=[pst[bV[hf]]])
                    S.op("act", lambda e, b=bV[hf], hf=hf, cc=cc, i=i: e.activation(out=vh[i][:, cc, hf * 512:(hf + 1) * 512], in_=psf[b][:, :], func=AF.Gelu_apprx_tanh),
                         r=[pst[bV[hf]]], w=[vhT[i][cc]])
                    S.op("dve", lambda e, hf=hf, cc=cc, i=i: e.bn_stats(out=bst[:, cc, hf * 6:(hf + 1) * 6], in_=vh[i][:, cc, hf * 512:(hf + 1) * 512]),
                         r=[vhT[i][cc]], w=[bstT])
                S.op("dve", lambda e, cc=cc: e.bn_aggr(out=mv[:, cc, :], in_=bst[:, cc, :]), r=[bstT], w=[mvT])
            S.op("dve", lambda e: e.tensor_scalar(out=sdv[:, :], in0=mv[:, :, 1], scalar1=EPS, scalar2=None, op0=ALU.add), r=[mvT], w=[sdvT])
            S.op("act", lambda e: e.activation(out=sdv[:, :], in_=sdv[:, :], func=AF.Sqrt), r=[sdvT], w=[sdvT])
            S.op("dve", lambda e: e.reciprocal(out=rv[:, :], in_=sdv[:, :]), r=[sdvT], w=[rvT])
            for cc in range(4):
                S.op("dve", lambda e, cc=cc, i=i: e.tensor_scalar(out=vh[i][:, cc, :], in0=vh[i][:, cc, :], scalar1=mv[:, cc, 0:1], scalar2=rv[:, cc:cc + 1],
                                                                  op0=ALU.subtract, op1=ALU.mult),
                     r=[vhT[i][cc], mvT, rvT], w=[vhT[i][cc]])

        def stageA_u(T):
            i = T % 2
            for m in range(KC):
                bU = newbank()
                for k in range(KC):
                    S.op("pe", lambda e, bU=bU, k=k, m=m: e.matmul(psf[bU][:, :], lhsT=Wu[:, k, m * 128:(m + 1) * 128], rhs=hT4[:, T, k, :],
                                                                  start=(k == 0), stop=(k == KC - 1)),
                         r=[WuT[k // 2], hTT[T]], w=[pst[bU]])
                S.op("act", lambda e, bU=bU, m=m, i=i: e.activation(out=ub[i][:, m, :], in_=psf[bU][:, :], func=AF.Gelu_apprx_tanh,
                                                                   bias=vec[:, V_BU + m:V_BU + m + 1], scale=1.0),
                     r=[pst[bU], vecT], w=[ubT[i][m]])

        def stageA_sgu(T):
            i = T % 2
            for g in range(8):
                bS = newbank()
                for cc in range(4):
                    S.op("pe", lambda e, bS=bS, g=g, cc=cc, i=i: e.matmul(psf[bS][:, cc * 128:(cc + 1) * 128], lhsT=vh[i][:, cc, g * 128:(g + 1) * 128],
                                                                         rhs=WsT[:, g, :], start=True, stop=True),
                         r=[vhT[i][cc], WsTT], w=[pst[bS]])
                si = rr2[0]
                rr2[0] = (si + 1) % 2
                S.op("dve", lambda e, bS=bS, g=g, si=si: e.scalar_tensor_tensor(out=stmp[si][:, :].rearrange("p (c t) -> p c t", c=4),
                                                                               in0=psf[bS][:, :].rearrange("p (c t) -> p c t", c=4),
                                                                               scalar=vec[:, V_SLG + g:V_SLG + g + 1],
                                                                               in1=Bg[:, None, g * 128:(g + 1) * 128].to_broadcast([128, 4, 128]),
                                                                               op0=ALU.mult, op1=ALU.add),
                     r=[pst[bS], vecT, BgT], w=[stmpT[si]])
                S.op("dve", lambda e, g=g, si=si, i=i: e.tensor_tensor(out=ub[i][:, g, :], in0=ub[i][:, g, :], in1=stmp[si][:, :], op=ALU.mult),
                     r=[ubT[i][g], stmpT[si]], w=[ubT[i][g]])

        def stageA_proj(T):
            i = T % 2
            for m in range(KC):
                bG = newbank()
                for k in range(KC):
                    S.op("pe", lambda e, bG=bG, k=k, m=m: e.matmul(psf[bG][:, :], lhsT=Wg0[:, k, m * 128:(m + 1) * 128], rhs=hT4[:, T, k, :],
                                                                  start=(k == 0), stop=(k == KC - 1)),
                         r=[Wg0T[k // 2], hTT[T]], w=[pst[bG]])
                bY = newbank()
                for j in range(KC):
                    S.op("pe", lambda e, bY=bY, j=j, m=m, i=i: e.matmul(psf[bY][:, :], lhsT=Wpa[:, j, m * 128:(m + 1) * 128], rhs=ub[i][:, j, :],
                                                                       start=(j == 0), stop=(j == KC - 1)),
                         r=[WpaT[j // 2], ubT[i][j]], w=[pst[bY]])
                si = rr2[1]
                rr2[1] = (si + 1) % 2
                S.op("act", lambda e, bG=bG, si=si, m=m: e.activation(out=sg0[si][:, :], in_=psf[bG][:, :], func=AF.Sigmoid,
                                                                     bias=vec[:, V_BG0 + m:V_BG0 + m + 1], scale=1.0),
                     r=[pst[bG], vecT], w=[sg0T[si]])
                S.op("dve", lambda e, bY=bY, si=si: e.tensor_tensor(out=t1[si][:, :], in0=psf[bY][:, :], in1=sg0[si][:, :], op=ALU.mult),
                     r=[pst[bY], sg0T[si]], w=[t1T[si]])
                S.op("dve", lambda e, si=si, m=m: e.tensor_tensor(out=M2[:, T, m, :], in0=M2[:, T, m, :], in1=t1[si][:, :], op=ALU.add),
                     r=[M2T[T], t1T[si]], w=[M2T[T]])

        stageA_v(0)
        stageA_u(0)
        for T in range(NT):
            stageA_sgu(T)
            if T == 0 and dbg:
                dump("d_vh", vh[0][:, :, :], vhT[0])
            if T + 1 < NT:
                stageA_v(T + 1)
                if T + 1 == NT - 1:
                    load_next_w()
                stageA_u(T + 1)
                if T + 1 == NT - 1:
                    load_next_w()
            stageA_proj(T)
            if T == 0 and dbg:
                dump("d_su", ub[0][:, :, :], ubT[0])
        dump("d_mg", M2[:, :, :, :], M2T)
        load_next_w()
        load_next_w()

        X = P23.view(0, F32, [NCH, D])
        XT = [P23.tk(c * 4096, 4096) for c in range(NCH)]
        h2T4 = P1.view(0, BF16, [NT, KC, 512])
        h2TT = M2T
        Wout, WoutT = W("out")
        for c in range(NCH):
            S.dma("sp", lambda e, c=c: e.dma_start(out=X[:, c, :], in_=xv[:, c, :]), w=[XT[c]])

        def stageW(T):
            for cc in range(4):
                c = T * 4 + cc
                for hf in range(2):
                    bO = newbank()
                    for m in range(KC):
                        S.op("pe", lambda e, bO=bO, m=m, cc=cc, hf=hf: e.matmul(psf[bO][:, :], lhsT=M2[:, T, m, cc * 128:(cc + 1) * 128],
                                                                               rhs=Wout[:, m, hf * 512:(hf + 1) * 512], start=(m == 0), stop=(m == KC - 1)),
                             r=[M2T[T], WoutT[m // 2]], w=[pst[bO]])
                    S.op("dve", lambda e, bO=bO, c=c, hf=hf: e.tensor_tensor(out=X[:, c, hf * 512:(hf + 1) * 512], in0=psf[bO][:, :],
                                                                            in1=X[:, c, hf * 512:(hf + 1) * 512], op=ALU.add),
                         r=[pst[bO], XT[c]], w=[XT[c]])

        def norm2(T, xn_lo=0):
            norm_to_hT(T, lambda cc, T=T: X[:, T * 4 + cc, :], lambda cc, T=T: XT[T * 4 + cc], gb2, gb2T, h2T4, h2TT, ss2, ss2T, rs1, rs1T, sd1, sd1T,
                       xn_lo=xn_lo)

        stageW(0)
        stageW(1)
        norm2(0)
        stageW(2)
        norm2(1)
        stageW(3)
        load_next_w()
        norm2(2)
        if dbg:
            dump("d_x1", X[:, :, :], XT)

        fb = [P1.view(32768 + i * 8192, BF16, [KC, 512]) for i in range(2)]
        fbT = [[P1.tk(32768 + i * 8192 + j * 1024, 1024) for j in range(KC)] for i in range(2)]
        fb_rr = [0]

        def ffn1(q, T, fi):
            W1, W1T = W("f1_%d" % q)
            for j in range(KC):
                bF = newbank()
                for k in range(KC):
                    S.op("pe", lambda e, bF=bF, k=k, j=j, W1=W1: e.matmul(psf[bF][:, :], lhsT=W1[:, k, j * 128:(j + 1) * 128], rhs=h2T4[:, T, k, :],
                                                                         start=(k == 0), stop=(k == KC - 1)),
                         r=[W1T[k // 2], h2TT[T]], w=[pst[bF]])
                S.op("act", lambda e, bF=bF, j=j, fi=fi: e.activation(out=fb[fi][:, j, :], in_=psf[bF][:, :], func=AF.Relu), r=[pst[bF]], w=[fbT[fi][j]])
                S.op("dve", lambda e, j=j, fi=fi: e.tensor_tensor(out=fb[fi][:, j, :], in0=fb[fi][:, j, :], in1=fb[fi][:, j, :], op=ALU.mult),
                     r=[fbT[fi][j]], w=[fbT[fi][j]])

        def ffn2(q, T, fi):
            W2, W2T = W("f2_%d" % q)
            for cc in range(4):
                c = T * 4 + cc
                for hf in range(2):
                    bO = newbank()
                    for j in range(KC):
                        S.op("pe", lambda e, bO=bO, j=j, cc=cc, hf=hf, W2=W2: e.matmul(psf[bO][:, :], lhsT=fb[fi][:, j, cc * 128:(cc + 1) * 128],
                                                                                      rhs=W2[:, j, hf * 512:(hf + 1) * 512], start=(j == 0), stop=(j == KC - 1)),
                             r=[fbT[fi][j], W2T[j // 2]], w=[pst[bO]])
                    S.op("dve", lambda e, bO=bO, c=c, hf=hf: e.tensor_tensor(out=X[:, c, hf * 512:(hf + 1) * 512], in0=psf[bO][:, :],
                                                                            in1=X[:, c, hf * 512:(hf + 1) * 512], op=ALU.add),
                         r=[pst[bO], XT[c]], w=[XT[c]])

        def final_norm(T):
            for cc in range(4):
                c = T * 4 + cc
                S.op("act", lambda e, c=c: e.activation(out=junk[:, :], in_=X[:, c, :], func=AF.Square, accum_out=ss3[:, c:c + 1]),
                     r=[XT[c]], w=[junkT, ss3T])
            S.op("dve", lambda e: e.tensor_scalar(out=sd1[:, T * 4:T * 4 + 4], in0=ss3[:, T * 4:T * 4 + 4], scalar1=1.0 / D, scalar2=EPS,
                                                  op0=ALU.mult, op1=ALU.add), r=[ss3T], w=[sd1T])
            S.op("act", lambda e: e.activation(out=sd1[:, T * 4:T * 4 + 4], in_=sd1[:, T * 4:T * 4 + 4], func=AF.Sqrt), r=[sd1T], w=[sd1T])
            S.op("dve", lambda e: e.reciprocal(out=rs1[:, T * 4:T * 4 + 4], in_=sd1[:, T * 4:T * 4 + 4]), r=[sd1T], w=[rs1T])
            for cc in range(4):
                c = T * 4 + cc
                S.op("dve", lambda e, c=c: e.scalar_tensor_tensor(out=X[:, c, :], in0=X[:, c, :], scalar=rs1[:, c:c + 1], in1=gbf[:, :],
                                                                  op0=ALU.mult, op1=ALU.mult),
                     r=[XT[c], rs1T, gbfT], w=[XT[c]])
                out_toks.append(S.dma("sp", lambda e, c=c: e.dma_start(out=out[c * 128:(c + 1) * 128, :], in_=X[:, c, :]), r=[XT[c]]))

        seq = [(q, T) for q in range(4) for T in range(NT)]
        fis = []
        for n, (q, T) in enumerate(seq):
            fis.append(n % 2)
        ffn1(seq[0][0], seq[0][1], fis[0])
        norm2(3, xn_lo=4)
        dump("d_h2", h2T4[:, :, :, :], h2TT)
        for n, (q, T) in enumerate(seq):
            if n + 1 < len(seq):
                ffn1(seq[n + 1][0], seq[n + 1][1], fis[n + 1])
            if n + 1 < len(seq) and seq[n + 1][1] == NT - 1 and seq[n + 1][0] < 2:
                load_next_w()
            ffn2(q, T, fis[n])
            if T == NT - 1 and q < 2:
                load_next_w()
            if q == 3:
                final_norm(T)

        S.wait("sp", out_toks)
        S.run()
    return nc


_W_NAMES = ["norm1_g", "w_in", "b_in", "sgu_ln_g", "sgu_ln_b", "sgu_w", "sgu_b", "w_proj_a", "conv_w", "conv_b",
            "conv_ln_g", "conv_ln_b", "w_proj_b", "b_proj_b", "w_out", "norm2_g", "w_ff1", "w_ff2", "norm_f_g"]


def _prep(inputs):
    d = {}
    for nm in _W_NAMES:
        a = np.asarray(inputs[nm], dtype=np.float32)
        if nm != "norm_f_g":
            a = a[0]
        if nm == "sgu_b":
            a = a.reshape(-1)
        d[nm] = np.ascontiguousarray(a)
    return d


def kernel(**inputs):
    x = np.asarray(inputs["x"], dtype=np.float32)
    B = x.shape[0]
    wd = _prep(inputs)
    nc = build_nc()
    in_maps = []
    for b in range(B):
        m = dict(wd)
        m["x"] = np.ascontiguousarray(x[b])
        in_maps.append(m)
    res = run_bass_kernel_spmd(nc, in_maps, core_ids=list(range(B)))
    return np.stack([np.asarray(r["out"], dtype=np.float32) for r in res.results], axis=0)
```

```python
import contextlib
import numpy as np
import concourse.bass as bass
import concourse.mybir as mybir
from concourse.bass_utils import run_bass_kernel_spmd

F32 = mybir.dt.float32
BF16 = mybir.dt.bfloat16
AF = mybir.ActivationFunctionType
ALU = mybir.AluOpType
DSZ = {F32: 4, BF16: 2}

SEQ = 2048
D = 1024
NT = 4
NCH = 16
KC = 8
CK = 31
EPS = 1e-6


class Tk:
    __slots__ = ("sp", "lo", "hi", "lw", "rd", "ov")

    def __init__(self, sp, lo, hi):
        self.sp, self.lo, self.hi = sp, lo, hi
        self.lw = []
        self.rd = []
        self.ov = None


class Sched:
    ENG = ("pe", "dve", "act", "pool", "sp")

    def __init__(self, nc, n_dma_sems=40):
        self.nc = nc
        self.sem = {e: nc.alloc_semaphore(name="es_" + e) for e in self.ENG}
        self.cnt = {e: 0 for e in self.ENG}
        self.prog = {e: [] for e in self.ENG}
        self.seen = {e: {} for e in self.ENG}
        self.pending = {e: None for e in self.ENG}
        self.dsem = [nc.alloc_semaphore(name="ds_%d" % i) for i in range(n_dma_sems)]
        self.dval = [0] * n_dma_sems
        self.drr = 0
        self.tiles = {}

    def tile(self, sp, lo, hi):
        t = Tk(sp, lo, hi)
        lst = self.tiles.setdefault(sp, [])
        t.ov = [t]
        for o in lst:
            if o.lo < hi and lo < o.hi:
                t.ov.append(o)
                o.ov.append(t)
        lst.append(t)
        return t

    def _need(self, eng, tok):
        if tok[0] == "e":
            _, p, c = tok
            if c > self.cnt[p]:
                ent = self.pending[p]
                assert ent is not None and c == self.cnt[p] + 1, (p, c, self.cnt[p])
                ent["inc"] = True
                self.cnt[p] += 1
                self.pending[p] = None
            key = ("e", p)
            sem = self.sem[p]
            val = c
        else:
            _, i, val = tok
            key = ("d", i)
            sem = self.dsem[i]
        if self.seen[eng].get(key, 0) >= val:
            return None
        self.seen[eng][key] = val
        return (sem, val)

    def _deps(self, eng, r, w):
        toks = []
        for t in r:
            for o in t.ov:
                toks.extend(o.lw)
        same = []
        if eng != "pe":
            same = [tok for tok in toks if tok[0] == "e" and tok[1] == eng]
        for t in w:
            for o in t.ov:
                toks.extend(o.lw)
                toks.extend(o.rd)
        waits = []
        for tok in toks:
            if tok[0] == "e" and tok[1] == eng:
                continue
            wt = self._need(eng, tok)
            if wt is not None:
                waits.append(wt)
        for tok in same:
            wt = self._need(eng, tok)
            if wt is not None:
                waits.append(wt)
        return waits

    def _mark(self, tok, r, w):
        for t in r:
            if not t.rd or t.rd[-1] != tok:
                t.rd.append(tok)
        for t in w:
            for o in t.ov:
                o.lw = [tok]
                o.rd = []

    def op(self, eng, fn, r=(), w=()):
        waits = self._deps(eng, r, w)
        ent = {"fn": fn, "inc": False, "waits": waits, "dma": None}
        self.prog[eng].append(ent)
        self.pending[eng] = ent
        tok = ("e", eng, self.cnt[eng] + 1)
        self._mark(tok, r, w)
        return tok

    def dma(self, eng, fn, r=(), w=()):
        waits = self._deps(eng, r, w)
        i = self.drr
        self.drr = (self.drr + 1) % len(self.dsem)
        if self.dval[i] > 0:
            wt = self._need(eng, ("d", i, self.dval[i]))
            if wt is not None:
                waits.append(wt)
        self.dval[i] += 16
        tok = ("d", i, self.dval[i])
        self.prog[eng].append({"fn": fn, "inc": False, "waits": waits, "dma": i})
        self._mark(tok, r, w)
        return tok

    def wait(self, eng, toks):
        waits = []
        for tok in toks:
            wt = self._need(eng, tok)
            if wt is not None:
                waits.append(wt)
        if waits:
            self.prog[eng].append({"fn": None, "inc": False, "waits": waits, "dma": None})

    def replay(self, eng, e):
        sem = self.sem[eng]
        for ent in self.prog[eng]:
            for (s, v) in ent["waits"]:
                e.wait_ge(s, v)
            if ent["fn"] is None:
                continue
            ins = ent["fn"](e)
            if ent["dma"] is not None:
                ins.then_inc(self.dsem[ent["dma"]], 16)
            elif ent["inc"]:
                ins.then_inc(sem, 1)

    def run(self):
        with self.nc.Block() as block:
            @block.tensor
            def _(e):
                self.replay("pe", e)

            @block.vector
            def _(e):
                self.replay("dve", e)

            @block.scalar
            def _(e):
                self.replay("act", e)

            @block.gpsimd
            def _(e):
                self.replay("pool", e)

            @block.sync
            def _(e):
                self.replay("sp", e)


class Region:
    def __init__(self, S, stack, name, nbytes):
        self.S = S
        self.name = name
        self.nbytes = nbytes
        self.t = stack.enter_context(S.nc.sbuf_tensor(name, [128, nbytes // 4], F32))

    def view(self, lo, dtype, shape):
        n = 1
        for s in shape:
            n *= s
        nb = n * DSZ[dtype]
        assert lo % 4 == 0 and nb % 4 == 0 and lo + nb <= self.nbytes, (self.name, lo, nb)
        ap = self.t[:, lo // 4:(lo + nb) // 4]
        if dtype != F32:
            ap = ap.bitcast(dtype)
        if len(shape) == 2:
            ap = ap.rearrange("p (a b) -> p a b", a=shape[0])
        elif len(shape) == 3:
            ap = ap.rearrange("p (a b c) -> p a b c", a=shape[0], b=shape[1])
        return ap

    def tk(self, lo, nb):
        assert lo + nb <= self.nbytes
        return self.S.tile("sb:" + self.name, lo, lo + nb)


def build_nc(dbg=False):
    nc = bass.Bass("TRN2", target_bir_lowering=False)

    def din(name, shape):
        return nc.dram_tensor(name, shape, F32, kind="ExternalInput").ap()

    x = din("x", [SEQ, D])
    norm1_g = din("norm1_g", [D])
    w_in = din("w_in", [D, 6 * D])
    b_in = din("b_in", [6 * D])
    sgu_ln_g = din("sgu_ln_g", [D])
    sgu_ln_b = din("sgu_ln_b", [D])
    sgu_w = din("sgu_w", [8, 128, 128])
    sgu_b = din("sgu_b", [D])
    w_proj_a = din("w_proj_a", [D, D])
    conv_w = din("conv_w", [CK, D])
    conv_b = din("conv_b", [D])
    conv_ln_g = din("conv_ln_g", [D])
    conv_ln_b = din("conv_ln_b", [D])
    w_proj_b = din("w_proj_b", [D, D])
    b_proj_b = din("b_proj_b", [D])
    w_out = din("w_out", [D, D])
    norm2_g = din("norm2_g", [D])
    w_ff1 = din("w_ff1", [D, 4 * D])
    w_ff2 = din("w_ff2", [4 * D, D])
    norm_f_g = din("norm_f_g", [D])
    out = nc.dram_tensor("out", [SEQ, D], F32, kind="ExternalOutput").ap()
    dbg_t = {}
    if dbg:
        for nm, shp, dt_ in (("d_hT", [128, NT * KC * 512], BF16), ("d_cs", [128, KC * SEQ], BF16),
                             ("d_m2", [128, NT * KC * 512], BF16), ("d_mg", [128, NT * KC * 512], BF16),
                             ("d_x1", [128, NCH * D], F32), ("d_h2", [128, NT * KC * 512], BF16),
                             ("d_vec", [128, 128], F32), ("d_cw", [128, KC * CK], F32),
                             ("d_bg", [128, D], F32), ("d_wst", [128, D], BF16),
                             ("d_vh", [128, 4 * D], BF16), ("d_su", [128, KC * 512], BF16)):
            dbg_t[nm] = nc.dram_tensor(nm, shp, dt_, kind="ExternalOutput").ap()

    S = Sched(nc)
    with contextlib.ExitStack() as st:
        RING = Region(S, st, "RING", 65536)
        P1 = Region(S, st, "P1", 49152)
        P23 = Region(S, st, "P23", 65536)
        CST = Region(S, st, "CST", 28672)
        psf = [st.enter_context(nc.psum_tensor("ps%d" % i, [128, 512], F32)) for i in range(8)]
        psb = [p[:, :].bitcast(BF16) for p in psf]
        pst = [S.tile("ps", i, i + 1) for i in range(8)]
        bank_rr = [0]

        def newbank():
            i = bank_rr[0]
            bank_rr[0] = (i + 1) % 8
            return i

        out_toks = []

        def dump(nm, ap, tks):
            if dbg:
                out_toks.append(S.dma("sp", lambda e: e.dma_start(out=dbg_t[nm][:, :], in_=ap), r=tks))

        coff = [0]

        def calloc(dtype, shape):
            n = int(np.prod(shape)) * DSZ[dtype]
            n = (n + 31) // 32 * 32
            v = CST.view(coff[0], dtype, shape)
            t = CST.tk(coff[0], n)
            coff[0] += n
            return v, t

        ident, identT = calloc(BF16, [128])
        identf, identfT = calloc(F32, [128])
        vec, vecT = calloc(F32, [128])
        convw, convwT = calloc(F32, [KC, 32])
        gb1, gb1T = calloc(F32, [D])
        gb2, gb2T = calloc(F32, [D])
        gbf, gbfT = calloc(F32, [D])
        Bg, BgT = calloc(F32, [D])
        WsT, WsTT = calloc(BF16, [8, 128])
        bvrow, bvrowT = calloc(BF16, [D])
        onesrow, onesrowT = calloc(BF16, [128])
        onesm, onesmT = calloc(BF16, [128])
        epsc, epscT = calloc(F32, [8])
        ss1, ss1T = calloc(F32, [NCH])
        ss2, ss2T = calloc(F32, [NCH])
        ss3, ss3T = calloc(F32, [NCH])
        rs1, rs1T = calloc(F32, [NCH])
        sd1, sd1T = calloc(F32, [NCH])
        bst, bstT = calloc(F32, [4, 12])
        mv, mvT = calloc(F32, [4, 2])
        rv, rvT = calloc(F32, [4])
        sdv, sdvT = calloc(F32, [4])
        junk, junkT = calloc(F32, [D])
        V_BU, V_BV, V_BVAL, V_BGATE, V_BG0, V_BG1, V_N1, V_N2, V_SLG, V_CB, V_CLG, V_CLB, V_BPB = [8 * i for i in range(13)]

        S.op("pool", lambda e: e.memset(identf[:, :], 0.0), w=[identfT])
        S.op("pool", lambda e: e.affine_select(out=identf[:, :], in_=identf[:, :], pattern=[[-1, 128]],
                                               compare_op=ALU.not_equal, fill=1.0, base=0, channel_multiplier=1),
             r=[identfT], w=[identfT])
        S.op("dve", lambda e: e.tensor_copy(out=ident[:, :], in_=identf[:, :]), r=[identfT], w=[identT])
        S.op("pool", lambda e: e.memset(onesrow[0:1, :], 1.0), w=[onesrowT])
        S.op("pool", lambda e: e.memset(onesm[:, :], 1.0 / D), w=[onesmT])
        S.op("pool", lambda e: e.memset(epsc[:, :], EPS), w=[epscT])
        for s_, sT_ in ((ss1, ss1T), (ss2, ss2T), (ss3, ss3T)):
            S.op("pool", lambda e, s_=s_: e.memset(s_[:, :], 0.0), w=[sT_])
        slotv = [RING.view(s * 16384, BF16, [8, 1024]) for s in range(4)]
        slotT = [[RING.tk(s * 16384 + kp * 4096, 4096) for kp in range(4)] for s in range(4)]
        w_in_v = w_in.rearrange("(k p) e -> p k e", p=128)
        wsrc = {
            "u": w_in_v[:, :, 0 * D:1 * D], "v": w_in_v[:, :, 1 * D:2 * D], "val": w_in_v[:, :, 2 * D:3 * D],
            "gate": w_in_v[:, :, 3 * D:4 * D], "g0": w_in_v[:, :, 4 * D:5 * D], "g1": w_in_v[:, :, 5 * D:6 * D],
            "pa": w_proj_a.rearrange("(k p) e -> p k e", p=128), "pb": w_proj_b.rearrange("(k p) e -> p k e", p=128),
            "out": w_out.rearrange("(k p) e -> p k e", p=128),
        }
        w1v = w_ff1.rearrange("(k p) e -> p k e", p=128)
        w2v = w_ff2.rearrange("(j p) e -> p j e", p=128)
        for q in range(4):
            wsrc["f1_%d" % q] = w1v[:, :, q * D:(q + 1) * D]
            wsrc["f2_%d" % q] = w2v[:, q * 8:(q + 1) * 8, :]
        worder = ["val", "gate", "pb", "g1", "v", "u", "pa", "g0", "out",
                  "f1_0", "f2_0", "f1_1", "f2_1", "f1_2", "f2_2", "f1_3", "f2_3"]
        wslot = {nm: i % 4 for i, nm in enumerate(worder)}
        wnext = [0]

        def load_next_w():
            if wnext[0] >= len(worder):
                return
            nm = worder[wnext[0]]
            wnext[0] += 1
            s = wslot[nm]
            src = wsrc[nm]
            for kp in range(4):
                S.dma("pool", lambda e, s=s, kp=kp, src=src: e.dma_start(out=slotv[s][:, 2 * kp:2 * kp + 2, :], in_=src[:, 2 * kp:2 * kp + 2, :]),
                      w=[slotT[s][kp]])

        def W(nm):
            s = wslot[nm]
            return slotv[s], slotT[s]

        for _ in range(2):
            load_next_w()

        so = 32768
        VR = P23.view(so, F32, [128]); VRT = P23.tk(so, 512); so += 512
        CW = P23.view(so, F32, [D]); CWT = P23.tk(so, 4096); so += 4096
        Wtf = P23.view(so, F32, [8, 128]); WtfT = P23.tk(so, 4096); so += 4096
        Wtb = P23.view(so, BF16, [8, 128]); WtbT = P23.tk(so, 2048); so += 2048
        betaf = P23.view(so, F32, [D]); betafT = P23.tk(so, 4096); so += 4096
        betab = P23.view(so, BF16, [D]); betabT = P23.tk(so, 2048); so += 2048

        hT4 = P23.view(0, BF16, [NT, KC, 512])
        hTT = [P23.tk(T * 8192, 8192) for T in range(NT)]
        xt = [P1.view(i * 16384, F32, [4, D]) for i in range(2)]
        xtT = [[P1.tk(i * 16384 + cc * 4096, 4096) for cc in range(4)] for i in range(2)]
        xv = x.rearrange("(c p) d -> p c d", p=128)

        def load_x_tile(T):
            i = T % 2
            for cc in range(4):
                c = T * 4 + cc
                S.dma("sp", lambda e, i=i, cc=cc, c=c: e.dma_start(out=xt[i][:, cc, :], in_=xv[:, c, :]), w=[xtT[i][cc]])

        S.op("dve", lambda e: e.memset(VR[:, :], 0.0), w=[VRT])
        load_x_tile(0)
        S.dma("sp", lambda e: e.dma_start(out=gb1[:, :], in_=norm1_g.partition_broadcast(128)), w=[gb1T])
        load_x_tile(1)
        S.dma("pool", lambda e: e.dma_start(out=bvrow[0:1, :], in_=b_in[D:2 * D].rearrange("(o n) -> o n", o=1)), w=[bvrowT])
        S.dma("pool", lambda e: e.dma_start(out=CW[0:CK, :], in_=conv_w[:, :]), w=[CWT])
        S.dma("pool", lambda e: e.dma_start(out=VR[0:48, :], in_=b_in.rearrange("(r p) -> r p", p=128)), w=[VRT])
        for i, v_ in enumerate((norm1_g, norm2_g, sgu_ln_g, conv_b, conv_ln_g, conv_ln_b, b_proj_b)):
            S.dma("pool", lambda e, i=i, v_=v_: e.dma_start(out=VR[48 + 8 * i:56 + 8 * i, :], in_=v_.rearrange("(r p) -> r p", p=128)), w=[VRT])

        def late_setup_dmas():
            S.dma("sp", lambda e: e.dma_start(out=Wtf[:, :, :], in_=sgu_w.rearrange("g t s -> t g s")), w=[WtfT])
            S.dma("sp", lambda e: e.dma_start(out=Bg[:, :], in_=sgu_b.partition_broadcast(128)), w=[BgT])
            S.dma("sp", lambda e: e.dma_start(out=betaf[:, :], in_=sgu_ln_b.partition_broadcast(128)), w=[betafT])
            S.dma("sp", lambda e: e.dma_start(out=gb2[:, :], in_=norm2_g.partition_broadcast(128)), w=[gb2T])
            S.dma("sp", lambda e: e.dma_start(out=gbf[:, :], in_=norm_f_g.partition_broadcast(128)), w=[gbfT])


        xn = [P1.view(32768 + i * 2048, BF16, [D]) for i in range(8)]
        xnT = [P1.tk(32768 + i * 2048, 2048) for i in range(8)]
        xn_rr = [0]

        def norm_to_hT(T, src_ap_fn, src_tk_fn, gb, gbT, dst4, dstT, ss, ssT, rs, rsT, sd, sdT, xn_lo=0):
            for cc in range(4):
                c = T * 4 + cc
                S.op("act", lambda e, cc=cc, c=c: e.activation(out=junk[:, :], in_=src_ap_fn(cc), func=AF.Square, accum_out=ss[:, c:c + 1]),
                     r=[src_tk_fn(cc)], w=[junkT, ssT])
            S.op("dve", lambda e: e.tensor_scalar(out=sd[:, T * 4:T * 4 + 4], in0=ss[:, T * 4:T * 4 + 4], scalar1=1.0 / D, scalar2=EPS,
                                                  op0=ALU.mult, op1=ALU.add), r=[ssT], w=[sdT])
            S.op("act", lambda e: e.activation(out=sd[:, T * 4:T * 4 + 4], in_=sd[:, T * 4:T * 4 + 4], func=AF.Sqrt), r=[sdT], w=[sdT])
            S.op("dve", lambda e: e.reciprocal(out=rs[:, T * 4:T * 4 + 4], in_=sd[:, T * 4:T * 4 + 4]), r=[sdT], w=[rsT])
            for cc in range(4):
                c = T * 4 + cc
                xi = xn_lo + xn_rr[0] % (8 - xn_lo)
                xn_rr[0] += 1
                S.op("dve", lambda e, cc=cc, c=c, xi=xi: e.scalar_tensor_tensor(out=xn[xi][:, :], in0=src_ap_fn(cc), scalar=rs[:, c:c + 1],
                                                                               in1=gb[:, :], op0=ALU.mult, op1=ALU.mult),
                     r=[src_tk_fn(cc), rsT, gbT], w=[xnT[xi]])
                bk = newbank()
                for k in range(KC):
                    S.op("pe", lambda e, bk=bk, k=k, xi=xi: e.transpose(out=psb[bk][:, k * 128:(k + 1) * 128], in_=xn[xi][:, k * 128:(k + 1) * 128],
                                                                        identity=ident[:, :]), r=[xnT[xi], identT], w=[pst[bk]])
                eng = "act" if (cc % 2 == 0) else "dve"
                if eng == "act":
                    S.op("act", lambda e, bk=bk, cc=cc: e.activation(out=dst4[:, T, :, cc * 128:(cc + 1) * 128],
                                                                     in_=psb[bk][:, :].rearrange("p (k t) -> p k t", k=KC), func=AF.Copy),
                         r=[pst[bk]], w=[dstT[T]])
                else:
                    S.op("dve", lambda e, bk=bk, cc=cc: e.tensor_copy(out=dst4[:, T, :, cc * 128:(cc + 1) * 128],
                                                                      in_=psb[bk][:, :].rearrange("p (k t) -> p k t", k=KC)),
                         r=[pst[bk]], w=[dstT[T]])

        for T in range(NT):
            i = T % 2
            norm_to_hT(T, lambda cc, i=i: xt[i][:, cc, :], lambda cc, i=i: xtT[i][cc], gb1, gb1T, hT4, hTT, ss1, ss1T, rs1, rs1T, sd1, sd1T)
            if T + 2 < NT:
                load_x_tile(T + 2)
            if T == NT - 1:
                late_setup_dmas()
        bk = newbank()
        S.op("pe", lambda e, bk=bk: e.transpose(out=psf[bk][:, 0:104], in_=VR[0:104, :], identity=identf[0:104, 0:104]),
             r=[VRT, identfT], w=[pst[bk]])
        S.op("dve", lambda e, bk=bk: e.tensor_copy(out=vec[:, 0:104], in_=psf[bk][:, 0:104]), r=[pst[bk]], w=[vecT])
        bk = newbank()
        for j in range(KC):
            S.op("pe", lambda e, bk=bk, j=j: e.transpose(out=psf[bk][:, j * 32:j * 32 + CK], in_=CW[0:CK, j * 128:(j + 1) * 128],
                                                         identity=identf[0:CK, 0:CK]), r=[CWT, identfT], w=[pst[bk]])
        S.op("dve", lambda e, bk=bk: e.tensor_copy(out=convw[:, :, 0:CK],
                                                   in_=psf[bk][:, 0:256].rearrange("p (j k) -> p j k", k=32)[:, :, 0:CK]),
             r=[pst[bk]], w=[convwT])
        dump("d_hT", hT4[:, :, :, :], hTT)
        dump("d_vec", vec[:, :], [vecT])
        dump("d_cw", convw[:, :, 0:CK], [convwT])
        S.op("pool", lambda e: e.affine_select(out=Wtf[:, :, :], in_=Wtf[:, :, :], pattern=[[0, 8], [-1, 128]],
                                               compare_op=ALU.is_ge, fill=0.0, base=0, channel_multiplier=1),
             r=[WtfT], w=[WtfT])
        S.op("dve", lambda e: e.tensor_copy(out=Wtb[:, :, :], in_=Wtf[:, :, :]), r=[WtfT], w=[WtbT])
        bk = newbank()
        for g in range(8):
            S.op("pe", lambda e, bk=bk, g=g: e.transpose(out=psb[bk][:, g * 128:(g + 1) * 128], in_=Wtb[:, g, :], identity=ident[:, :]),
                 r=[WtbT, identT], w=[pst[bk]])
        S.op("dve", lambda e, bk=bk: e.tensor_copy(out=WsT[:, :, :], in_=psb[bk][:, :].rearrange("p (g t) -> p g t", g=8)),
             r=[pst[bk]], w=[WsTT])
        S.op("dve", lambda e: e.tensor_copy(out=betab[:, :], in_=betaf[:, :]), r=[betafT], w=[betabT])
        for hh in range(2):
            bk = newbank()
            for g4 in range(4):
                g = hh * 4 + g4
                S.op("pe", lambda e, bk=bk, g=g, g4=g4: e.matmul(psf[bk][:, g4 * 128:(g4 + 1) * 128], lhsT=betab[:, g * 128:(g + 1) * 128],
                                                                rhs=WsT[:, g, :], start=True, stop=True),
                     r=[betabT, WsTT], w=[pst[bk]])
            S.op("dve", lambda e, bk=bk, hh=hh: e.tensor_tensor(out=Bg[:, hh * 512:(hh + 1) * 512], in0=psf[bk][:, :],
                                                               in1=Bg[:, hh * 512:(hh + 1) * 512], op=ALU.add),
                 r=[pst[bk], BgT], w=[BgT])
        dump("d_bg", Bg[:, :], [BgT])
        dump("d_wst", WsT[:, :, :], [WsTT])

        CV = P23.view(32768, BF16, [KC, SEQ])
        CVT = [[P23.tk(32768 + j * 4096 + T * 1024, 1024) for T in range(NT)] for j in range(KC)]
        APAD = 2080
        abuf = [P1.view(i * 4160, BF16, [APAD]) for i in range(2)]
        abufT = [[P1.tk(i * 4160, 60)] + [P1.tk(i * 4160 + 60 + T * 1024, 1024) for T in range(NT)] for i in range(2)]
        dg = [P1.view(8320 + i * 7936, BF16, [CK, 128]) for i in range(2)]
        dgT = [P1.tk(8320 + i * 7936, 7936) for i in range(2)]
        sig = [P1.view(24192 + i * 1024, BF16, [512]) for i in range(2)]
        sigT = [P1.tk(24192 + i * 1024, 1024) for i in range(2)]
        M2 = P1.view(0, BF16, [NT, KC, 512])
        M2T = [P1.tk(T * 8192, 8192) for T in range(NT)]
        sqt = [P1.view(26240 + i * 1024, BF16, [512]) for i in range(8)]
        sqtT = [P1.tk(26240 + i * 1024, 1024) for i in range(8)]
        NYB = 3
        yb = [P1.view(34432 + i * 2048, F32, [512]) for i in range(NYB)]
        ybT = [P1.tk(34432 + i * 2048, 2048) for i in range(NYB)]
        stt = P1.view(40576, F32, [512]); sttT = P1.tk(40576, 2048)
        rstdb = P1.view(42624, F32, [512]); rstdbT = P1.tk(42624, 2048)
        nmrb = P1.view(44672, F32, [512]); nmrbT = P1.tk(44672, 2048)
        sg = [P1.view(46720 + i * 1024, BF16, [512]) for i in range(2)]
        sgT = [P1.tk(46720 + i * 1024, 1024) for i in range(2)]
        Wpb, WpbT = W("pb")
        Wg1, Wg1T = W("g1")

        def btail_sq(T):
            for j in range(KC):
                S.op("act", lambda e, j=j: e.activation(out=sqt[j][:, :], in_=CV[:, j, T * 512:(T + 1) * 512], func=AF.Square),
                     r=[CVT[j][T]], w=[sqtT[j]])

        def btail_stats(T):
            bM = newbank()
            bQ = newbank()
            for j in range(KC):
                qi = j
                S.op("pe", lambda e, j=j, bM=bM: e.matmul(psf[bM][:, :], lhsT=onesm[:, :], rhs=CV[:, j, T * 512:(T + 1) * 512], start=(j == 0), stop=(j == KC - 1)),
                     r=[onesmT, CVT[j][T]], w=[pst[bM]])
                S.op("pe", lambda e, j=j, bQ=bQ, qi=qi: e.matmul(psf[bQ][:, :], lhsT=onesm[:, :], rhs=sqt[qi][:, :], start=(j == 0), stop=(j == KC - 1)),
                     r=[onesmT, sqtT[qi]], w=[pst[bQ]])
            S.op("act", lambda e: e.activation(out=stt[:, :], in_=psf[bM][:, :], func=AF.Square), r=[pst[bM]], w=[sttT])
            S.op("dve", lambda e: e.tensor_tensor(out=stt[:, :], in0=psf[bQ][:, :], in1=stt[:, :], op=ALU.subtract), r=[pst[bQ], sttT], w=[sttT])
            S.op("act", lambda e: e.activation(out=stt[:, :], in_=stt[:, :], func=AF.Sqrt, bias=epsc[:, 0:1], scale=1.0), r=[sttT, epscT], w=[sttT])
            S.op("dve", lambda e: e.reciprocal(out=rstdb[:, :], in_=stt[:, :]), r=[sttT], w=[rstdbT])
            S.op("dve", lambda e: e.scalar_tensor_tensor(out=nmrb[:, :], in0=psf[bM][:, :], scalar=-1.0, in1=rstdb[:, :], op0=ALU.mult, op1=ALU.mult),
                 r=[pst[bM], rstdbT], w=[nmrbT])

        def btail_norm(T):
            for j in range(KC):
                yi = j % NYB
                S.op("dve", lambda e, j=j, yi=yi: e.tensor_tensor(out=yb[yi][:, :], in0=CV[:, j, T * 512:(T + 1) * 512], in1=rstdb[:, :], op=ALU.mult),
                     r=[CVT[j][T], rstdbT], w=[ybT[yi]])
                S.op("dve", lambda e, j=j, yi=yi: e.tensor_tensor(out=yb[yi][:, :], in0=yb[yi][:, :], in1=nmrb[:, :], op=ALU.add),
                     r=[ybT[yi], nmrbT], w=[ybT[yi]])
                S.op("act", lambda e, j=j, yi=yi: e.activation(out=CV[:, j, T * 512:(T + 1) * 512], in_=yb[yi][:, :], func=AF.Silu,
                                                               bias=vec[:, V_CLB + j:V_CLB + j + 1], scale=vec[:, V_CLG + j:V_CLG + j + 1]),
                     r=[ybT[yi], vecT], w=[CVT[j][T]])

        sg_rr = [0]

        def btail_proj(T):
            for m in range(KC):
                bG = newbank()
                for k in range(KC):
                    S.op("pe", lambda e, bG=bG, k=k, m=m: e.matmul(psf[bG][:, :], lhsT=Wg1[:, k, m * 128:(m + 1) * 128], rhs=hT4[:, T, k, :],
                                                                  start=(k == 0), stop=(k == KC - 1)),
                         r=[Wg1T[k // 2], hTT[T]], w=[pst[bG]])
                bY = newbank()
                for j in range(KC):
                    S.op("pe", lambda e, bY=bY, j=j, m=m: e.matmul(psf[bY][:, :], lhsT=Wpb[:, j, m * 128:(m + 1) * 128], rhs=CV[:, j, T * 512:(T + 1) * 512],
                                                                  start=(j == 0), stop=(j == KC - 1)),
                         r=[WpbT[j // 2], CVT[j][T]], w=[pst[bY]])
                si = sg_rr[0]
                sg_rr[0] = (si + 1) % 2
                S.op("act", lambda e, bG=bG, si=si, m=m: e.activation(out=sg[si][:, :], in_=psf[bG][:, :], func=AF.Sigmoid,
                                                                     bias=vec[:, V_BG1 + m:V_BG1 + m + 1], scale=1.0),
                     r=[pst[bG], vecT], w=[sgT[si]])
                S.op("dve", lambda e, bY=bY, si=si, m=m: e.scalar_tensor_tensor(out=M2[:, T, m, :], in0=psf[bY][:, :], scalar=vec[:, V_BPB + m:V_BPB + m + 1],
                                                                               in1=sg[si][:, :], op0=ALU.add, op1=ALU.mult),
                     r=[pst[bY], vecT, sgT[si]], w=[M2T[T]])

        Wval, WvalT = W("val")
        Wgate, WgateT = W("gate")
        for i in range(2):
            S.op("pool", lambda e, i=i: e.memset(abuf[i][:, 0:30], 0.0), w=[abufT[i][0]])
        sig_rr = [0]
        for j in range(KC):
            ab = abuf[j % 2]
            abT = abufT[j % 2]
            dgi = dg[j % 2]
            dgiT = dgT[j % 2]
            S.op("dve", lambda e, dgi=dgi, j=j: e.tensor_tensor(out=dgi[:, :, :], in0=ident[:, None, :].to_broadcast([128, CK, 128]),
                                                                in1=convw[:, j, 0:CK, None].to_broadcast([128, CK, 128]), op=ALU.mult),
                 r=[identT, convwT], w=[dgiT])
            pend = None
            for T in range(NT + 1):
                if T < NT:
                    bB = newbank()
                    for k in range(KC):
                        S.op("pe", lambda e, bB=bB, k=k, j=j, T=T: e.matmul(psf[bB][:, :], lhsT=Wgate[:, k, j * 128:(j + 1) * 128], rhs=hT4[:, T, k, :],
                                                                           start=(k == 0), stop=(k == KC - 1)),
                             r=[WgateT[k // 2], hTT[T]], w=[pst[bB]])
                    si = sig_rr[0]
                    sig_rr[0] = (si + 1) % 2
                    S.op("act", lambda e, bB=bB, si=si, j=j: e.activation(out=sig[si][:, :], in_=psf[bB][:, :], func=AF.Sigmoid,
                                                                         bias=vec[:, V_BGATE + j:V_BGATE + j + 1], scale=1.0),
                         r=[pst[bB], vecT], w=[sigT[si]])
                    bA = newbank()
                    for k in range(KC):
                        S.op("pe", lambda e, bA=bA, k=k, j=j, T=T: e.matmul(psf[bA][:, :], lhsT=Wval[:, k, j * 128:(j + 1) * 128], rhs=hT4[:, T, k, :],
                                                                           start=(k == 0), stop=(k == KC - 1)),
                             r=[WvalT[k // 2], hTT[T]], w=[pst[bA]])
                    S.op("dve", lambda e, bA=bA, si=si, j=j, T=T, ab=ab: e.scalar_tensor_tensor(out=ab[:, 30 + T * 512:30 + (T + 1) * 512], in0=psf[bA][:, :],
                                                                                                  scalar=vec[:, V_BVAL + j:V_BVAL + j + 1], in1=sig[si][:, :],
                                                                                                  op0=ALU.add, op1=ALU.mult),
                         r=[pst[bA], vecT, sigT[si]], w=[abT[T + 1]])
                if pend is not None:
                    Tp = pend
                    bC = newbank()
                    for k in range(CK):
                        S.op("pe", lambda e, bC=bC, k=k, Tp=Tp, ab=ab, dgi=dgi: e.matmul(psf[bC][:, :], lhsT=dgi[:, k, :], rhs=ab[:, Tp * 512 + k:Tp * 512 + k + 512],
                                                                                        start=(k == 0), stop=(k == CK - 1)),
                             r=[dgiT, abT[Tp], abT[Tp + 1]], w=[pst[bC]])
                    S.op("act", lambda e, bC=bC, Tp=Tp, j=j: e.activation(out=CV[:, j, Tp * 512:(Tp + 1) * 512], in_=psf[bC][:, :], func=AF.Identity,
                                                                         bias=vec[:, V_CB + j:V_CB + j + 1], scale=1.0),
                         r=[pst[bC], vecT], w=[CVT[j][Tp]])
                    if j == KC - 1:
                        if Tp >= 1:
                            btail_stats(Tp - 1)
                            btail_norm(Tp - 1)
                        btail_sq(Tp)
                pend = T if T < NT else None
            if j == 1:
                load_next_w()
                load_next_w()
        load_next_w()
        load_next_w()

        btail_proj(0)
        btail_stats(NT - 1)
        btail_norm(NT - 1)
        for T in range(1, NT):
            btail_proj(T)
        dump("d_cs", CV[:, :, :], [t for row in CVT for t in row])
        dump("d_m2", M2[:, :, :, :], M2T)
        load_next_w()
        load_next_w()

        vh = [P23.view(32768 + i * 8192, BF16, [4, D]) for i in range(2)]
        vhT = [[P23.tk(32768 + i * 8192 + cc * 2048, 2048) for cc in range(4)] for i in range(2)]
        ub = [P23.view(49152 + i * 8192, BF16, [KC, 512]) for i in range(2)]
        ubT = [[P23.tk(49152 + i * 8192 + m * 1024, 1024) for m in range(KC)] for i in range(2)]
        stmp = [P1.view(32768 + i * 1024, BF16, [512]) for i in range(2)]
        stmpT = [P1.tk(32768 + i * 1024, 1024) for i in range(2)]
        sg0 = [P1.view(34816 + i * 1024, BF16, [512]) for i in range(2)]
        sg0T = [P1.tk(34816 + i * 1024, 1024) for i in range(2)]
        t1 = [P1.view(36864 + i * 1024, BF16, [512]) for i in range(2)]
        t1T = [P1.tk(36864 + i * 1024, 1024) for i in range(2)]
        Wv, WvT = W("v")
        Wu, WuT = W("u")
        Wpa, WpaT = W("pa")
        Wg0, Wg0T = W("g0")
        rr2 = [0, 0, 0]

        def stageA_v(T):
            i = T % 2
            for cc in range(4):
                bV = [newbank(), newbank()]
                for hf in range(2):
                    for k in range(KC):
                        S.op("pe", lambda e, b=bV[hf], k=k, cc=cc, hf=hf: e.matmul(psf[b][:, :], lhsT=hT4[:, T, k, cc * 128:(cc + 1) * 128],
                                                                                  rhs=Wv[:, k, hf * 512:(hf + 1) * 512], start=(k == 0), stop=False),
                             r=[hTT[T], WvT[k // 2]], w=[pst[bV[hf]]])
                    S.op("pe", lambda e, b=bV[hf], hf=hf: e.matmul(psf[b][:, :], lhsT=onesrow[0:1, :], rhs=bvrow[0:1, hf * 512:(hf + 1) * 512], start=False, stop=True),
                         r=[onesrowT, bvrowT], w=[pst[bV[hf]]])
                    S.op("act", lambda e, b=bV[hf], hf=hf, cc=cc, i=i: e.activation(out=vh[i][:, cc, hf * 512:(hf + 1) * 512], in_=psf[b][:, :], func=AF.Gelu_apprx_tanh),
                         r=[pst[bV[hf]]], w=[vhT[i][cc]])
                    S.op("dve", lambda e, hf=hf, cc=cc, i=i: e.bn_stats(out=bst[:, cc, hf * 6:(hf + 1) * 6], in_=vh[i][:, cc, hf * 512:(hf + 1) * 512]),
                         r=[vhT[i][cc]], w=[bstT])
                S.op("dve", lambda e, cc=cc: e.bn_aggr(out=mv[:, cc, :], in_=bst[:, cc, :]), r=[bstT], w=[mvT])
            S.op("dve", lambda e: e.tensor_scalar(out=sdv[:, :], in0=mv[:, :, 1], scalar1=EPS, scalar2=None, op0=ALU.add), r=[mvT], w=[sdvT])
            S.op("act", lambda e: e.activation(out=sdv[:, :], in_=sdv[:, :], func=AF.Sqrt), r=[sdvT], w=[sdvT])
            S.op("dve", lambda e: e.reciprocal(out=rv[:, :], in_=sdv[:, :]), r=[sdvT], w=[rvT])
            for cc in range(4):
                S.op("dve", lambda e, cc=cc, i=i: e.tensor_scalar(out=vh[i][:, cc, :], in0=vh[i][:, cc, :], scalar1=mv[:, cc, 0:1], scalar2=rv[:, cc:cc + 1],
                                                                  op0=ALU.subtract, op1=ALU.mult),
                     r=[vhT[i][cc], mvT, rvT], w=[vhT[i][cc]])

        def stageA_u(T):
            i = T % 2
            for m in range(KC):
                bU = newbank()
                for k in range(KC):
                    S.op("pe", lambda e, bU=bU, k=k, m=m: e.matmul(psf[bU][:, :], lhsT=Wu[:, k, m * 128:(m + 1) * 128], rhs=hT4[:, T, k, :],
                                                                  start=(k == 0), stop=(k == KC - 1)),
                         r=[WuT[k // 2], hTT[T]], w=[pst[bU]])
                S.op("act", lambda e, bU=bU, m=m, i=i: e.activation(out=ub[i][:, m, :], in_=psf[bU][:, :], func=AF.Gelu_apprx_tanh,
                                                                   bias=vec[:, V_BU + m:V_BU + m + 1], scale=1.0),
                     r=[pst[bU], vecT], w=[ubT[i][m]])

        def stageA_sgu(T):
            i = T % 2
            for g in range(8):
                bS = newbank()
                for cc in range(4):
                    S.op("pe", lambda e, bS=bS, g=g, cc=cc, i=i: e.matmul(psf[bS][:, cc * 128:(cc + 1) * 128], lhsT=vh[i][:, cc, g * 128:(g + 1) * 128],
                                                                         rhs=WsT[:, g, :], start=True, stop=True),
                         r=[vhT[i][cc], WsTT], w=[pst[bS]])
                si = rr2[0]
                rr2[0] = (si + 1) % 2
                S.op("dve", lambda e, bS=bS, g=g, si=si: e.scalar_tensor_tensor(out=stmp[si][:, :].rearrange("p (c t) -> p c t", c=4),
                                                                               in0=psf[bS][:, :].rearrange("p (c t) -> p c t", c=4),
                                                                               scalar=vec[:, V_SLG + g:V_SLG + g + 1],
                                                                               in1=Bg[:, None, g * 128:(g + 1) * 128].to_broadcast([128, 4, 128]),
                                                                               op0=ALU.mult, op1=ALU.add),
                     r=[pst[bS], vecT, BgT], w=[stmpT[si]])
                S.op("dve", lambda e, g=g, si=si, i=i: e.tensor_tensor(out=ub[i][:, g, :], in0=ub[i][:, g, :], in1=stmp[si][:, :], op=ALU.mult),
                     r=[ubT[i][g], stmpT[si]], w=[ubT[i][g]])

        def stageA_proj(T):
            i = T % 2
            for m in range(KC):
                bG = newbank()
                for k in range(KC):
                    S.op("pe", lambda e, bG=bG, k=k, m=m: e.matmul(psf[bG][:, :], lhsT=Wg0[:, k, m * 128:(m + 1) * 128], rhs=hT4[:, T, k, :],
                                                                  start=(k == 0), stop=(k == KC - 1)),
                         r=[Wg0T[k // 2], hTT[T]], w=[pst[bG]])
                bY = newbank()
                for j in range(KC):
                    S.op("pe", lambda e, bY=bY, j=j, m=m, i=i: e.matmul(psf[bY][:, :], lhsT=Wpa[:, j, m * 128:(m + 1) * 128], rhs=ub[i][:, j, :],
                                                                       start=(j == 0), stop=(j == KC - 1)),
                         r=[WpaT[j // 2], ubT[i][j]], w=[pst[bY]])
                si = rr2[1]
                rr2[1] = (si + 1) % 2
                S.op("act", lambda e, bG=bG, si=si, m=m: e.activation(out=sg0[si][:, :], in_=psf[bG][:, :], func=AF.Sigmoid,
                                                                     bias=vec[:, V_BG0 + m:V_BG0 + m + 1], scale=1.0),
                     r=[pst[bG], vecT], w=[sg0T[si]])
                S.op("dve", lambda e, bY=bY, si=si: e.tensor_tensor(out=t1[si][:, :], in0=psf[bY][:, :], in1=sg0[si][:, :], op=ALU.mult),
                     r=[pst[bY], sg0T[si]], w=[t1T[si]])
                S.op("dve", lambda e, si=si, m=m: e.tensor_tensor(out=M2[:, T, m, :], in0=M2[:, T, m, :], in1=t1[si][:, :], op=ALU.add),
                     r=[M2T[T], t1T[si]], w=[M2T[T]])

        stageA_v(0)
        stageA_u(0)
        for T in range(NT):
            stageA_sgu(T)
            if T == 0 and dbg:
                dump("d_vh", vh[0][:, :, :], vhT[0])
            if T + 1 < NT:
                stageA_v(T + 1)
                if T + 1 == NT - 1:
                    load_next_w()
                stageA_u(T + 1)
                if T + 1 == NT - 1:
                    load_next_w()
            stageA_proj(T)
            if T == 0 and dbg:
                dump("d_su", ub[0][:, :, :], ubT[0])
        dump("d_mg", M2[:, :, :, :], M2T)
        load_next_w()
        load_next_w()

        X = P23.view(0, F32, [NCH, D])
        XT = [P23.tk(c * 4096, 4096) for c in range(NCH)]
        h2T4 = P1.view(0, BF16, [NT, KC, 512])
        h2TT = M2T
        Wout, WoutT = W("out")
        for c in range(NCH):
            S.dma("sp", lambda e, c=c: e.dma_start(out=X[:, c, :], in_=xv[:, c, :]), w=[XT[c]])

        def stageW(T):
            for cc in range(4):
                c = T * 4 + cc
                for hf in range(2):
                    bO = newbank()
                    for m in range(KC):
                        S.op("pe", lambda e, bO=bO, m=m, cc=cc, hf=hf: e.matmul(psf[bO][:, :], lhsT=M2[:, T, m, cc * 128:(cc + 1) * 128],
                                                                               rhs=Wout[:, m, hf * 512:(hf + 1) * 512], start=(m == 0), stop=(m == KC - 1)),
                             r=[M2T[T], WoutT[m // 2]], w=[pst[bO]])
                    S.op("dve", lambda e, bO=bO, c=c, hf=hf: e.tensor_tensor(out=X[:, c, hf * 512:(hf + 1) * 512], in0=psf[bO][:, :],
                                                                            in1=X[:, c, hf * 512:(hf + 1) * 512], op=ALU.add),
                         r=[pst[bO], XT[c]], w=[XT[c]])

        def norm2(T, xn_lo=0):
            norm_to_hT(T, lambda cc, T=T: X[:, T * 4 + cc, :], lambda cc, T=T: XT[T * 4 + cc], gb2, gb2T, h2T4, h2TT, ss2, ss2T, rs1, rs1T, sd1, sd1T,
                       xn_lo=xn_lo)

        stageW(0)
        stageW(1)
        norm2(0)
        stageW(2)
        norm2(1)
        stageW(3)
        load_next_w()
        norm2(2)
        if dbg:
            dump("d_x1", X[:, :, :], XT)

        fb = [P1.view(32768 + i * 8192, BF16, [KC, 512]) for i in range(2)]
        fbT = [[P1.tk(32768 + i * 8192 + j * 1024, 1024) for j in range(KC)] for i in range(2)]
        fb_rr = [0]

        def ffn1(q, T, fi):
            W1, W1T = W("f1_%d" % q)
            for j in range(KC):
                bF = newbank()
                for k in range(KC):
                    S.op("pe", lambda e, bF=bF, k=k, j=j, W1=W1: e.matmul(psf[bF][:, :], lhsT=W1[:, k, j * 128:(j + 1) * 128], rhs=h2T4[:, T, k, :],
                                                                         start=(k == 0), stop=(k == KC - 1)),
                         r=[W1T[k // 2], h2TT[T]], w=[pst[bF]])
                S.op("act", lambda e, bF=bF, j=j, fi=fi: e.activation(out=fb[fi][:, j, :], in_=psf[bF][:, :], func=AF.Relu), r=[pst[bF]], w=[fbT[fi][j]])
                S.op("dve", lambda e, j=j, fi=fi: e.tensor_tensor(out=fb[fi][:, j, :], in0=fb[fi][:, j, :], in1=fb[fi][:, j, :], op=ALU.mult),
                     r=[fbT[fi][j]], w=[fbT[fi][j]])

        def ffn2(q, T, fi):
            W2, W2T = W("f2_%d" % q)
            for cc in range(4):
                c = T * 4 + cc
                for hf in range(2):
                    bO = newbank()
                    for j in range(KC):
                        S.op("pe", lambda e, bO=bO, j=j, cc=cc, hf=hf, W2=W2: e.matmul(psf[bO][:, :], lhsT=fb[fi][:, j, cc * 128:(cc + 1) * 128],
                                                                                      rhs=W2[:, j, hf * 512:(hf + 1) * 512], start=(j == 0), stop=(j == KC - 1)),
                             r=[fbT[fi][j], W2T[j // 2]], w=[pst[bO]])
                    S.op("dve", lambda e, bO=bO, c=c, hf=hf: e.tensor_tensor(out=X[:, c, hf * 512:(hf + 1) * 512], in0=psf[bO][:, :],
                                                                            in1=X[:, c, hf * 512:(hf + 1) * 512], op=ALU.add),
                         r=[pst[bO], XT[c]], w=[XT[c]])

        def final_norm(T):
            for cc in range(4):
                c = T * 4 + cc
                S.op("act", lambda e, c=c: e.activation(out=junk[:, :], in_=X[:, c, :], func=AF.Square, accum_out=ss3[:, c:c + 1]),
                     r=[XT[c]], w=[junkT, ss3T])
            S.op("dve", lambda e: e.tensor_scalar(out=sd1[:, T * 4:T * 4 + 4], in0=ss3[:, T * 4:T * 4 + 4], scalar1=1.0 / D, scalar2=EPS,
                                                  op0=ALU.mult, op1=ALU.add), r=[ss3T], w=[sd1T])
            S.op("act", lambda e: e.activation(out=sd1[:, T * 4:T * 4 + 4], in_=sd1[:, T * 4:T * 4 + 4], func=AF.Sqrt), r=[sd1T], w=[sd1T])
            S.op("dve", lambda e: e.reciprocal(out=rs1[:, T * 4:T * 4 + 4], in_=sd1[:, T * 4:T * 4 + 4]), r=[sd1T], w=[rs1T])
            for cc in range(4):
                c = T * 4 + cc
                S.op("dve", lambda e, c=c: e.scalar_tensor_tensor(out=X[:, c, :], in0=X[:, c, :], scalar=rs1[:, c:c + 1], in1=gbf[:, :],
                                                                  op0=ALU.mult, op1=ALU.mult),
                     r=[XT[c], rs1T, gbfT], w=[XT[c]])
                out_toks.append(S.dma("sp", lambda e, c=c: e.dma_start(out=out[c * 128:(c + 1) * 128, :], in_=X[:, c, :]), r=[XT[c]]))

        seq = [(q, T) for q in range(4) for T in range(NT)]
        fis = []
        for n, (q, T) in enumerate(seq):
            fis.append(n % 2)
        ffn1(seq[0][0], seq[0][1], fis[0])
        norm2(3, xn_lo=4)
        dump("d_h2", h2T4[:, :, :, :], h2TT)
        for n, (q, T) in enumerate(seq):
            if n + 1 < len(seq):
                ffn1(seq[n + 1][0], seq[n + 1][1], fis[n + 1])
            if n + 1 < len(seq) and seq[n + 1][1] == NT - 1 and seq[n + 1][0] < 2:
                load_next_w()
            ffn2(q, T, fis[n])
            if T == NT - 1 and q < 2:
                load_next_w()
            if q == 3:
                final_norm(T)

        S.wait("sp", out_toks)
        S.run()
    return nc


_W_NAMES = ["norm1_g", "w_in", "b_in", "sgu_ln_g", "sgu_ln_b", "sgu_w", "sgu_b", "w_proj_a", "conv_w", "conv_b",
            "conv_ln_g", "conv_ln_b", "w_proj_b", "b_proj_b", "w_out", "norm2_g", "w_ff1", "w_ff2", "norm_f_g"]


def _prep(inputs):
    d = {}
    for nm in _W_NAMES:
        a = np.asarray(inputs[nm], dtype=np.float32)
        if nm != "norm_f_g":
            a = a[0]
        if nm == "sgu_b":
            a = a.reshape(-1)
        d[nm] = np.ascontiguousarray(a)
    return d


def kernel(**inputs):
    x = np.asarray(inputs["x"], dtype=np.float32)
    B = x.shape[0]
    wd = _prep(inputs)
    nc = build_nc()
    in_maps = []
    for b in range(B):
        m = dict(wd)
        m["x"] = np.ascontiguousarray(x[b])
        in_maps.append(m)
    res = run_bass_kernel_spmd(nc, in_maps, core_ids=list(range(B)))
    return np.stack([np.asarray(r["out"], dtype=np.float32) for r in res.results], axis=0)
```

```python
import contextlib
import numpy as np
import concourse.bass as bass
import concourse.mybir as mybir
from concourse.bass_utils import run_bass_kernel_spmd

F32 = mybir.dt.float32
BF16 = mybir.dt.bfloat16
AF = mybir.ActivationFunctionType
ALU = mybir.AluOpType
DSZ = {F32: 4, BF16: 2}

SEQ = 2048
D = 1024
NT = 4
NCH = 16
KC = 8
CK = 31
EPS = 1e-6


class Tk:
    __slots__ = ("sp", "lo", "hi", "lw", "rd", "ov")

    def __init__(self, sp, lo, hi):
        self.sp, self.lo, self.hi = sp, lo, hi
        self.lw = []
        self.rd = []
        self.ov = None


class Sched:
    ENG = ("pe", "dve", "act", "pool", "sp")

    def __init__(self, nc, n_dma_sems=40):
        self.nc = nc
        self.sem = {e: nc.alloc_semaphore(name="es_" + e) for e in self.ENG}
        self.cnt = {e: 0 for e in self.ENG}
        self.prog = {e: [] for e in self.ENG}
        self.seen = {e: {} for e in self.ENG}
        self.pending = {e: None for e in self.ENG}
        self.dsem = [nc.alloc_semaphore(name="ds_%d" % i) for i in range(n_dma_sems)]
        self.dval = [0] * n_dma_sems
        self.drr = 0
        self.tiles = {}

    def tile(self, sp, lo, hi):
        t = Tk(sp, lo, hi)
        lst = self.tiles.setdefault(sp, [])
        t.ov = [t]
        for o in lst:
            if o.lo < hi and lo < o.hi:
                t.ov.append(o)
                o.ov.append(t)
        lst.append(t)
        return t

    def _need(self, eng, tok):
        if tok[0] == "e":
            _, p, c = tok
            if c > self.cnt[p]:
                ent = self.pending[p]
                assert ent is not None and c == self.cnt[p] + 1, (p, c, self.cnt[p])
                ent["inc"] = True
                self.cnt[p] += 1
                self.pending[p] = None
            key = ("e", p)
            sem = self.sem[p]
            val = c
        else:
            _, i, val = tok
            key = ("d", i)
            sem = self.dsem[i]
        if self.seen[eng].get(key, 0) >= val:
            return None
        self.seen[eng][key] = val
        return (sem, val)

    def _deps(self, eng, r, w):
        toks = []
        for t in r:
            for o in t.ov:
                toks.extend(o.lw)
        same = []
        if eng != "pe":
            same = [tok for tok in toks if tok[0] == "e" and tok[1] == eng]
        for t in w:
            for o in t.ov:
                toks.extend(o.lw)
                toks.extend(o.rd)
        waits = []
        for tok in toks:
            if tok[0] == "e" and tok[1] == eng:
                continue
            wt = self._need(eng, tok)
            if wt is not None:
                waits.append(wt)
        for tok in same:
            wt = self._need(eng, tok)
            if wt is not None:
                waits.append(wt)
        return waits

    def _mark(self, tok, r, w):
        for t in r:
            if not t.rd or t.rd[-1] != tok:
                t.rd.append(tok)
        for t in w:
            for o in t.ov:
                o.lw = [tok]
                o.rd = []

    def op(self, eng, fn, r=(), w=()):
        waits = self._deps(eng, r, w)
        ent = {"fn": fn, "inc": False, "waits": waits, "dma": None}
        self.prog[eng].append(ent)
        self.pending[eng] = ent
        tok = ("e", eng, self.cnt[eng] + 1)
        self._mark(tok, r, w)
        return tok

    def dma(self, eng, fn, r=(), w=()):
        waits = self._deps(eng, r, w)
        i = self.drr
        self.drr = (self.drr + 1) % len(self.dsem)
        if self.dval[i] > 0:
            wt = self._need(eng, ("d", i, self.dval[i]))
            if wt is not None:
                waits.append(wt)
        self.dval[i] += 16
        tok = ("d", i, self.dval[i])
        self.prog[eng].append({"fn": fn, "inc": False, "waits": waits, "dma": i})
        self._mark(tok, r, w)
        return tok

    def wait(self, eng, toks):
        waits = []
        for tok in toks:
            wt = self._need(eng, tok)
            if wt is not None:
                waits.append(wt)
        if waits:
            self.prog[eng].append({"fn": None, "inc": False, "waits": waits, "dma": None})

    def replay(self, eng, e):
        sem = self.sem[eng]
        for ent in self.prog[eng]:
            for (s, v) in ent["waits"]:
                e.wait_ge(s, v)
            if ent["fn"] is None:
                continue
            ins = ent["fn"](e)
            if ent["dma"] is not None:
                ins.then_inc(self.dsem[ent["dma"]], 16)
            elif ent["inc"]:
                ins.then_inc(sem, 1)

    def run(self):
        with self.nc.Block() as block:
            @block.tensor
            def _(e):
                self.replay("pe", e)

            @block.vector
            def _(e):
                self.replay("dve", e)

            @block.scalar
            def _(e):
                self.replay("act", e)

            @block.gpsimd
            def _(e):
                self.replay("pool", e)

            @block.sync
            def _(e):
                self.replay("sp", e)


class Region:
    def __init__(self, S, stack, name, nbytes):
        self.S = S
        self.name = name
        self.nbytes = nbytes
        self.t = stack.enter_context(S.nc.sbuf_tensor(name, [128, nbytes // 4], F32))

    def view(self, lo, dtype, shape):
        n = 1
        for s in shape:
            n *= s
        nb = n * DSZ[dtype]
        assert lo % 4 == 0 and nb % 4 == 0 and lo + nb <= self.nbytes, (self.name, lo, nb)
        ap = self.t[:, lo // 4:(lo + nb) // 4]
        if dtype != F32:
            ap = ap.bitcast(dtype)
        if len(shape) == 2:
            ap = ap.rearrange("p (a b) -> p a b", a=shape[0])
        elif len(shape) == 3:
            ap = ap.rearrange("p (a b c) -> p a b c", a=shape[0], b=shape[1])
        return ap

    def tk(self, lo, nb):
        assert lo + nb <= self.nbytes
        return self.S.tile("sb:" + self.name, lo, lo + nb)


def build_nc(dbg=False):
    nc = bass.Bass("TRN2", target_bir_lowering=False)

    def din(name, shape):
        return nc.dram_tensor(name, shape, F32, kind="ExternalInput").ap()

    x = din("x", [SEQ, D])
    norm1_g = din("norm1_g", [D])
    w_in = din("w_in", [D, 6 * D])
    b_in = din("b_in", [6 * D])
    sgu_ln_g = din("sgu_ln_g", [D])
    sgu_ln_b = din("sgu_ln_b", [D])
    sgu_w = din("sgu_w", [8, 128, 128])
    sgu_b = din("sgu_b", [D])
    w_proj_a = din("w_proj_a", [D, D])
    conv_w = din("conv_w", [CK, D])
    conv_b = din("conv_b", [D])
    conv_ln_g = din("conv_ln_g", [D])
    conv_ln_b = din("conv_ln_b", [D])
    w_proj_b = din("w_proj_b", [D, D])
    b_proj_b = din("b_proj_b", [D])
    w_out = din("w_out", [D, D])
    norm2_g = din("norm2_g", [D])
    w_ff1 = din("w_ff1", [D, 4 * D])
    w_ff2 = din("w_ff2", [4 * D, D])
    norm_f_g = din("norm_f_g", [D])
    out = nc.dram_tensor("out", [SEQ, D], F32, kind="ExternalOutput").ap()
    dbg_t = {}
    if dbg:
        for nm, shp, dt_ in (("d_hT", [128, NT * KC * 512], BF16), ("d_cs", [128, KC * SEQ], BF16),
                             ("d_m2", [128, NT * KC * 512], BF16), ("d_mg", [128, NT * KC * 512], BF16),
                             ("d_x1", [128, NCH * D], F32), ("d_h2", [128, NT * KC * 512], BF16),
                             ("d_vec", [128, 128], F32), ("d_cw", [128, KC * CK], F32),
                             ("d_bg", [128, D], F32), ("d_wst", [128, D], BF16),
                             ("d_vh", [128, 4 * D], BF16), ("d_su", [128, KC * 512], BF16)):
            dbg_t[nm] = nc.dram_tensor(nm, shp, dt_, kind="ExternalOutput").ap()

    S = Sched(nc)
    with contextlib.ExitStack() as st:
        RING = Region(S, st, "RING", 65536)
        P1 = Region(S, st, "P1", 49152)
        P23 = Region(S, st, "P23", 65536)
        CST = Region(S, st, "CST", 28672)
        psf = [st.enter_context(nc.psum_tensor("ps%d" % i, [128, 512], F32)) for i in range(8)]
        psb = [p[:, :].bitcast(BF16) for p in psf]
        pst = [S.tile("ps", i, i + 1) for i in range(8)]
        bank_rr = [0]

        def newbank():
            i = bank_rr[0]
            bank_rr[0] = (i + 1) % 8
            return i

        out_toks = []

        def dump(nm, ap, tks):
            if dbg:
                out_toks.append(S.dma("sp", lambda e: e.dma_start(out=dbg_t[nm][:, :], in_=ap), r=tks))

        coff = [0]

        def calloc(dtype, shape):
            n = int(np.prod(shape)) * DSZ[dtype]
            n = (n + 31) // 32 * 32
            v = CST.view(coff[0], dtype, shape)
            t = CST.tk(coff[0], n)
            coff[0] += n
            return v, t

        ident, identT = calloc(BF16, [128])
        identf, identfT = calloc(F32, [128])
        vec, vecT = calloc(F32, [128])
        convw, convwT = calloc(F32, [KC, 32])
        gb1, gb1T = calloc(F32, [D])
        gb2, gb2T = calloc(F32, [D])
        gbf, gbfT = calloc(F32, [D])
        Bg, BgT = calloc(F32, [D])
        WsT, WsTT = calloc(BF16, [8, 128])
        bvrow, bvrowT = calloc(BF16, [D])
        onesrow, onesrowT = calloc(BF16, [128])
        onesm, onesmT = calloc(BF16, [128])
        epsc, epscT = calloc(F32, [8])
        ss1, ss1T = calloc(F32, [NCH])
        ss2, ss2T = calloc(F32, [NCH])
        ss3, ss3T = calloc(F32, [NCH])
        rs1, rs1T = calloc(F32, [NCH])
        sd1, sd1T = calloc(F32, [NCH])
        bst, bstT = calloc(F32, [4, 12])
        mv, mvT = calloc(F32, [4, 2])
        rv, rvT = calloc(F32, [4])
        sdv, sdvT = calloc(F32, [4])
        junk, junkT = calloc(F32, [D])
        V_BU, V_BV, V_BVAL, V_BGATE, V_BG0, V_BG1, V_N1, V_N2, V_SLG, V_CB, V_CLG, V_CLB, V_BPB = [8 * i for i in range(13)]

        S.op("pool", lambda e: e.memset(identf[:, :], 0.0), w=[identfT])
        S.op("pool", lambda e: e.affine_select(out=identf[:, :], in_=identf[:, :], pattern=[[-1, 128]],
                                               compare_op=ALU.not_equal, fill=1.0, base=0, channel_multiplier=1),
             r=[identfT], w=[identfT])
        S.op("dve", lambda e: e.tensor_copy(out=ident[:, :], in_=identf[:, :]), r=[identfT], w=[identT])
        S.op("pool", lambda e: e.memset(onesrow[0:1, :], 1.0), w=[onesrowT])
        S.op("pool", lambda e: e.memset(onesm[:, :], 1.0 / D), w=[onesmT])
        S.op("pool", lambda e: e.memset(epsc[:, :], EPS), w=[epscT])
        for s_, sT_ in ((ss1, ss1T), (ss2, ss2T), (ss3, ss3T)):
            S.op("pool", lambda e, s_=s_: e.memset(s_[:, :], 0.0), w=[sT_])
        slotv = [RING.view(s * 16384, BF16, [8, 1024]) for s in range(4)]
        slotT = [[RING.tk(s * 16384 + kp * 4096, 4096) for kp in range(4)] for s in range(4)]
        w_in_v = w_in.rearrange("(k p) e -> p k e", p=128)
        wsrc = {
            "u": w_in_v[:, :, 0 * D:1 * D], "v": w_in_v[:, :, 1 * D:2 * D], "val": w_in_v[:, :, 2 * D:3 * D],
            "gate": w_in_v[:, :, 3 * D:4 * D], "g0": w_in_v[:, :, 4 * D:5 * D], "g1": w_in_v[:, :, 5 * D:6 * D],
            "pa": w_proj_a.rearrange("(k p) e -> p k e", p=128), "pb": w_proj_b.rearrange("(k p) e -> p k e", p=128),
            "out": w_out.rearrange("(k p) e -> p k e", p=128),
        }
        w1v = w_ff1.rearrange("(k p) e -> p k e", p=128)
        w2v = w_ff2.rearrange("(j p) e -> p j e", p=128)
        for q in range(4):
            wsrc["f1_%d" % q] = w1v[:, :, q * D:(q + 1) * D]
            wsrc["f2_%d" % q] = w2v[:, q * 8:(q + 1) * 8, :]
        worder = ["val", "gate", "pb", "g1", "v", "u", "pa", "g0", "out",
                  "f1_0", "f2_0", "f1_1", "f2_1", "f1_2", "f2_2", "f1_3", "f2_3"]
        wslot = {nm: i % 4 for i, nm in enumerate(worder)}
        wnext = [0]

        def load_next_w():
            if wnext[0] >= len(worder):
                return
            nm = worder[wnext[0]]
            wnext[0] += 1
            s = wslot[nm]
            src = wsrc[nm]
            for kp in range(4):
                S.dma("pool", lambda e, s=s, kp=kp, src=src: e.dma_start(out=slotv[s][:, 2 * kp:2 * kp + 2, :], in_=src[:, 2 * kp:2 * kp + 2, :]),
                      w=[slotT[s][kp]])

        def W(nm):
            s = wslot[nm]
            return slotv[s], slotT[s]

        for _ in range(2):
            load_next_w()

        so = 32768
        VR = P23.view(so, F32, [128]); VRT = P23.tk(so, 512); so += 512
        CW = P23.view(so, F32, [D]); CWT = P23.tk(so, 4096); so += 4096
        Wtf = P23.view(so, F32, [8, 128]); WtfT = P23.tk(so, 4096); so += 4096
        Wtb = P23.view(so, BF16, [8, 128]); WtbT = P23.tk(so, 2048); so += 2048
        betaf = P23.view(so, F32, [D]); betafT = P23.tk(so, 4096); so += 4096
        betab = P23.view(so, BF16, [D]); betabT = P23.tk(so, 2048); so += 2048

        hT4 = P23.view(0, BF16, [NT, KC, 512])
        hTT = [P23.tk(T * 8192, 8192) for T in range(NT)]
        xt = [P1.view(i * 16384, F32, [4, D]) for i in range(2)]
        xtT = [[P1.tk(i * 16384 + cc * 4096, 4096) for cc in range(4)] for i in range(2)]
        xv = x.rearrange("(c p) d -> p c d", p=128)

        def load_x_tile(T):
            i = T % 2
            for cc in range(4):
                c = T * 4 + cc
                S.dma("sp", lambda e, i=i, cc=cc, c=c: e.dma_start(out=xt[i][:, cc, :], in_=xv[:, c, :]), w=[xtT[i][cc]])

        S.op("dve", lambda e: e.memset(VR[:, :], 0.0), w=[VRT])
        load_x_tile(0)
        S.dma("sp", lambda e: e.dma_start(out=gb1[:, :], in_=norm1_g.partition_broadcast(128)), w=[gb1T])
        load_x_tile(1)
        S.dma("pool", lambda e: e.dma_start(out=bvrow[0:1, :], in_=b_in[D:2 * D].rearrange("(o n) -> o n", o=1)), w=[bvrowT])
        S.dma("pool", lambda e: e.dma_start(out=CW[0:CK, :], in_=conv_w[:, :]), w=[CWT])
        S.dma("pool", lambda e: e.dma_start(out=VR[0:48, :], in_=b_in.rearrange("(r p) -> r p", p=128)), w=[VRT])
        for i, v_ in enumerate((norm1_g, norm2_g, sgu_ln_g, conv_b, conv_ln_g, conv_ln_b, b_proj_b)):
            S.dma("pool", lambda e, i=i, v_=v_: e.dma_start(out=VR[48 + 8 * i:56 + 8 * i, :], in_=v_.rearrange("(r p) -> r p", p=128)), w=[VRT])

        def late_setup_dmas():
            S.dma("sp", lambda e: e.dma_start(out=Wtf[:, :, :], in_=sgu_w.rearrange("g t s -> t g s")), w=[WtfT])
            S.dma("sp", lambda e: e.dma_start(out=Bg[:, :], in_=sgu_b.partition_broadcast(128)), w=[BgT])
            S.dma("sp", lambda e: e.dma_start(out=betaf[:, :], in_=sgu_ln_b.partition_broadcast(128)), w=[betafT])
            S.dma("sp", lambda e: e.dma_start(out=gb2[:, :], in_=norm2_g.partition_broadcast(128)), w=[gb2T])
            S.dma("sp", lambda e: e.dma_start(out=gbf[:, :], in_=norm_f_g.partition_broadcast(128)), w=[gbfT])


        xn = [P1.view(32768 + i * 2048, BF16, [D]) for i in range(8)]
        xnT = [P1.tk(32768 + i * 2048, 2048) for i in range(8)]
        xn_rr = [0]

        def norm_to_hT(T, src_ap_fn, src_tk_fn, gb, gbT, dst4, dstT, ss, ssT, rs, rsT, sd, sdT, xn_lo=0):
            for cc in range(4):
                c = T * 4 + cc
                S.op("act", lambda e, cc=cc, c=c: e.activation(out=junk[:, :], in_=src_ap_fn(cc), func=AF.Square, accum_out=ss[:, c:c + 1]),
                     r=[src_tk_fn(cc)], w=[junkT, ssT])
            S.op("dve", lambda e: e.tensor_scalar(out=sd[:, T * 4:T * 4 + 4], in0=ss[:, T * 4:T * 4 + 4], scalar1=1.0 / D, scalar2=EPS,
                                                  op0=ALU.mult, op1=ALU.add), r=[ssT], w=[sdT])
            S.op("act", lambda e: e.activation(out=sd[:, T * 4:T * 4 + 4], in_=sd[:, T * 4:T * 4 + 4], func=AF.Sqrt), r=[sdT], w=[sdT])
            S.op("dve", lambda e: e.reciprocal(out=rs[:, T * 4:T * 4 + 4], in_=sd[:, T * 4:T * 4 + 4]), r=[sdT], w=[rsT])
            for cc in range(4):
                c = T * 4 + cc
                xi = xn_lo + xn_rr[0] % (8 - xn_lo)
                xn_rr[0] += 1
                S.op("dve", lambda e, cc=cc, c=c, xi=xi: e.scalar_tensor_tensor(out=xn[xi][:, :], in0=src_ap_fn(cc), scalar=rs[:, c:c + 1],
                                                                               in1=gb[:, :], op0=ALU.mult, op1=ALU.mult),
                     r=[src_tk_fn(cc), rsT, gbT], w=[xnT[xi]])
                bk = newbank()
                for k in range(KC):
                    S.op("pe", lambda e, bk=bk, k=k, xi=xi: e.transpose(out=psb[bk][:, k * 128:(k + 1) * 128], in_=xn[xi][:, k * 128:(k + 1) * 128],
                                                                        identity=ident[:, :]), r=[xnT[xi], identT], w=[pst[bk]])
                eng = "act" if (cc % 2 == 0) else "dve"
                if eng == "act":
                    S.op("act", lambda e, bk=bk, cc=cc: e.activation(out=dst4[:, T, :, cc * 128:(cc + 1) * 128],
                                                                     in_=psb[bk][:, :].rearrange("p (k t) -> p k t", k=KC), func=AF.Copy),
                         r=[pst[bk]], w=[dstT[T]])
                else:
                    S.op("dve", lambda e, bk=bk, cc=cc: e.tensor_copy(out=dst4[:, T, :, cc * 128:(cc + 1) * 128],
                                                                      in_=psb[bk][:, :].rearrange("p (k t) -> p k t", k=KC)),
                         r=[pst[bk]], w=[dstT[T]])

        for T in range(NT):
            i = T % 2
            norm_to_hT(T, lambda cc, i=i: xt[i][:, cc, :], lambda cc, i=i: xtT[i][cc], gb1, gb1T, hT4, hTT, ss1, ss1T, rs1, rs1T, sd1, sd1T)
            if T + 2 < NT:
                load_x_tile(T + 2)
            if T == NT - 1:
                late_setup_dmas()
        bk = newbank()
        S.op("pe", lambda e, bk=bk: e.transpose(out=psf[bk][:, 0:104], in_=VR[0:104, :], identity=identf[0:104, 0:104]),
             r=[VRT, identfT], w=[pst[bk]])
        S.op("dve", lambda e, bk=bk: e.tensor_copy(out=vec[:, 0:104], in_=psf[bk][:, 0:104]), r=[pst[bk]], w=[vecT])
        bk = newbank()
        for j in range(KC):
            S.op("pe", lambda e, bk=bk, j=j: e.transpose(out=psf[bk][:, j * 32:j * 32 + CK], in_=CW[0:CK, j * 128:(j + 1) * 128],
                                                         identity=identf[0:CK, 0:CK]), r=[CWT, identfT], w=[pst[bk]])
        S.op("dve", lambda e, bk=bk: e.tensor_copy(out=convw[:, :, 0:CK],
                                                   in_=psf[bk][:, 0:256].rearrange("p (j k) -> p j k", k=32)[:, :, 0:CK]),
             r=[pst[bk]], w=[convwT])
        dump("d_hT", hT4[:, :, :, :], hTT)
        dump("d_vec", vec[:, :], [vecT])
        dump("d_cw", convw[:, :, 0:CK], [convwT])
        S.op("pool", lambda e: e.affine_select(out=Wtf[:, :, :], in_=Wtf[:, :, :], pattern=[[0, 8], [-1, 128]],
                                               compare_op=ALU.is_ge, fill=0.0, base=0, channel_multiplier=1),
             r=[WtfT], w=[WtfT])
        S.op("dve", lambda e: e.tensor_copy(out=Wtb[:, :, :], in_=Wtf[:, :, :]), r=[WtfT], w=[WtbT])
        bk = newbank()
        for g in range(8):
            S.op("pe", lambda e, bk=bk, g=g: e.transpose(out=psb[bk][:, g * 128:(g + 1) * 128], in_=Wtb[:, g, :], identity=ident[:, :]),
                 r=[WtbT, identT], w=[pst[bk]])
        S.op("dve", lambda e, bk=bk: e.tensor_copy(out=WsT[:, :, :], in_=psb[bk][:, :].rearrange("p (g t) -> p g t", g=8)),
             r=[pst[bk]], w=[WsTT])
        S.op("dve", lambda e: e.tensor_copy(out=betab[:, :], in_=betaf[:, :]), r=[betafT], w=[betabT])
        for hh in range(2):
            bk = newbank()
            for g4 in range(4):
                g = hh * 4 + g4
                S.op("pe", lambda e, bk=bk, g=g, g4=g4: e.matmul(psf[bk][:, g4 * 128:(g4 + 1) * 128], lhsT=betab[:, g * 128:(g + 1) * 128],
                                                                rhs=WsT[:, g, :], start=True, stop=True),
                     r=[betabT, WsTT], w=[pst[bk]])
            S.op("dve", lambda e, bk=bk, hh=hh: e.tensor_tensor(out=Bg[:, hh * 512:(hh + 1) * 512], in0=psf[bk][:, :],
                                                               in1=Bg[:, hh * 512:(hh + 1) * 512], op=ALU.add),
                 r=[pst[bk], BgT], w=[BgT])
        dump("d_bg", Bg[:, :], [BgT])
        dump("d_wst", WsT[:, :, :], [WsTT])

        CV = P23.view(32768, BF16, [KC, SEQ])
        CVT = [[P23.tk(32768 + j * 4096 + T * 1024, 1024) for T in range(NT)] for j in range(KC)]
        APAD = 2080
        abuf = [P1.view(i * 4160, BF16, [APAD]) for i in range(2)]
        abufT = [[P1.tk(i * 4160, 60)] + [P1.tk(i * 4160 + 60 + T * 1024, 1024) for T in range(NT)] for i in range(2)]
        dg = [P1.view(8320 + i * 7936, BF16, [CK, 128]) for i in range(2)]
        dgT = [P1.tk(8320 + i * 7936, 7936) for i in range(2)]
        sig = [P1.view(24192 + i * 1024, BF16, [512]) for i in range(2)]
        sigT = [P1.tk(24192 + i * 1024, 1024) for i in range(2)]
        M2 = P1.view(0, BF16, [NT, KC, 512])
        M2T = [P1.tk(T * 8192, 8192) for T in range(NT)]
        sqt = [P1.view(26240 + i * 1024, BF16, [512]) for i in range(8)]
        sqtT = [P1.tk(26240 + i * 1024, 1024) for i in range(8)]
        NYB = 3
        yb = [P1.view(34432 + i * 2048, F32, [512]) for i in range(NYB)]
        ybT = [P1.tk(34432 + i * 2048, 2048) for i in range(NYB)]
        stt = P1.view(40576, F32, [512]); sttT = P1.tk(40576, 2048)
        rstdb = P1.view(42624, F32, [512]); rstdbT = P1.tk(42624, 2048)
        nmrb = P1.view(44672, F32, [512]); nmrbT = P1.tk(44672, 2048)
        sg = [P1.view(46720 + i * 1024, BF16, [512]) for i in range(2)]
        sgT = [P1.tk(46720 + i * 1024, 1024) for i in range(2)]
        Wpb, WpbT = W("pb")
        Wg1, Wg1T = W("g1")

        def btail_sq(T):
            for j in range(KC):
                S.op("act", lambda e, j=j: e.activation(out=sqt[j][:, :], in_=CV[:, j, T * 512:(T + 1) * 512], func=AF.Square),
                     r=[CVT[j][T]], w=[sqtT[j]])

        def btail_stats(T):
            bM = newbank()
            bQ = newbank()
            for j in range(KC):
                qi = j
                S.op("pe", lambda e, j=j, bM=bM: e.matmul(psf[bM][:, :], lhsT=onesm[:, :], rhs=CV[:, j, T * 512:(T + 1) * 512], start=(j == 0), stop=(j == KC - 1)),
                     r=[onesmT, CVT[j][T]], w=[pst[bM]])
                S.op("pe", lambda e, j=j, bQ=bQ, qi=qi: e.matmul(psf[bQ][:, :], lhsT=onesm[:, :], rhs=sqt[qi][:, :], start=(j == 0), stop=(j == KC - 1)),
                     r=[onesmT, sqtT[qi]], w=[pst[bQ]])
            S.op("act", lambda e: e.activation(out=stt[:, :], in_=psf[bM][:, :], func=AF.Square), r=[pst[bM]], w=[sttT])
            S.op("dve", lambda e: e.tensor_tensor(out=stt[:, :], in0=psf[bQ][:, :], in1=stt[:, :], op=ALU.subtract), r=[pst[bQ], sttT], w=[sttT])
            S.op("act", lambda e: e.activation(out=stt[:, :], in_=stt[:, :], func=AF.Sqrt, bias=epsc[:, 0:1], scale=1.0), r=[sttT, epscT], w=[sttT])
            S.op("dve", lambda e: e.reciprocal(out=rstdb[:, :], in_=stt[:, :]), r=[sttT], w=[rstdbT])
            S.op("dve", lambda e: e.scalar_tensor_tensor(out=nmrb[:, :], in0=psf[bM][:, :], scalar=-1.0, in1=rstdb[:, :], op0=ALU.mult, op1=ALU.mult),
                 r=[pst[bM], rstdbT], w=[nmrbT])

        def btail_norm(T):
            for j in range(KC):
                yi = j % NYB
                S.op("dve", lambda e, j=j, yi=yi: e.tensor_tensor(out=yb[yi][:, :], in0=CV[:, j, T * 512:(T + 1) * 512], in1=rstdb[:, :], op=ALU.mult),
                     r=[CVT[j][T], rstdbT], w=[ybT[yi]])
                S.op("dve", lambda e, j=j, yi=yi: e.tensor_tensor(out=yb[yi][:, :], in0=yb[yi][:, :], in1=nmrb[:, :], op=ALU.add),
                     r=[ybT[yi], nmrbT], w=[ybT[yi]])
                S.op("act", lambda e, j=j, yi=yi: e.activation(out=CV[:, j, T * 512:(T + 1) * 512], in_=yb[yi][:, :], func=AF.Silu,
                                                               bias=vec[:, V_CLB + j:V_CLB + j + 1], scale=vec[:, V_CLG + j:V_CLG + j + 1]),
                     r=[ybT[yi], vecT], w=[CVT[j][T]])

        sg_rr = [0]

        def btail_proj(T, m_lo=0, m_hi=KC):
            for m in range(m_lo, m_hi):
                bG = newbank()
                for k in range(KC):
                    S.op("pe", lambda e, bG=bG, k=k, m=m: e.matmul(psf[bG][:, :], lhsT=Wg1[:, k, m * 128:(m + 1) * 128], rhs=hT4[:, T, k, :],
                                                                  start=(k == 0), stop=(k == KC - 1)),
                         r=[Wg1T[k // 2], hTT[T]], w=[pst[bG]])
                bY = newbank()
                for j in range(KC):
                    S.op("pe", lambda e, bY=bY, j=j, m=m: e.matmul(psf[bY][:, :], lhsT=Wpb[:, j, m * 128:(m + 1) * 128], rhs=CV[:, j, T * 512:(T + 1) * 512],
                                                                  start=(j == 0), stop=(j == KC - 1)),
                         r=[WpbT[j // 2], CVT[j][T]], w=[pst[bY]])
                si = sg_rr[0]
                sg_rr[0] = (si + 1) % 2
                S.op("act", lambda e, bG=bG, si=si, m=m: e.activation(out=sg[si][:, :], in_=psf[bG][:, :], func=AF.Sigmoid,
                                                                     bias=vec[:, V_BG1 + m:V_BG1 + m + 1], scale=1.0),
                     r=[pst[bG], vecT], w=[sgT[si]])
                S.op("dve", lambda e, bY=bY, si=si, m=m: e.scalar_tensor_tensor(out=M2[:, T, m, :], in0=psf[bY][:, :], scalar=vec[:, V_BPB + m:V_BPB + m + 1],
                                                                               in1=sg[si][:, :], op0=ALU.add, op1=ALU.mult),
                     r=[pst[bY], vecT, sgT[si]], w=[M2T[T]])

        Wval, WvalT = W("val")
        Wgate, WgateT = W("gate")
        for i in range(2):
            S.op("pool", lambda e, i=i: e.memset(abuf[i][:, 0:30], 0.0), w=[abufT[i][0]])
        sig_rr = [0]
        for j in range(KC):
            ab = abuf[j % 2]
            abT = abufT[j % 2]
            dgi = dg[j % 2]
            dgiT = dgT[j % 2]
            S.op("dve", lambda e, dgi=dgi, j=j: e.tensor_tensor(out=dgi[:, :, :], in0=ident[:, None, :].to_broadcast([128, CK, 128]),
                                                                in1=convw[:, j, 0:CK, None].to_broadcast([128, CK, 128]), op=ALU.mult),
                 r=[identT, convwT], w=[dgiT])
            pend = None
            for T in range(NT + 1):
                if T < NT:
                    bB = newbank()
                    for k in range(KC):
                        S.op("pe", lambda e, bB=bB, k=k, j=j, T=T: e.matmul(psf[bB][:, :], lhsT=Wgate[:, k, j * 128:(j + 1) * 128], rhs=hT4[:, T, k, :],
                                                                           start=(k == 0), stop=(k == KC - 1)),
                             r=[WgateT[k // 2], hTT[T]], w=[pst[bB]])
                    si = sig_rr[0]
                    sig_rr[0] = (si + 1) % 2
                    S.op("act", lambda e, bB=bB, si=si, j=j: e.activation(out=sig[si][:, :], in_=psf[bB][:, :], func=AF.Sigmoid,
                                                                         bias=vec[:, V_BGATE + j:V_BGATE + j + 1], scale=1.0),
                         r=[pst[bB], vecT], w=[sigT[si]])
                    bA = newbank()
                    for k in range(KC):
                        S.op("pe", lambda e, bA=bA, k=k, j=j, T=T: e.matmul(psf[bA][:, :], lhsT=Wval[:, k, j * 128:(j + 1) * 128], rhs=hT4[:, T, k, :],
                                                                           start=(k == 0), stop=(k == KC - 1)),
                             r=[WvalT[k // 2], hTT[T]], w=[pst[bA]])
                    S.op("dve", lambda e, bA=bA, si=si, j=j, T=T, ab=ab: e.scalar_tensor_tensor(out=ab[:, 30 + T * 512:30 + (T + 1) * 512], in0=psf[bA][:, :],
                                                                                                  scalar=vec[:, V_BVAL + j:V_BVAL + j + 1], in1=sig[si][:, :],
                                                                                                  op0=ALU.add, op1=ALU.mult),
                         r=[pst[bA], vecT, sigT[si]], w=[abT[T + 1]])
                if pend is not None:
                    Tp = pend
                    bC = newbank()
                    for k in range(CK):
                        S.op("pe", lambda e, bC=bC, k=k, Tp=Tp, ab=ab, dgi=dgi: e.matmul(psf[bC][:, :], lhsT=dgi[:, k, :], rhs=ab[:, Tp * 512 + k:Tp * 512 + k + 512],
                                                                                        start=(k == 0), stop=(k == CK - 1)),
                             r=[dgiT, abT[Tp], abT[Tp + 1]], w=[pst[bC]])
                    S.op("act", lambda e, bC=bC, Tp=Tp, j=j: e.activation(out=CV[:, j, Tp * 512:(Tp + 1) * 512], in_=psf[bC][:, :], func=AF.Identity,
                                                                         bias=vec[:, V_CB + j:V_CB + j + 1], scale=1.0),
                         r=[pst[bC], vecT], w=[CVT[j][Tp]])
                    if j == KC - 1:
                        if Tp >= 1:
                            btail_stats(Tp - 1)
                        if Tp < NT - 1:
                            if Tp >= 1:
                                btail_norm(Tp - 1)
                            btail_sq(Tp)
                pend = T if T < NT else None
            if j == 1:
                load_next_w()
                load_next_w()
        load_next_w()
        load_next_w()

        btail_proj(0, 0, 4)
        btail_norm(NT - 2)
        btail_proj(0, 4, 8)
        btail_sq(NT - 1)
        btail_proj(1, 0, 4)
        btail_stats(NT - 1)
        btail_norm(NT - 1)
        btail_proj(1, 4, 8)
        for T in range(2, NT):
            btail_proj(T)
        dump("d_cs", CV[:, :, :], [t for row in CVT for t in row])
        dump("d_m2", M2[:, :, :, :], M2T)
        load_next_w()
        load_next_w()

        vh = [P23.view(32768 + i * 8192, BF16, [4, D]) for i in range(2)]
        vhT = [[P23.tk(32768 + i * 8192 + cc * 2048, 2048) for cc in range(4)] for i in range(2)]
        ub = [P23.view(49152 + i * 8192, BF16, [KC, 512]) for i in range(2)]
        ubT = [[P23.tk(49152 + i * 8192 + m * 1024, 1024) for m in range(KC)] for i in range(2)]
        stmp = [P1.view(32768 + i * 1024, BF16, [512]) for i in range(2)]
        stmpT = [P1.tk(32768 + i * 1024, 1024) for i in range(2)]
        sg0 = [P1.view(34816 + i * 1024, BF16, [512]) for i in range(2)]
        sg0T = [P1.tk(34816 + i * 1024, 1024) for i in range(2)]
        t1 = [P1.view(36864 + i * 1024, BF16, [512]) for i in range(2)]
        t1T = [P1.tk(36864 + i * 1024, 1024) for i in range(2)]
        Wv, WvT = W("v")
        Wu, WuT = W("u")
        Wpa, WpaT = W("pa")
        Wg0, Wg0T = W("g0")
        rr2 = [0, 0, 0]

        def stageA_v(T):
            i = T % 2
            for cc in range(4):
                bV = [newbank(), newbank()]
                for hf in range(2):
                    for k in range(KC):
                        S.op("pe", lambda e, b=bV[hf], k=k, cc=cc, hf=hf: e.matmul(psf[b][:, :], lhsT=hT4[:, T, k, cc * 128:(cc + 1) * 128],
                                                                                  rhs=Wv[:, k, hf * 512:(hf + 1) * 512], start=(k == 0), stop=False),
                             r=[hTT[T], WvT[k // 2]], w=[pst[bV[hf]]])
                    S.op("pe", lambda e, b=bV[hf], hf=hf: e.matmul(psf[b][:, :], lhsT=onesrow[0:1, :], rhs=bvrow[0:1, hf * 512:(hf + 1) * 512], start=False, stop=True),
                         r=[onesrowT, bvrowT], w=[pst[bV[hf]]])
                    S.op("act", lambda e, b=bV[hf], hf=hf, cc=cc, i=i: e.activation(out=vh[i][:, cc, hf * 512:(hf + 1) * 512], in_=psf[b][:, :], func=AF.Gelu_apprx_tanh),
                         r=[pst[bV[hf]]], w=[vhT[i][cc]])
                    S.op("dve", lambda e, hf=hf, cc=cc, i=i: e.bn_stats(out=bst[:, cc, hf * 6:(hf + 1) * 6], in_=vh[i][:, cc, hf * 512:(hf + 1) * 512]),
                         r=[vhT[i][cc]], w=[bstT])
                S.op("dve", lambda e, cc=cc: e.bn_aggr(out=mv[:, cc, :], in_=bst[:, cc, :]), r=[bstT], w=[mvT])
            S.op("dve", lambda e: e.tensor_scalar(out=sdv[:, :], in0=mv[:, :, 1], scalar1=EPS, scalar2=None, op0=ALU.add), r=[mvT], w=[sdvT])
            S.op("act", lambda e: e.activation(out=sdv[:, :], in_=sdv[:, :], func=AF.Sqrt), r=[sdvT], w=[sdvT])
            S.op("dve", lambda e: e.reciprocal(out=rv[:, :], in_=sdv[:, :]), r=[sdvT], w=[rvT])
            for cc in range(4):
                S.op("dve", lambda e, cc=cc, i=i: e.tensor_scalar(out=vh[i][:, cc, :], in0=vh[i][:, cc, :], scalar1=mv[:, cc, 0:1], scalar2=rv[:, cc:cc + 1],
                                                                  op0=ALU.subtract, op1=ALU.mult),
                     r=[vhT[i][cc], mvT, rvT], w=[vhT[i][cc]])

        def stageA_u(T):
            i = T % 2
            for m in range(KC):
                bU = newbank()
                for k in range(KC):
                    S.op("pe", lambda e, bU=bU, k=k, m=m: e.matmul(psf[bU][:, :], lhsT=Wu[:, k, m * 128:(m + 1) * 128], rhs=hT4[:, T, k, :],
                                                                  start=(k == 0), stop=(k == KC - 1)),
                         r=[WuT[k // 2], hTT[T]], w=[pst[bU]])
                S.op("act", lambda e, bU=bU, m=m, i=i: e.activation(out=ub[i][:, m, :], in_=psf[bU][:, :], func=AF.Gelu_apprx_tanh,
                                                                   bias=vec[:, V_BU + m:V_BU + m + 1], scale=1.0),
                     r=[pst[bU], vecT], w=[ubT[i][m]])

        def stageA_sgu(T):
            i = T % 2
            for g in range(8):
                bS = newbank()
                for cc in range(4):
                    S.op("pe", lambda e, bS=bS, g=g, cc=cc, i=i: e.matmul(psf[bS][:, cc * 128:(cc + 1) * 128], lhsT=vh[i][:, cc, g * 128:(g + 1) * 128],
                                                                         rhs=WsT[:, g, :], start=True, stop=True),
                         r=[vhT[i][cc], WsTT], w=[pst[bS]])
                si = rr2[0]
                rr2[0] = (si + 1) % 2
                S.op("dve", lambda e, bS=bS, g=g, si=si: e.scalar_tensor_tensor(out=stmp[si][:, :].rearrange("p (c t) -> p c t", c=4),
                                                                               in0=psf[bS][:, :].rearrange("p (c t) -> p c t", c=4),
                                                                               scalar=vec[:, V_SLG + g:V_SLG + g + 1],
                                                                               in1=Bg[:, None, g * 128:(g + 1) * 128].to_broadcast([128, 4, 128]),
                                                                               op0=ALU.mult, op1=ALU.add),
                     r=[pst[bS], vecT, BgT], w=[stmpT[si]])
                S.op("dve", lambda e, g=g, si=si, i=i: e.tensor_tensor(out=ub[i][:, g, :], in0=ub[i][:, g, :], in1=stmp[si][:, :], op=ALU.mult),
                     r=[ubT[i][g], stmpT[si]], w=[ubT[i][g]])

        def stageA_proj(T):
            i = T % 2
            for m in range(KC):
                bG = newbank()
                for k in range(KC):
                    S.op("pe", lambda e, bG=bG, k=k, m=m: e.matmul(psf[bG][:, :], lhsT=Wg0[:, k, m * 128:(m + 1) * 128], rhs=hT4[:, T, k, :],
                                                                  start=(k == 0), stop=(k == KC - 1)),
                         r=[Wg0T[k // 2], hTT[T]], w=[pst[bG]])
                bY = newbank()
                for j in range(KC):
                    S.op("pe", lambda e, bY=bY, j=j, m=m, i=i: e.matmul(psf[bY][:, :], lhsT=Wpa[:, j, m * 128:(m + 1) * 128], rhs=ub[i][:, j, :],
                                                                       start=(j == 0), stop=(j == KC - 1)),
                         r=[WpaT[j // 2], ubT[i][j]], w=[pst[bY]])
                si = rr2[1]
                rr2[1] = (si + 1) % 2
                S.op("act", lambda e, bG=bG, si=si, m=m: e.activation(out=sg0[si][:, :], in_=psf[bG][:, :], func=AF.Sigmoid,
                                                                     bias=vec[:, V_BG0 + m:V_BG0 + m + 1], scale=1.0),
                     r=[pst[bG], vecT], w=[sg0T[si]])
                S.op("dve", lambda e, bY=bY, si=si: e.tensor_tensor(out=t1[si][:, :], in0=psf[bY][:, :], in1=sg0[si][:, :], op=ALU.mult),
                     r=[pst[bY], sg0T[si]], w=[t1T[si]])
                S.op("dve", lambda e, si=si, m=m: e.tensor_tensor(out=M2[:, T, m, :], in0=M2[:, T, m, :], in1=t1[si][:, :], op=ALU.add),
                     r=[M2T[T], t1T[si]], w=[M2T[T]])

        stageA_v(0)
        stageA_u(0)
        for T in range(NT):
            stageA_sgu(T)
            if T == 0 and dbg:
                dump("d_vh", vh[0][:, :, :], vhT[0])
            if T + 1 < NT:
                stageA_v(T + 1)
                if T + 1 == NT - 1:
                    load_next_w()
                stageA_u(T + 1)
                if T + 1 == NT - 1:
                    load_next_w()
            stageA_proj(T)
            if T == 0 and dbg:
                dump("d_su", ub[0][:, :, :], ubT[0])
        dump("d_mg", M2[:, :, :, :], M2T)
        load_next_w()
        load_next_w()

        X = P23.view(0, F32, [NCH, D])
        XT = [P23.tk(c * 4096, 4096) for c in range(NCH)]
        h2T4 = P1.view(0, BF16, [NT, KC, 512])
        h2TT = M2T
        Wout, WoutT = W("out")
        for c in range(NCH):
            S.dma("sp", lambda e, c=c: e.dma_start(out=X[:, c, :], in_=xv[:, c, :]), w=[XT[c]])

        def stageW(T):
            for cc in range(4):
                c = T * 4 + cc
                for hf in range(2):
                    bO = newbank()
                    for m in range(KC):
                        S.op("pe", lambda e, bO=bO, m=m, cc=cc, hf=hf: e.matmul(psf[bO][:, :], lhsT=M2[:, T, m, cc * 128:(cc + 1) * 128],
                                                                               rhs=Wout[:, m, hf * 512:(hf + 1) * 512], start=(m == 0), stop=(m == KC - 1)),
                             r=[M2T[T], WoutT[m // 2]], w=[pst[bO]])
                    S.op("dve", lambda e, bO=bO, c=c, hf=hf: e.tensor_tensor(out=X[:, c, hf * 512:(hf + 1) * 512], in0=psf[bO][:, :],
                                                                            in1=X[:, c, hf * 512:(hf + 1) * 512], op=ALU.add),
                         r=[pst[bO], XT[c]], w=[XT[c]])

        def norm2(T, xn_lo=0):
            norm_to_hT(T, lambda cc, T=T: X[:, T * 4 + cc, :], lambda cc, T=T: XT[T * 4 + cc], gb2, gb2T, h2T4, h2TT, ss2, ss2T, rs1, rs1T, sd1, sd1T,
                       xn_lo=xn_lo)

        stageW(0)
        stageW(1)
        norm2(0)
        stageW(2)
        norm2(1)
        stageW(3)
        load_next_w()
        norm2(2)
        if dbg:
            dump("d_x1", X[:, :, :], XT)

        fb = [P1.view(32768 + i * 8192, BF16, [KC, 512]) for i in range(2)]
        fbT = [[P1.tk(32768 + i * 8192 + j * 1024, 1024) for j in range(KC)] for i in range(2)]
        fb_rr = [0]

        def ffn1(q, T, fi):
            W1, W1T = W("f1_%d" % q)
            for j in range(KC):
                bF = newbank()
                for k in range(KC):
                    S.op("pe", lambda e, bF=bF, k=k, j=j, W1=W1: e.matmul(psf[bF][:, :], lhsT=W1[:, k, j * 128:(j + 1) * 128], rhs=h2T4[:, T, k, :],
                                                                         start=(k == 0), stop=(k == KC - 1)),
                         r=[W1T[k // 2], h2TT[T]], w=[pst[bF]])
                S.op("act", lambda e, bF=bF, j=j, fi=fi: e.activation(out=fb[fi][:, j, :], in_=psf[bF][:, :], func=AF.Relu), r=[pst[bF]], w=[fbT[fi][j]])
                S.op("dve", lambda e, j=j, fi=fi: e.tensor_tensor(out=fb[fi][:, j, :], in0=fb[fi][:, j, :], in1=fb[fi][:, j, :], op=ALU.mult),
                     r=[fbT[fi][j]], w=[fbT[fi][j]])

        def ffn2(q, T, fi):
            W2, W2T = W("f2_%d" % q)
            for cc in range(4):
                c = T * 4 + cc
                for hf in range(2):
                    bO = newbank()
                    for j in range(KC):
                        S.op("pe", lambda e, bO=bO, j=j, cc=cc, hf=hf, W2=W2: e.matmul(psf[bO][:, :], lhsT=fb[fi][:, j, cc * 128:(cc + 1) * 128],
                                                                                      rhs=W2[:, j, hf * 512:(hf + 1) * 512], start=(j == 0), stop=(j == KC - 1)),
                             r=[fbT[fi][j], W2T[j // 2]], w=[pst[bO]])
                    S.op("dve", lambda e, bO=bO, c=c, hf=hf: e.tensor_tensor(out=X[:, c, hf * 512:(hf + 1) * 512], in0=psf[bO][:, :],
                                                                            in1=X[:, c, hf * 512:(hf + 1) * 512], op=ALU.add),
                         r=[pst[bO], XT[c]], w=[XT[c]])

        def final_norm(T):
            for cc in range(4):
                c = T * 4 + cc
                S.op("act", lambda e, c=c: e.activation(out=junk[:, :], in_=X[:, c, :], func=AF.Square, accum_out=ss3[:, c:c + 1]),
                     r=[XT[c]], w=[junkT, ss3T])
            S.op("dve", lambda e: e.tensor_scalar(out=sd1[:, T * 4:T * 4 + 4], in0=ss3[:, T * 4:T * 4 + 4], scalar1=1.0 / D, scalar2=EPS,
                                                  op0=ALU.mult, op1=ALU.add), r=[ss3T], w=[sd1T])
            S.op("act", lambda e: e.activation(out=sd1[:, T * 4:T * 4 + 4], in_=sd1[:, T * 4:T * 4 + 4], func=AF.Sqrt), r=[sd1T], w=[sd1T])
            S.op("dve", lambda e: e.reciprocal(out=rs1[:, T * 4:T * 4 + 4], in_=sd1[:, T * 4:T * 4 + 4]), r=[sd1T], w=[rs1T])
            for cc in range(4):
                c = T * 4 + cc
                S.op("dve", lambda e, c=c: e.scalar_tensor_tensor(out=X[:, c, :], in0=X[:, c, :], scalar=rs1[:, c:c + 1], in1=gbf[:, :],
                                                                  op0=ALU.mult, op1=ALU.mult),
                     r=[XT[c], rs1T, gbfT], w=[XT[c]])
                out_toks.append(S.dma("sp", lambda e, c=c: e.dma_start(out=out[c * 128:(c + 1) * 128, :], in_=X[:, c, :]), r=[XT[c]]))

        seq = [(q, T) for q in range(4) for T in range(NT)]
        fis = []
        for n, (q, T) in enumerate(seq):
            fis.append(n % 2)
        ffn1(seq[0][0], seq[0][1], fis[0])
        norm2(3, xn_lo=4)
        dump("d_h2", h2T4[:, :, :, :], h2TT)
        for n, (q, T) in enumerate(seq):
            if n + 1 < len(seq):
                ffn1(seq[n + 1][0], seq[n + 1][1], fis[n + 1])
            if n + 1 < len(seq) and seq[n + 1][1] == NT - 1 and seq[n + 1][0] < 2:
                load_next_w()
            ffn2(q, T, fis[n])
            if T == NT - 1 and q < 2:
                load_next_w()
            if q == 3:
                final_norm(T)

        S.wait("sp", out_toks)
        S.run()
    return nc


_W_NAMES = ["norm1_g", "w_in", "b_in", "sgu_ln_g", "sgu_ln_b", "sgu_w", "sgu_b", "w_proj_a", "conv_w", "conv_b",
            "conv_ln_g", "conv_ln_b", "w_proj_b", "b_proj_b", "w_out", "norm2_g", "w_ff1", "w_ff2", "norm_f_g"]


def _prep(inputs):
    d = {}
    for nm in _W_NAMES:
        a = np.asarray(inputs[nm], dtype=np.float32)
        if nm != "norm_f_g":
            a = a[0]
        if nm == "sgu_b":
            a = a.reshape(-1)
        d[nm] = np.ascontiguousarray(a)
    return d


def kernel(**inputs):
    x = np.asarray(inputs["x"], dtype=np.float32)
    B = x.shape[0]
    wd = _prep(inputs)
    nc = build_nc()
    in_maps = []
    for b in range(B):
        m = dict(wd)
        m["x"] = np.ascontiguousarray(x[b])
        in_maps.append(m)
    res = run_bass_kernel_spmd(nc, in_maps, core_ids=list(range(B)))
    return np.stack([np.asarray(r["out"], dtype=np.float32) for r in res.results], axis=0)
```
